# Optimizing a Trainium2 kernel written in Bass

```python
import math
import jax, jax.numpy as jnp
from jax import lax
import numpy as np

D_MODEL = 1024
BATCH = 8
SEQ = 2048
DEPTH = 2
DEC_BATCH = 32
DEC_SEQ = 4
PAST_LEN = 8192
PAGE_SIZE = 128

HEAD_DIM = 64
SB_HEADS = 4
SB_WIDTH = SB_HEADS * HEAD_DIM
FOX_HEADS = 4
FOX_WIDTH = FOX_HEADS * HEAD_DIM
SSM_HEADS = 8
SSM_HEAD_DIM = 64
SSM_INNER = SSM_HEADS * SSM_HEAD_DIM
SSM_GROUPS = 2
SSM_STATE = 128
SSM_CONV = 4
SSM_CONV_DIM = SSM_INNER + 2 * SSM_GROUPS * SSM_STATE
SSM_CHUNK = 128
Q_BLOCK = 128
N_BRANCH = 3
FFN_HIDDEN = 4 * D_MODEL
RMS_EPS = 1e-6
NEG_INF = -1e30
DT_MIN = 0.001
DT_MAX = 0.1
SPLIT_SIZES = (SB_WIDTH, SB_WIDTH, SB_WIDTH, FOX_WIDTH, FOX_WIDTH, FOX_WIDTH, FOX_HEADS,
               SSM_INNER, SSM_CONV_DIM, SSM_HEADS, N_BRANCH * D_MODEL)
IN_WIDTH = sum(SPLIT_SIZES)

kernel_name = 'hybrid_stickbreak_ssd_fox_step'


def rms_norm(x, g):
    xf = x.astype(jnp.float32)
    y = xf * lax.rsqrt(jnp.mean(xf * xf, axis=-1, keepdims=True) + RMS_EPS)
    return (y * g.astype(jnp.float32)).astype(x.dtype)


def split_projection(proj):
    idx = np.cumsum(SPLIT_SIZES)[:-1].tolist()
    return jnp.split(proj, idx, axis=-1)


def to_heads(a, n):
    return a.reshape(a.shape[0], a.shape[1], n, HEAD_DIM)


def stick_breaking_core(q, k, v, q_pos, k_pos):
    z = jnp.einsum('bqhd,bkhd->bhqk', q, k).astype(jnp.float32) * (HEAD_DIM ** -0.5)
    mask = k_pos[None, :] < q_pos[:, None]
    log_beta = jax.nn.log_sigmoid(z)
    log_keep = jnp.where(mask, jax.nn.log_sigmoid(-z), 0.0)
    later = lax.cumsum(log_keep, axis=3, reverse=True) - log_keep
    w = jnp.where(mask, jnp.exp(log_beta + later), 0.0)
    return jnp.einsum('bhqk,bkhd->bqhd', w.astype(v.dtype), v)


def forgetting_core(q, k, v, fq, fk, q_pos, k_pos):
    s = jnp.einsum('bqhd,bkhd->bhqk', q, k).astype(jnp.float32) * (HEAD_DIM ** -0.5)
    s = s + jnp.moveaxis(fq, 1, 2)[:, :, :, None] - jnp.moveaxis(fk, 1, 2)[:, :, None, :]
    mask = k_pos[None, :] <= q_pos[:, None]
    p = jax.nn.softmax(jnp.where(mask, s, NEG_INF), axis=-1)
    return jnp.einsum('bhqk,bkhd->bqhd', p.astype(v.dtype), v)


def sweep_query_blocks(block_fn, q_pos, *q_arrays):
    n_blocks = q_pos.shape[0] // Q_BLOCK

    def to_blocks(a):
        return jnp.moveaxis(a.reshape(a.shape[0], n_blocks, Q_BLOCK, *a.shape[2:]), 1, 0)

    args = (q_pos.reshape(n_blocks, Q_BLOCK),) + tuple(to_blocks(a) for a in q_arrays)
    out = lax.map(lambda xs: block_fn(*xs), args)
    out = jnp.moveaxis(out, 0, 1)
    return out.reshape(out.shape[0], -1, *out.shape[3:])


def causal_conv(xbc, conv_state, w, bias):
    t = xbc.shape[1]
    xp = jnp.concatenate([conv_state.astype(xbc.dtype), xbc], axis=1)
    out = bias + xp[:, 0:t] * w[0]
    for i in range(1, SSM_CONV):
        out = out + xp[:, i:i + t] * w[i]
    return jax.nn.silu(out), xp[:, t:]


def ssd_scan(x, dt, a, b_in, c_in, h0, chunk):
    bsz, t, nh, hp = x.shape
    rep = nh // b_in.shape[2]
    bh = jnp.repeat(b_in, rep, axis=2)
    ch = jnp.repeat(c_in, rep, axis=2)
    nc = t // chunk
    x = x.reshape(bsz, nc, chunk, nh, hp)
    dt = dt.reshape(bsz, nc, chunk, nh)
    bh = bh.reshape(bsz, nc, chunk, nh, -1)
    ch = ch.reshape(bsz, nc, chunk, nh, -1)
    a_cs = jnp.cumsum(dt * a, axis=2)
    seg = a_cs[:, :, :, None, :] - a_cs[:, :, None, :, :]
    tri = jnp.tril(jnp.ones((chunk, chunk), dtype=bool))[None, None, :, :, None]
    decay = jnp.exp(jnp.where(tri, seg, -jnp.inf))
    scores = jnp.einsum('bclhn,bcshn->bclsh', ch, bh)
    xdt = x * dt[..., None]
    y_diag = jnp.einsum('bclsh,bcshp->bclhp', scores * decay, xdt)
    to_end = jnp.exp(a_cs[:, :, -1:, :] - a_cs)
    states = jnp.einsum('bclhn,bclh,bclhp->bchpn', bh, to_end * dt, x)
    chunk_decay = jnp.exp(a_cs[:, :, -1, :])

    def step(h, inp):
        st, dec = inp
        return h * dec[..., None, None] + st, h

    h_final, h_starts = lax.scan(step, h0, (jnp.moveaxis(states, 1, 0), jnp.moveaxis(chunk_decay, 1, 0)))
    h_starts = jnp.moveaxis(h_starts, 0, 1)
    y_off = jnp.einsum('bclhn,bchpn,bclh->bclhp', ch, h_starts, jnp.exp(a_cs))
    return (y_diag + y_off).reshape(bsz, t, nh, hp), h_final


def ssm_branch(z, xbc, dt_raw, conv_state, ssm_state, chunk, lp):
    bsz, t, _ = z.shape
    f32 = jnp.float32
    xbc_act, conv_new = causal_conv(xbc, conv_state, lp['conv_w'], lp['conv_b'])
    xs, b_in, c_in = jnp.split(xbc_act, [SSM_INNER, SSM_INNER + SSM_GROUPS * SSM_STATE], axis=-1)
    xs = xs.reshape(bsz, t, SSM_HEADS, SSM_HEAD_DIM).astype(f32)
    b_in = b_in.reshape(bsz, t, SSM_GROUPS, SSM_STATE).astype(f32)
    c_in = c_in.reshape(bsz, t, SSM_GROUPS, SSM_STATE).astype(f32)
    dt = jax.nn.softplus(dt_raw.astype(f32) + lp['dt_bias'].astype(f32))
    a = -jnp.exp(lp['a_log'].astype(f32))
    y, ssm_new = ssd_scan(xs, dt, a, b_in, c_in, ssm_state.astype(f32), chunk)
    y = y + lp['d_skip'].astype(f32)[:, None] * xs
    y = y.reshape(bsz, t, SSM_INNER) * jax.nn.silu(z.astype(f32))
    yg = y.reshape(bsz, t, SSM_GROUPS, -1)
    yg = yg * lax.rsqrt(jnp.mean(yg * yg, axis=-1, keepdims=True) + RMS_EPS)
    y = yg.reshape(bsz, t, SSM_INNER) * lp['ssm_norm_g'].astype(f32)
    return y.astype(z.dtype), conv_new, ssm_new


def merge_and_ffn(x, y_sb, y_ssm, y_fox, gate_logits, lp):
    bsz, t, _ = x.shape
    g = jax.nn.sigmoid(gate_logits.astype(jnp.float32)).reshape(bsz, t, N_BRANCH, D_MODEL)
    br_sb = jnp.einsum('bte,ed->btd', y_sb.reshape(bsz, t, SB_WIDTH), lp['w_sb_out'])
    br_ssm = jnp.einsum('bte,ed->btd', y_ssm, lp['w_ssm_out'])
    br_fox = jnp.einsum('bte,ed->btd', y_fox.reshape(bsz, t, FOX_WIDTH), lp['w_fox_out'])
    mixed = g[:, :, 0] * br_sb + g[:, :, 1] * br_ssm + g[:, :, 2] * br_fox
    x = x + jnp.einsum('btd,de->bte', mixed.astype(x.dtype), lp['w_o'])
    h = rms_norm(x, lp['norm2_g'])
    u = jax.nn.relu(jnp.einsum('btd,df->btf', h, lp['w_up']))
    return x + jnp.einsum('btf,fd->btd', u * u, lp['w_down'])


def mixer_inputs(x, lp):
    h = rms_norm(x, lp['norm1_g'])
    return split_projection(jnp.einsum('btd,de->bte', h, lp['w_in']))


def prompt_layer(x, lp):
    bsz, t, _ = x.shape
    q_sb, k_sb, v_sb, q_fx, k_fx, v_fx, f_fx, z, xbc, dt_raw, gates = mixer_inputs(x, lp)
    q_sb, k_sb, v_sb = to_heads(q_sb, SB_HEADS), to_heads(k_sb, SB_HEADS), to_heads(v_sb, SB_HEADS)
    q_fx, k_fx, v_fx = to_heads(q_fx, FOX_HEADS), to_heads(k_fx, FOX_HEADS), to_heads(v_fx, FOX_HEADS)
    pos = jnp.arange(t)
    y_sb = sweep_query_blocks(lambda qp, qb: stick_breaking_core(qb, k_sb, v_sb, qp, pos), pos, q_sb)
    logf = jax.nn.log_sigmoid(f_fx.astype(jnp.float32) + lp['b_forget'].astype(jnp.float32))
    cum = jnp.cumsum(logf, axis=1)
    y_fx = sweep_query_blocks(lambda qp, qb, fb: forgetting_core(qb, k_fx, v_fx, fb, cum, qp, pos), pos, q_fx, cum)
    conv0 = jnp.zeros((bsz, SSM_CONV - 1, SSM_CONV_DIM), x.dtype)
    ssm0 = jnp.zeros((bsz, SSM_HEADS, SSM_HEAD_DIM, SSM_STATE), jnp.float32)
    y_ssm, conv_new, ssm_new = ssm_branch(z, xbc, dt_raw, conv0, ssm0, min(SSM_CHUNK, t), lp)
    x = merge_and_ffn(x, y_sb, y_ssm, y_fx, gates, lp)
    return x, (k_sb, v_sb, k_fx, v_fx, logf, ssm_new, conv_new)


def sample_layer(x, l, lp, cache_sb_k, cache_sb_v, cache_fox_k, cache_fox_v, cache_fox_logf,
                 state_ssm, state_conv, page_table):
    bsz, t, _ = x.shape
    past = page_table.shape[1] * PAGE_SIZE

    def gather(pool):
        g = pool[l, page_table]
        return g.reshape(bsz, past, *g.shape[3:])

    q_sb, k_sb, v_sb, q_fx, k_fx, v_fx, f_fx, z, xbc, dt_raw, gates = mixer_inputs(x, lp)
    q_sb, k_sb, v_sb = to_heads(q_sb, SB_HEADS), to_heads(k_sb, SB_HEADS), to_heads(v_sb, SB_HEADS)
    q_fx, k_fx, v_fx = to_heads(q_fx, FOX_HEADS), to_heads(k_fx, FOX_HEADS), to_heads(v_fx, FOX_HEADS)
    q_pos = past + jnp.arange(t)
    k_pos = jnp.arange(past + t)
    k_sb_all = jnp.concatenate([gather(cache_sb_k).astype(k_sb.dtype), k_sb], axis=1)
    v_sb_all = jnp.concatenate([gather(cache_sb_v).astype(v_sb.dtype), v_sb], axis=1)
    y_sb = stick_breaking_core(q_sb, k_sb_all, v_sb_all, q_pos, k_pos)
    logf = jax.nn.log_sigmoid(f_fx.astype(jnp.float32) + lp['b_forget'].astype(jnp.float32))
    cum = jnp.cumsum(jnp.concatenate([gather(cache_fox_logf).astype(jnp.float32), logf], axis=1), axis=1)
    k_fx_all = jnp.concatenate([gather(cache_fox_k).astype(k_fx.dtype), k_fx], axis=1)
    v_fx_all = jnp.concatenate([gather(cache_fox_v).astype(v_fx.dtype), v_fx], axis=1)
    y_fx = forgetting_core(q_fx, k_fx_all, v_fx_all, cum[:, past:], cum, q_pos, k_pos)
    y_ssm, conv_new, ssm_new = ssm_branch(z, xbc, dt_raw, state_conv[l], state_ssm[l], t, lp)
    x = merge_and_ffn(x, y_sb, y_ssm, y_fx, gates, lp)
    return x, (k_sb, v_sb, k_fx, v_fx, logf, ssm_new, conv_new)


def setup_inputs(seed: int = 0) -> dict:
    key = jax.random.key(seed)
    ks = jax.random.split(key, 32)
    f32 = jnp.float32
    n_pages = PAST_LEN // PAGE_SIZE
    n_used = DEC_BATCH * n_pages
    n_pool = n_used + max(1, n_used // 4)

    def nrm(k, shape, scale=1.0):
        return jax.random.normal(k, shape, f32) * scale

    u_dt = jax.random.uniform(ks[14], (DEPTH, SSM_HEADS), f32)
    dt0 = jnp.exp(u_dt * (math.log(DT_MAX) - math.log(DT_MIN)) + math.log(DT_MIN))
    return {
        'x_prompt': nrm(ks[0], (BATCH, SEQ, D_MODEL)),
        'x_sample': nrm(ks[1], (DEC_BATCH, DEC_SEQ, D_MODEL)),
        'cache_sb_k': nrm(ks[2], (DEPTH, n_pool, PAGE_SIZE, SB_HEADS, HEAD_DIM)),
        'cache_sb_v': nrm(ks[3], (DEPTH, n_pool, PAGE_SIZE, SB_HEADS, HEAD_DIM)),
        'cache_fox_k': nrm(ks[4], (DEPTH, n_pool, PAGE_SIZE, FOX_HEADS, HEAD_DIM)),
        'cache_fox_v': nrm(ks[5], (DEPTH, n_pool, PAGE_SIZE, FOX_HEADS, HEAD_DIM)),
        'cache_fox_logf': jax.nn.log_sigmoid(3.0 + nrm(ks[6], (DEPTH, n_pool, PAGE_SIZE, FOX_HEADS), 1.5)),
        'state_ssm': nrm(ks[7], (DEPTH, DEC_BATCH, SSM_HEADS, SSM_HEAD_DIM, SSM_STATE), 0.5),
        'state_conv': nrm(ks[8], (DEPTH, DEC_BATCH, SSM_CONV - 1, SSM_CONV_DIM)),
        'page_table': jax.random.permutation(ks[9], n_pool)[:n_used].reshape(DEC_BATCH, n_pages).astype(jnp.int32),
        'norm1_g': 1.0 + nrm(ks[10], (DEPTH, D_MODEL), 0.05),
        'w_in': nrm(ks[11], (DEPTH, D_MODEL, IN_WIDTH), D_MODEL ** -0.5),
        'b_forget': 3.0 + nrm(ks[12], (DEPTH, FOX_HEADS), 1.5),
        'conv_w': nrm(ks[13], (DEPTH, SSM_CONV, SSM_CONV_DIM), SSM_CONV ** -0.5),
        'conv_b': nrm(ks[15], (DEPTH, SSM_CONV_DIM), 0.01),
        'dt_bias': dt0 + jnp.log(-jnp.expm1(-dt0)),
        'a_log': jnp.log(jax.random.uniform(ks[16], (DEPTH, SSM_HEADS), f32, minval=1.0, maxval=16.0)),
        'd_skip': 1.0 + nrm(ks[17], (DEPTH, SSM_HEADS), 0.1),
        'ssm_norm_g': 1.0 + nrm(ks[18], (DEPTH, SSM_INNER), 0.05),
        'w_sb_out': nrm(ks[19], (DEPTH, SB_WIDTH, D_MODEL), SB_WIDTH ** -0.5),
        'w_ssm_out': nrm(ks[20], (DEPTH, SSM_INNER, D_MODEL), SSM_INNER ** -0.5),
        'w_fox_out': nrm(ks[21], (DEPTH, FOX_WIDTH, D_MODEL), FOX_WIDTH ** -0.5),
        'w_o': nrm(ks[22], (DEPTH, D_MODEL, D_MODEL), D_MODEL ** -0.5),
        'norm2_g': 1.0 + nrm(ks[23], (DEPTH, D_MODEL), 0.05),
        'w_up': nrm(ks[24], (DEPTH, D_MODEL, FFN_HIDDEN), D_MODEL ** -0.5),
        'w_down': nrm(ks[25], (DEPTH, FFN_HIDDEN, D_MODEL), FFN_HIDDEN ** -0.5),
        'final_norm_g': 1.0 + nrm(ks[26], (D_MODEL,), 0.05),
    }


def reference(x_prompt, x_sample, cache_sb_k, cache_sb_v, cache_fox_k, cache_fox_v, cache_fox_logf,
              state_ssm, state_conv, page_table, norm1_g, w_in, b_forget, conv_w, conv_b, dt_bias,
              a_log, d_skip, ssm_norm_g, w_sb_out, w_ssm_out, w_fox_out, w_o, norm2_g, w_up, w_down,
              final_norm_g):
    xp, xs = x_prompt, x_sample
    prompt_states, sample_states = [], []
    for l in range(DEPTH):
        lp = {'norm1_g': norm1_g[l], 'w_in': w_in[l], 'b_forget': b_forget[l], 'conv_w': conv_w[l],
              'conv_b': conv_b[l], 'dt_bias': dt_bias[l], 'a_log': a_log[l], 'd_skip': d_skip[l],
              'ssm_norm_g': ssm_norm_g[l], 'w_sb_out': w_sb_out[l], 'w_ssm_out': w_ssm_out[l],
              'w_fox_out': w_fox_out[l], 'w_o': w_o[l], 'norm2_g': norm2_g[l], 'w_up': w_up[l],
              'w_down': w_down[l]}
        xp, st_p = prompt_layer(xp, lp)
        xs, st_s = sample_layer(xs, l, lp, cache_sb_k, cache_sb_v, cache_fox_k, cache_fox_v,
                                cache_fox_logf, state_ssm, state_conv, page_table)
        prompt_states.append(st_p)
        sample_states.append(st_s)
    y_prompt = rms_norm(xp, final_norm_g)
    y_sample = rms_norm(xs, final_norm_g)
    p_sb_k, p_sb_v, p_fox_k, p_fox_v, p_fox_logf, p_ssm, p_conv = [jnp.stack(s) for s in zip(*prompt_states)]
    s_sb_k, s_sb_v, s_fox_k, s_fox_v, s_fox_logf, s_ssm, s_conv = [jnp.stack(s) for s in zip(*sample_states)]
    return (y_prompt, y_sample, p_sb_k, p_sb_v, p_fox_k, p_fox_v, p_fox_logf, p_ssm, p_conv,
            s_sb_k, s_sb_v, s_fox_k, s_fox_v, s_fox_logf, s_ssm, s_conv)
```

```python
import numpy as np
from contextlib import ExitStack
import concourse.bass as bass
import concourse.mybir as mybir
from concourse.bass_utils import run_bass_kernel_spmd

F32 = mybir.dt.float32
BF16 = mybir.dt.bfloat16
I32 = mybir.dt.int32
AF = mybir.ActivationFunctionType
ALU = mybir.AluOpType
AX = mybir.AxisListType

ENG = ['pe', 'act', 'dve', 'pool', 'sp']
D = 1024
NCORES = 8
EPS = 1e-6
CFG_FULL = dict(T=2048, NB=4, NPG=64, NPOOL=2560, DEPTH=2)


class Prog:
    def __init__(self, nc, n_dma_sems=8):
        self.nc = nc
        self.ops = {e: [] for e in ENG}
        self.cnt = {e: 0 for e in ENG}
        self.lastw = {}
        self.readers = {}
        self.dma_cnt = {}
        self.dma_rr = {q: 0 for q in ENG}
        self.nd = n_dma_sems

    def _deps(self, reads, writes, eng=None):
        deps = {}

        def add(k, v):
            if deps.get(k, 0) < v:
                deps[k] = v
        for r in reads:
            if r in self.lastw:
                add(*self.lastw[r])
            if r.startswith('ps'):
                for k, v in self.readers.get(r, {}).items():
                    if k != ('c', eng):
                        add(k, v)
        for w in writes:
            if w in self.lastw:
                add(*self.lastw[w])
            for k, v in self.readers.get(w, {}).items():
                add(k, v)
        return deps

    def _commit(self, tok, reads, writes):
        k, v = tok
        for r in reads:
            d = self.readers.setdefault(r, {})
            if d.get(k, 0) < v:
                d[k] = v
        for w in writes:
            self.lastw[w] = tok
            self.readers[w] = {}

    def op(self, eng, fn, reads=(), writes=()):
        deps = self._deps(reads, writes, eng)
        self.cnt[eng] += 1
        tok = (('c', eng), self.cnt[eng])
        self.ops[eng].append((fn, deps, tok))
        self._commit(tok, reads, writes)

    def dma(self, q, fn, reads=(), writes=()):
        deps = self._deps(reads, writes)
        k = self.dma_rr[q]
        self.dma_rr[q] = (k + 1) % self.nd
        c = self.dma_cnt.get((q, k), 0)
        if c > 0:
            deps[('d', q, k)] = max(deps.get(('d', q, k), 0), c)
        self.dma_cnt[(q, k)] = c + 1
        tok = (('d', q, k), c + 1)
        self.ops[q].append((fn, deps, tok))
        self._commit(tok, reads, writes)

    def barrier(self):
        deps = {('c', e): self.cnt[e] for e in ENG if self.cnt[e] > 0}
        for (q, k), c in self.dma_cnt.items():
            deps[('d', q, k)] = c
        for e in ENG:
            self.ops[e].append((None, dict(deps), None))

    def emit(self):
        nc = self.nc
        sem_c = {e: nc.alloc_semaphore(name=f"c_{e}") for e in ENG}
        sem_d = {}
        for (q, k) in sorted(self.dma_cnt):
            sem_d[(q, k)] = nc.alloc_semaphore(name=f"d_{q}{k}")

        def semof(key):
            if key[0] == 'c':
                return sem_c[key[1]], 1
            return sem_d[(key[1], key[2])], 16

        final_waits = list(self.dma_cnt.items())

        def run(ename, eng):
            seen = {}
            for fn, deps, tok in self.ops[ename]:
                for key, val in deps.items():
                    if key == ('c', 'pe') and ename == 'pe':
                        continue
                    if seen.get(key, 0) >= val:
                        continue
                    s, mul = semof(key)
                    eng.wait_ge(s, val * mul)
                    seen[key] = val
                if fn is None:
                    continue
                ins = fn(eng)
                s, mul = semof(tok[0])
                ins.then_inc(s, mul)
            if ename == 'sp':
                for (q, k), c in final_waits:
                    eng.wait_ge(sem_d[(q, k)], 16 * c)

        with nc.Block() as block:
            @block.sync
            def _(e):
                run('sp', e)

            @block.scalar
            def _(e):
                run('act', e)

            @block.vector
            def _(e):
                run('dve', e)

            @block.gpsimd
            def _(e):
                run('pool', e)

            @block.tensor
            def _(e):
                run('pe', e)


SB_W, FX_W = 256, 256
OFF_QSB, OFF_KSB, OFF_VSB = 0, 256, 512
OFF_QFX, OFF_KFX, OFF_VFX = 768, 1024, 1280
OFF_F = 1536
OFF_Z = 1540
OFF_XBC = 2052
OFF_DT = 3076
OFF_G = 3084


def wblocks():
    bl = []
    for cb in range(40):
        bl.append((f'fm{cb}', 1024))
    bl.append(('f4', 32))
    bl += [('tmA', 4096), ('tmB', 4096), ('tmC', 4096), ('tmD', 96)]
    for cb in range(8):
        bl.append((f'sbo{cb}', 256))
    for cb in range(8):
        bl.append((f'sso{cb}', 512))
    for cb in range(8):
        bl.append((f'fxo{cb}', 256))
    for cb in range(8):
        bl.append((f'wo{cb}', 1024))
    for cb in range(32):
        bl.append((f'up{cb}', 1024))
    for cb in range(8):
        bl.append((f'dn{cb}', 4096))
    return bl


def wlayout():
    off = {}
    r = 0
    for name, E in wblocks():
        off[name] = (r, E)
        r += 128 * E // 1024
    nrows = ((r + 1023) // 1024) * 1024
    return off, nrows


def _blk(W, cols):
    K = W.shape[0]
    sub = W[:, cols]
    return np.ascontiguousarray(sub.reshape(K // 128, 128, len(cols)).transpose(1, 0, 2))


def prep_weights(inp, depth):
    off, nrows = wlayout()
    wall = np.zeros((depth, nrows * 1024), np.float32)
    fm_cols = (list(range(OFF_QSB, OFF_QSB + 256)) + list(range(OFF_KSB, OFF_KSB + 256)) +
               list(range(OFF_QFX, OFF_QFX + 256)) + list(range(OFF_KFX, OFF_KFX + 256)) +
               list(range(OFF_XBC, OFF_XBC + 1024)) + list(range(OFF_G, OFF_G + 3072)))
    for l in range(depth):
        win = inp['w_in'][l]

        def put(name, arr):
            r, E = off[name]
            a = arr.reshape(128, -1)
            assert a.shape[1] == E, (name, a.shape, E)
            wall[l, r * 1024: r * 1024 + 128 * E] = a.reshape(-1)
        for cb in range(40):
            put(f'fm{cb}', _blk(win, fm_cols[cb * 128:(cb + 1) * 128]))
        put('f4', _blk(win, list(range(OFF_F, OFF_F + 4))))
        put('tmA', _blk(win, list(range(OFF_KSB, OFF_KSB + 512))))
        put('tmB', _blk(win, list(range(OFF_KFX, OFF_KFX + 512))))
        put('tmC', _blk(win, list(range(OFF_Z, OFF_Z + 512))))
        put('tmD', _blk(win, list(range(OFF_F, OFF_F + 4)) + list(range(OFF_DT, OFF_DT + 8))))
        for cb in range(8):
            cs = list(range(cb * 128, (cb + 1) * 128))
            put(f'sbo{cb}', _blk(inp['w_sb_out'][l], cs))
            put(f'sso{cb}', _blk(inp['w_ssm_out'][l], cs))
            put(f'fxo{cb}', _blk(inp['w_fox_out'][l], cs))
            put(f'wo{cb}', _blk(inp['w_o'][l], cs))
            put(f'dn{cb}', _blk(inp['w_down'][l], cs))
        for cb in range(32):
            put(f'up{cb}', _blk(inp['w_up'][l], list(range(cb * 128, (cb + 1) * 128))))
    return wall.reshape(depth, nrows, 1024)


PAR = {}
_o = 0
for _n, _w in [('g1', 8), ('g2', 8), ('gf', 8), ('bf_rep', 4), ('bf_col', 1), ('cw', 32), ('cb', 8),
               ('dtb', 8), ('alog', 8), ('dsk', 8), ('gn', 512)]:
    PAR[_n] = (_o, _w)
    _o += _w
NPAR = _o


def prep_params(inp, depth):
    par = np.zeros((depth, 128, NPAR), np.float32)

    def col(v):
        return v.reshape(8, 128).T
    for l in range(depth):
        def put(n, a):
            o, w = PAR[n]
            par[l, :, o:o + w] = a
        put('g1', col(inp['norm1_g'][l]))
        put('g2', col(inp['norm2_g'][l]))
        put('gf', col(inp['final_norm_g']))
        put('bf_rep', np.broadcast_to(inp['b_forget'][l][None, :], (128, 4)))
        bc = np.zeros((128, 1), np.float32)
        bc[0:4, 0] = inp['b_forget'][l]
        put('bf_col', bc)
        put('cw', inp['conv_w'][l].reshape(4, 8, 128).transpose(2, 1, 0).reshape(128, 32))
        put('cb', col(inp['conv_b'][l]))
        put('dtb', np.broadcast_to(inp['dt_bias'][l][None, :], (128, 8)))
        put('alog', np.broadcast_to(inp['a_log'][l][None, :], (128, 8)))
        put('dsk', np.broadcast_to(inp['d_skip'][l][None, :], (128, 8)))
        put('gn', np.broadcast_to(inp['ssm_norm_g'][l][None, :], (128, 512)))
    return par


CST = {}
_o = 0
for _n, _w in [('ident', 128), ('linc', 128), ('ustr', 128), ('ones', 128), ('uincr', 128),
               ('lincS', 16), ('ustrS', 16), ('onesS', 16), ('selB', 512), ('rowB', 4), ('colB', 64),
               ('mnsb', 8), ('mnfx', 8), ('iota', 1)]:
    CST[_n] = (_o, _w)
    _o += _w
NCST = _o


def prep_cmask():
    j = np.arange(128)[:, None]
    q = np.arange(512)[None, :]
    m = np.zeros((128, 4, 513), np.float32)
    for i in range(4):
        m[:, i, 1:] = ((i * 128 + j) <= q)
    return m.reshape(128, 4 * 513)


def prep_consts():
    c = np.zeros((128, NCST), np.float32)
    j = np.arange(128)[:, None]
    l = np.arange(128)[None, :]

    def put(n, a):
        o, w = CST[n]
        c[:a.shape[0], o:o + w] = a
    put('ident', (j == l).astype(np.float32))
    put('linc', (j <= l).astype(np.float32))
    put('ustr', (j > l).astype(np.float32))
    put('ones', np.ones((128, 128), np.float32))
    put('uincr', (j >= l).astype(np.float32))
    j16 = np.arange(16)[:, None]
    l16 = np.arange(16)[None, :]
    same = (j16 // 4 == l16 // 4)
    put('lincS', (same & (j16 <= l16)).astype(np.float32))
    put('ustrS', (same & (j16 > l16)).astype(np.float32))
    put('onesS', same.astype(np.float32))
    selB = np.zeros((16, 4, 128), np.float32)
    for b in range(4):
        selB[4 * b:4 * b + 4, b, :] = 1.0
    put('selB', selB.reshape(16, 512))
    rowB = np.zeros((16, 4), np.float32)
    for b in range(4):
        rowB[4 * b:4 * b + 4, b] = 1.0
    put('rowB', rowB)
    colB = np.zeros((128, 4, 16), np.float32)
    for b in range(4):
        colB[:, b, 4 * b:4 * b + 4] = 1.0
    put('colB', colB.reshape(128, 64))
    qq = np.tile(np.arange(4), 2)[None, :]
    t4 = np.arange(128)[:, None]
    put('mnsb', ((t4 < qq) & (t4 < 4)).astype(np.float32))
    put('mnfx', ((t4 <= qq) & (t4 < 4)).astype(np.float32))
    put('iota', np.arange(128, dtype=np.float32)[:, None])
    return c


class Bag:
    pass


class _Stop(Exception):
    pass


def build(cfg, stop_at=None):
    T, NB, NPG, NPOOL, DEPTH = cfg['T'], cfg['NB'], cfg['NPG'], cfg['NPOOL'], cfg['DEPTH']
    NS = NB * 4
    NT = T // 128
    NG = T // 512
    woff, wrows = wlayout()

    nc = bass.Bass("TRN2", target_bir_lowering=False)
    P = Prog(nc)

    def din(name, shape, dt=F32):
        return nc.dram_tensor(name, shape, dt, kind="ExternalInput").ap()

    def dout(name, shape, dt=F32):
        return nc.dram_tensor(name, shape, dt, kind="ExternalOutput").ap()

    xp_d = din("xp", [T, D])
    xs_d = din("xs", [NS, D])
    csk_d = din("csk", [DEPTH * NPOOL, 128, 256])
    csv_d = din("csv", [DEPTH * NPOOL, 128, 256])
    cfk_d = din("cfk", [DEPTH * NPOOL, 128, 256])
    cfv_d = din("cfv", [DEPTH * NPOOL, 128, 256])
    clf_d = din("clf", [DEPTH, NPOOL, 512])
    sst_d = din("sst", [DEPTH, NB, 8, 64, 128])
    scv_d = din("scv", [DEPTH, 128, 8, NB, 3])
    ptab_d = din("ptab", [1, NB * NPG], I32)
    wall_d = din("wall", [DEPTH, wrows, 1024])
    par_d = din("par", [DEPTH, 128, NPAR])
    cst_d = din("cst", [128, NCST])
    cmask_d = din("cmask", [128, 4 * 513])
    wscr = nc.dram_tensor("wscr", [DEPTH, wrows, 1024], BF16, kind="Internal").ap()
    xscr = nc.dram_tensor("xscr", [NG + 1, 128, 8 * 512], F32, kind="Internal").ap()

    yp_o = dout("yp", [T, D])
    ys_o = dout("ys", [NS, D])
    psbk_o = dout("psbk", [DEPTH, T, 256])
    psbv_o = dout("psbv", [DEPTH, T, 256])
    pfxk_o = dout("pfxk", [DEPTH, T, 256])
    pfxv_o = dout("pfxv", [DEPTH, T, 256])
    plf_o = dout("plf", [DEPTH, T, 4])
    pssm_o = dout("pssm", [DEPTH, 8, 64, 128])
    pcv_o = dout("pcv", [DEPTH, 3, 1024])
    ssbk_o = dout("ssbk", [DEPTH, NS, 256])
    ssbv_o = dout("ssbv", [DEPTH, NS, 256])
    sfxk_o = dout("sfxk", [DEPTH, NS, 256])
    sfxv_o = dout("sfxv", [DEPTH, NS, 256])
    slf_o = dout("slf", [DEPTH, NS, 4])
    sssm_o = dout("sssm", [DEPTH, NB, 8, 64, 128])
    scv_o = dout("scvo", [DEPTH, NB, 3, 1024])

    uid = {'n': 0}

    open_scopes = []

    def ck(name):
        if stop_at is not None and name == stop_at:
            raise _Stop()

    class Scope:
        def __init__(self):
            self.es = ExitStack()
            open_scopes.append(self)

        def sb(self, name, shape, dt=F32):
            uid['n'] += 1
            return self.es.enter_context(nc.sbuf_tensor(f"{name}_{uid['n']}", shape, dt))

        def close(self):
            P.barrier()
            self.es.close()
            open_scopes.remove(self)

    G = Scope()
    sb = G.sb

    def mm(out, lhsT, rhs, start, stop, r, w):
        P.op('pe', lambda e: e.matmul(out, lhsT, rhs, start=start, stop=stop), r, w)

    def tr(out, in_, ident, r, w):
        P.op('pe', lambda e: e.transpose(out, in_, ident), r, w)

    def act(out, in_, func, r, w, bias=None, scale=None, accum=None):
        kw = {}
        if bias is not None:
            kw['bias'] = bias
        if scale is not None:
            kw['scale'] = scale
        if accum is not None:
            kw['accum_out'] = accum
        P.op('act', lambda e: e.activation(out, in_, func, **kw), r, w)

    def tt(out, a, b, op, r, w, eng='dve'):
        P.op(eng, lambda e: e.tensor_tensor(out, a, b, op), r, w)

    def ts(out, a, s1, s2, op0, op1, r, w, eng='dve'):
        if op1 is None:
            P.op(eng, lambda e: e.tensor_scalar(out, a, s1, None, op0), r, w)
        else:
            P.op(eng, lambda e: e.tensor_scalar(out, a, s1, s2, op0, op1), r, w)

    def stt(out, a, s, b, op0, op1, r, w):
        P.op('dve', lambda e: e.scalar_tensor_tensor(out, a, s, b, op0, op1), r, w)

    def cp(out, in_, r, w, eng='dve'):
        if eng == 'act':
            P.op('act', lambda e: e.copy(out, in_), r, w)
        else:
            P.op(eng, lambda e: e.tensor_copy(out, in_), r, w)

    def memset(ap, v, w, eng='dve'):
        P.op(eng, lambda e: e.memset(ap, v), (), w)

    def dma(out, in_, r, w, q='sp'):
        P.dma(q, lambda e: e.dma_start(out=out, in_=in_), r, w)

    def scan(out, d0, d1, init, r, w):
        P.op('dve', lambda e: e.tensor_tensor_scan(out, d0, d1, init, ALU.mult, ALU.add), r, w)

    def recip(out, in_, r, w):
        P.op('dve', lambda e: e.reciprocal(out, in_), r, w)

    pes = ExitStack()
    banks = [pes.enter_context(nc.psum_tensor(f"ps{i}", [128, 512], F32)) for i in range(8)]
    bstate = {'i': 0}

    reserved = set()

    def bank(hold=False):
        for _ in range(8):
            i = bstate['i']
            bstate['i'] = (i + 1) % 8
            if i not in reserved:
                break
        else:
            raise RuntimeError("all PSUM banks reserved")
        if hold:
            reserved.add(i)
        return banks[i], f'ps{i}'

    def release(*names):
        for n_ in names:
            reserved.discard(int(n_[2:]))

    Fp = [sb(f"F{i}", [128, 512]) for i in range(8)]
    Hp = [sb(f"H{i}", [128, 512], BF16) for i in range(8)]

    def F(i):
        return Fp[i], f'F{i}'

    def H(i):
        return Hp[i], f'H{i}'

    stg = {'i': 0}

    def next_stage():
        i = 6 + stg['i']
        stg['i'] = 1 - stg['i']
        return Fp[i], f'F{i}'

    cst = sb("cst", [128, NCST])
    dma(cst[:], cst_d[:, :], (), ['cst'])

    def C(n, rows=128, c0=0, c1=None):
        o, w = CST[n]
        if c1 is None:
            c1 = w
        return cst[0:rows, o + c0:o + c1]

    onesb = sb("onesb", [128, 128], BF16)
    nuincb = sb("nuincb", [128, 128], BF16)
    nonesb = sb("nonesb", [128, 128], BF16)
    mext = sb("mext", [128, 4, 513], BF16)
    cp(onesb[:], C('ones'), ['cst'], ['cb16'])
    ts(nuincb[:], C('uincr'), -1.0, None, ALU.mult, None, ['cst'], ['cb16'])
    ts(nonesb[:], C('ones'), -1.0, None, ALU.mult, None, ['cst'], ['cb16'])
    for i in range(4):
        f_, fn = F(i)
        dma(f_[:, 0:512], cmask_d[:, i * 513:i * 513 + 512], (), [fn])
        cp(mext[:, i, 0:512], f_[:, 0:512], [fn], ['cb16'])
        memset(mext[:, i, 512:513], 1.0, ['cb16'])
    CB = ['cst', 'cb16']
    ident = C('ident')

    def msb(di):
        return mext[:, di, 0:512]

    def mfx(di):
        return mext[:, di, 1:513]

    epst = sb("epst", [128, 1])
    one_t = sb("one_t", [128, 1])
    ones_f = sb("ones_f", [128, 128])
    memset(epst[:], EPS, ['cb16'])
    memset(one_t[:], 1.0, ['cb16'])
    memset(ones_f[:], 1.0, ['cb16'])
    onesT = sb("onesT", [4, 512])
    sel4 = sb("sel4", [4, 4, 128])
    memset(onesT[:], 1.0, ['cb16'])
    for h in range(4):
        cp(sel4[:, h, :], C('ident', rows=4, c0=h, c1=h + 1).to_broadcast([4, 128]), ['cst'], ['cb16'])

    par = sb("par", [128, NPAR])
    arep = sb("arep", [128, 8])
    NBF = sb("NBF", [128, 1])

    def PR(n, rows=128, c0=0, c1=None):
        o, w = PAR[n]
        if c1 is None:
            c1 = w
        return par[0:rows, o + c0:o + c1]

    def cast_layer(l):
        for ch in range(wrows // 1024):
            P.dma('pool', lambda e, l=l, ch=ch: e.dma_start(out=wscr[l, ch * 1024:(ch + 1) * 1024, :],
                                                         in_=wall_d[l, ch * 1024:(ch + 1) * 1024, :]),
                  (), [f'scr{l}.{ch}', f'castslot{ch % 2}'])

    wbig = [sb(f"wbig{i}", [128, 4096], BF16) for i in range(2)]
    wsm = [sb(f"wsm{i}", [128, 1024], BF16) for i in range(4)]
    wstate = {'b': 0, 's': 0}

    def load_w(l, name):
        r0, E = woff[name]
        if E > 1024:
            i = wstate['b']
            wstate['b'] = (i + 1) % 2
            tile_, res = wbig[i], f'wb{i}'
        else:
            i = wstate['s']
            wstate['s'] = (i + 1) % 4
            tile_, res = wsm[i], f'ws{i}'
        nr = 128 * E // 1024
        chs = sorted(set([r0 // 1024, (r0 + nr - 1) // 1024]))
        src = wscr[l, r0:r0 + nr, :].rearrange("r c -> (r c)").rearrange("(p e) -> p e", p=128)
        dma(tile_[:, 0:E], src, [f'scr{l}.{c}' for c in chs], [res])
        return tile_, res

    xg = sb("xg", [128, 8, 512])
    hT = sb("hT", [128, 8, 512], BF16)
    rstd = sb("rstd", [128, 512])
    carryF = sb("carryF", [128, 4])
    carryFT = sb("carryFT", [4, 1])
    ccar = sb("ccar", [128, 8, 3])
    hst = sb("hst", [128, 8, 64])
    hstb = sb("hstb", [128, 8, 64], BF16)
    lfT = sb("lfT", [4, 512])
    cumT = sb("cumT", [4, 512])
    tmp8 = sb("tmp8", [128, 16])
    lf_tok = sb("lf_tok", [128, 4])
    idx_t = sb("idx_t", [128, NB * NPG], I32)
    ptr_t = sb("ptr_t", [128, NB * NPG], I32)
    iota_i = sb("iota_i", [128, 1], I32)
    pcol = sb("pcol", [64, NB], I32)

    dma(ptr_t[:], ptab_d[0:1, :].partition_broadcast(128), (), ['ptr_t'])
    cp(iota_i[:], C('iota'), ['cst'], ['iota_i'])
    ts(idx_t[:], ptr_t[:], 128, iota_i[:, 0:1], ALU.mult, ALU.add, ['ptr_t', 'iota_i'], ['idx_t'], eng='pool')
    for b in range(NB):
        dma(pcol[0:NPG, b:b + 1], ptab_d[0:1, b * NPG:(b + 1) * NPG].rearrange("a j -> j a"), (), ['pcol'])

    def load_x(src_rows, ntok, c0):
        for half in range(2):
            st_, sn = next_stage()
            dma(st_[0:ntok, :], src_rows[:, half * 512:(half + 1) * 512], (), [sn])
            pb, pn = bank()
            for k in range(4):
                tr(pb[:, k * 128:k * 128 + ntok], st_[0:ntok, k * 128:(k + 1) * 128], C('ident', rows=ntok, c1=ntok),
                   [sn, 'cst'], [pn])
            outv = xg[:, half * 4:half * 4 + 4, c0:c0 + ntok]
            inv = pb[:, :].rearrange("p (k t) -> p k t", k=4)[:, :, 0:ntok]
            cp(outv, inv, [pn], ['xg'], eng=('dve' if half == 0 else 'act'))

    def rms_to_hT(n, gname, out_f32=None, ores='hT'):
        pb, pn = bank()
        for kc in range(8):
            s, sn = H(kc % 2)
            act(s[:, 0:n], xg[:, kc, 0:n], AF.Square, ['xg'], [sn])
            mm(pb[:, 0:n], onesb[:], s[:, 0:n], kc == 0, kc == 7, [sn] + CB, [pn])
        act(rstd[:, 0:n], pb[:, 0:n], AF.Ln, [pn, 'cb16'], ['rstd'], bias=epst[:, 0:1], scale=1.0 / D)
        act(rstd[:, 0:n], rstd[:, 0:n], AF.Exp, ['rstd'], ['rstd'], scale=-0.5)
        for kc in range(8):
            o = hT[:, kc, 0:n] if out_f32 is None else out_f32[:, kc, 0:n]
            stt(o, xg[:, kc, 0:n], PR(gname, c0=kc, c1=kc + 1), rstd[:, 0:n], ALU.mult, ALU.mult,
                ['xg', 'rstd', 'par'], [ores])

    def ssd_tile(l, A, L, ti, ntok, sample):
        cs = slice(ti * 128, ti * 128 + ntok)
        linc = C('lincS', rows=16) if sample else C('linc')
        ustr = C('ustrS', rows=16) if sample else C('ustr')
        ones_ = C('onesS', rows=16) if sample else C('ones')
        idn = C('ident', rows=ntok, c1=ntok)
        xs_tok, xsn = F(0)
        yo_sb, yon = F(1)
        yv, yvn = F(2)
        y2, y2n = F(3)
        dec, decn = F(4)
        xdt, xdtn = H(2)
        xw, xwn = H(3)
        pa, pan = bank()
        for c in range(4):
            tr(pa[0:ntok, c * 128:(c + 1) * 128], A.xsT[:, c, cs], ident, ['xsT', 'cst'], [pan])
        cp(xs_tok[0:ntok, :], pa[0:ntok, :], [pan], [xsn])
        pb_, pbn = bank()
        for g2 in range(2):
            tr(pb_[0:ntok, g2 * 128:(g2 + 1) * 128], A.BTf[:, g2, cs], ident, ['BT', 'cst'], [pbn])
        cp(A.B_tok[0:ntok, :], pb_[0:ntok, 0:256], [pbn], ['B_tok'], eng='act')
        tt(A.dta[0:ntok, :], A.dts[0:ntok, ti, :], arep[0:ntok, :], ALU.mult, ['dts', 'par2'], ['dta'])
        pc, pcn = bank()
        mm(pc[0:ntok, 0:8], linc, A.dta[0:ntok, :], True, True, ['dta', 'cst'], [pcn])
        mm(pc[0:ntok, 8:16], ones_, A.dta[0:ntok, :], True, True, ['dta', 'cst'], [pcn])
        cp(A.acs[0:ntok, :], pc[0:ntok, 0:8], [pcn], ['acs'], eng='act')
        act(A.ea[0:ntok, :], pc[0:ntok, 0:8], AF.Exp, [pcn], ['ea'])
        tt(A.te[0:ntok, :], pc[0:ntok, 8:16], A.acs[0:ntok, :], ALU.subtract, [pcn, 'acs'], ['te'])
        act(A.te[0:ntok, :], A.te[0:ntok, :], AF.Exp, ['te'], ['te'])
        tt(A.wdt[0:ntok, :], A.te[0:ntok, :], A.dts[0:ntok, ti, :], ALU.mult, ['te', 'dts'], ['wdt'])
        if not sample:
            act(A.cd[:, :], pc[:, 8:16], AF.Exp, [pcn], ['cd'])
        xs3 = xs_tok[0:ntok, :].rearrange("p (h d) -> p h d", h=8)
        tt(xdt[0:ntok, :].rearrange("p (h d) -> p h d", h=8), xs3,
           A.dts[0:ntok, ti, :].unsqueeze(2).to_broadcast([ntok, 8, 64]), ALU.mult, [xsn, 'dts'], [xdtn])
        tt(xw[0:ntok, :].rearrange("p (h d) -> p h d", h=8), xs3,
           A.wdt[0:ntok, :].unsqueeze(2).to_broadcast([ntok, 8, 64]), ALU.mult, [xsn, 'wdt'], [xwn])
        psc, pscn = bank()
        for g2 in range(2):
            mm(psc[0:ntok, g2 * 128:g2 * 128 + ntok], A.BTb[:, g2, cs], A.CTb[:, g2, cs], True, True, ['BT', 'CT'], [pscn])
        for g2 in range(2):
            tt(A.scm[0:ntok, g2, 0:ntok], psc[0:ntok, g2 * 128:g2 * 128 + ntok], linc, ALU.mult, [pscn, 'cst'], ['scm'])
        pY, pYn = bank(True)
        pYo, pYon = bank(True)
        if not sample:
            pSt, pStn = bank(True)
        for g2 in range(2):
            pseg, psegn = bank()
            for hh in range(4):
                h = g2 * 4 + hh
                ld = A.Ld[hh % 2]
                ts(ld[0:ntok, 0:ntok], ustr, A.dta[0:ntok, h:h + 1], None, ALU.mult, None, ['cst', 'dta'], [f'Ld{hh % 2}'])
                mm(pseg[0:ntok, hh * 128:hh * 128 + ntok], ld[0:ntok, 0:ntok], linc, True, True,
                   [f'Ld{hh % 2}', 'cst'], [psegn])
            segv = pseg[0:ntok, :].rearrange("p (h s) -> p h s", h=4)[:, :, 0:ntok]
            decv = dec[0:ntok, :].rearrange("p (h s) -> p h s", h=4)[:, :, 0:ntok]
            act(decv, segv, AF.Exp, [psegn], [decn])
            tt(A.MT[0:ntok, :, 0:ntok], decv, A.scm[0:ntok, g2, 0:ntok].unsqueeze(1).to_broadcast([ntok, 4, ntok]),
               ALU.mult, [decn, 'scm'], ['MT'])
            for hh in range(4):
                h = g2 * 4 + hh
                hsl = slice(h * 64, (h + 1) * 64)
                mm(pY[0:ntok, hsl], A.MT[0:ntok, hh, 0:ntok], xdt[0:ntok, hsl], True, True, ['MT', xdtn], [pYn])
                if not sample:
                    mm(pYo[0:ntok, hsl], A.CTb[:, g2, cs], hstb[:, h, :], True, True, ['CT', 'hstb'], [pYon])
                    mm(pSt[:, hsl], A.B_tok[0:ntok, g2 * 128:(g2 + 1) * 128], xw[0:ntok, hsl], True, True,
                       ['B_tok', xwn], [pStn])
        if sample:
            sample_state(l, A, L, pYo, pYon, xw, xwn)
        tt(yo_sb[0:ntok, :].rearrange("p (h d) -> p h d", h=8), pYo[0:ntok, :].rearrange("p (h d) -> p h d", h=8),
           A.ea[0:ntok, :].unsqueeze(2).to_broadcast([ntok, 8, 64]), ALU.mult, [pYon, 'ea'], [yon])
        tt(yv[0:ntok, :], pY[0:ntok, :], yo_sb[0:ntok, :], ALU.add, [pYn, yon], [yvn])
        release(pYn, pYon)
        tt(y2[0:ntok, :].rearrange("p (h d) -> p h d", h=8), xs3,
           PR('dsk', rows=ntok).unsqueeze(2).to_broadcast([ntok, 8, 64]), ALU.mult, [xsn, 'par'], [y2n])
        tt(yv[0:ntok, :], yv[0:ntok, :], y2[0:ntok, :], ALU.add, [yvn, y2n], [yvn])
        tt(yv[0:ntok, :], yv[0:ntok, :], A.zs[0:ntok, ti, :], ALU.mult, [yvn, 'zs'], [yvn])
        for g2 in range(2):
            act(y2[0:ntok, g2 * 256:(g2 + 1) * 256], yv[0:ntok, g2 * 256:(g2 + 1) * 256], AF.Square, [yvn], [y2n, 'ssq'],
                accum=A.ssq[0:ntok, g2:g2 + 1])
        act(A.rs2[0:ntok, :], A.ssq[0:ntok, :], AF.Ln, ['ssq', 'cb16'], ['rs2'], bias=epst[0:ntok, 0:1], scale=1.0 / 256)
        act(A.rs2[0:ntok, :], A.rs2[0:ntok, :], AF.Exp, ['rs2'], ['rs2'], scale=-0.5)
        for g2 in range(2):
            gsl = slice(g2 * 256, (g2 + 1) * 256)
            stt(y2[0:ntok, gsl], yv[0:ntok, gsl], A.rs2[0:ntok, g2:g2 + 1], PR('gn', rows=ntok, c0=g2 * 256, c1=(g2 + 1) * 256),
                ALU.mult, ALU.mult, [yvn, 'rs2', 'par', y2n], [y2n])
        pT, pTn = bank()
        for c in range(4):
            tr(pT[:, c * 128:c * 128 + ntok], y2[0:ntok, c * 128:(c + 1) * 128], idn, [y2n, 'cst'], [pTn])
        cp(A.yssT[:, :, cs], pT[:, :].rearrange("p (c t) -> p c t", c=4)[:, :, 0:ntok], [pTn], ['yssT'], eng='act')
        if not sample:
            for h in range(8):
                stt(hst[:, h, :], hst[:, h, :], A.cd[:, h:h + 1], pSt[:, h * 64:(h + 1) * 64], ALU.mult, ALU.add,
                    ['hst', 'cd', pStn], ['hst'])
            cp(hstb[:].rearrange("p a b -> p (a b)"), hst[:].rearrange("p a b -> p (a b)"), ['hst'], ['hstb'])
            release(pStn)

    def sample_state(l, A, L, pYo, pYon, xw, xwn):
        for b in range(NB):
            tt(L.CTm[:, b, :, :], A.CTb[:, :, 0:16], C('colB', c0=b * 16, c1=(b + 1) * 16).unsqueeze(1).to_broadcast([128, 2, 16]),
               ALU.mult, ['CT', 'cst'], ['CTm'])
        pc, pcn = bank()
        for b in range(NB):
            mm(pc[:, b * 8:(b + 1) * 8], C('selB', rows=16, c0=b * 128, c1=(b + 1) * 128), A.dta[0:16, :], True, True,
               ['dta', 'cst'], [pcn])
        act(L.cdS[:].rearrange("p a b -> p (a b)"), pc[:, 0:NB * 8], AF.Exp, [pcn], ['cdS'])
        for b in range(NB):
            h0 = L.h0[b % 2]
            h0n = f'h0_{b % 2}'
            dma(h0[:, :, :], sst_d[l, b].rearrange("h p n -> p h n"), (), [h0n])
            for half in range(2):
                pb, pn = bank()
                for k in range(4):
                    h = half * 4 + k
                    tr(pb[:, k * 64:(k + 1) * 64], h0[:, h, :], C('ident', rows=64, c1=64), [h0n, 'cst'], [pn])
                cp(L.h0Tb[:, b, half * 4:half * 4 + 4, :].rearrange("p a b -> p (a b)"), pb[:, 0:256], [pn], ['h0Tb'])
            ts(L.xwm[:, :], xw[0:16, :], C('rowB', rows=16, c0=b, c1=b + 1), None, ALU.mult, None, [xwn, 'cst'], ['xwm'])
            for half in range(2):
                st_, sn = next_stage()
                pb, pn = bank()
                for k in range(4):
                    h = half * 4 + k
                    g2 = h // 4
                    mm(pb[0:64, k * 128:(k + 1) * 128], L.xwm[:, h * 64:(h + 1) * 64], A.B_tok[0:16, g2 * 128:(g2 + 1) * 128],
                       True, True, ['xwm', 'B_tok'], [pn])
                for k in range(4):
                    h = half * 4 + k
                    stt(st_[0:64, k * 128:(k + 1) * 128], h0[:, h, :], L.cdS[0:64, b, h:h + 1], pb[0:64, k * 128:(k + 1) * 128],
                        ALU.mult, ALU.add, [h0n, 'cdS', pn], [sn])
                dma(sssm_o[l, b, half * 4:half * 4 + 4].rearrange("h p n -> p h n"),
                    st_[0:64, :].rearrange("p (h n) -> p h n", h=4), [sn], [])
        for h in range(8):
            g2 = h // 4
            for b in range(NB):
                mm(pYo[0:16, h * 64:(h + 1) * 64], L.CTm[:, b, g2, :], L.h0Tb[:, b, h, :], b == 0, b == NB - 1,
                   ['CTm', 'h0Tb'], [pYon])

    def proj_group(l, A, L, gi, n, tiles, sample):
        rms_to_hT(n, 'g1')
        ck('p_rms')
        c0 = gi * 512
        for which, base in (('sb', 0), ('fx', 4)):
            for j in range(2):
                for kind, cbi in (('q', base + j), ('k', base + 2 + j)):
                    wt, wn = load_w(l, f'fm{cbi}')
                    pb, pn = bank()
                    for kc in range(8):
                        mm(pb[:, 0:n], wt[:, kc * 128:(kc + 1) * 128], hT[:, kc, 0:n], kc == 0, kc == 7, [wn, 'hT'], [pn])
                    if kind == 'q':
                        act(A.QT[which][:, j, 0:n], pb[:, 0:n], AF.Copy, [pn], [f'QT{which}'], scale=0.125)
                    elif sample:
                        cp(L.KTs[which][:, j, 0:n], pb[:, 0:n], [pn], [f'KTs{which}'])
                    else:
                        cp(L.KT[which][:, j, c0:c0 + n], pb[:, 0:n], [pn], [f'KT{which}{gi}'])
        ck('p_qk')
        if not sample:
            wt, wn = load_w(l, 'f4')
            pb, pn = bank()
            for kc in range(8):
                mm(pb[0:4, 0:n], wt[:, kc * 4:(kc + 1) * 4], hT[:, kc, 0:n], kc == 0, kc == 7, [wn, 'hT'], [pn])
            ck('f4a')
            act(lfT[:, 0:n], pb[0:4, 0:n], AF.Exp, [pn, 'par2'], ['lfT'], bias=NBF[0:4, 0:1], scale=-1.0)
            act(lfT[:, 0:n], lfT[:, 0:n], AF.Ln, ['lfT', 'cb16'], ['lfT'], bias=one_t[0:4, 0:1])
            ts(lfT[:, 0:n], lfT[:, 0:n], -1.0, None, ALU.mult, None, ['lfT'], ['lfT'])
            ck('f4b')
            scan(cumT[:, 0:n], onesT[:, 0:n], lfT[:, 0:n], carryFT[:, 0:1], ['lfT', 'cb16', 'carryFT'], ['cumT'])
            ck('f4c')
            cp(carryFT[:, 0:1], cumT[:, n - 1:n], ['cumT'], ['carryFT'])
        ck('p_f4')
        need_tm = sample or (gi == NG - 1)
        if need_tm:
            ntk = 16 if sample else 128
            tcs = slice(0, 16) if sample else slice(384, 512)
            ptm = [bank(True), bank(True)]
        for c in range(8):
            wt, wn = load_w(l, f'fm{8 + c}')
            pb, pn = bank()
            for kc in range(8):
                mm(pb[:, 0:n], wt[:, kc * 128:(kc + 1) * 128], hT[:, kc, 0:n], kc == 0, kc == 7, [wn, 'hT'], [pn])
            if need_tm:
                tb, tbn = ptm[c // 4]
                for kc in range(8):
                    mm(tb[0:ntk, (c % 4) * 128:(c % 4 + 1) * 128], hT[:, kc, tcs], wt[:, kc * 128:(kc + 1) * 128],
                       kc == 0, kc == 7, [wn, 'hT'], [tbn])
            acc, accn = F(5)
            if sample:
                cp(L.XS[:, c, :, 3:7], pb[:, 0:16].rearrange("p (b t) -> p b t", b=NB), [pn], ['XS'])
                xv = lambda i: L.XS[:, c, :, i:i + 4]
                accv = acc[:, 0:16].rearrange("p (b t) -> p b t", b=NB)
                xres = 'XS'
            else:
                Xb = A.Xb[c % 2]
                xres = f'Xb{c % 2}'
                cp(Xb[:, 0:3], ccar[:, c, :], ['ccar'], [xres])
                cp(Xb[:, 3:3 + n], pb[:, 0:n], [pn], [xres], eng='act')
                cp(ccar[:, c, :], Xb[:, n:n + 3], [xres], ['ccar'])
                xv = lambda i: Xb[:, i:i + n]
                accv = acc[:, 0:n]
            ts(accv, xv(0), PR('cw', c0=c * 4, c1=c * 4 + 1), PR('cb', c0=c, c1=c + 1), ALU.mult, ALU.add,
               [xres, 'par'], [accn])
            for i in range(1, 4):
                stt(accv, xv(i), PR('cw', c0=c * 4 + i, c1=c * 4 + i + 1), accv, ALU.mult, ALU.add,
                    [xres, 'par', accn], [accn])
            if c < 4:
                act(A.xsT[:, c, 0:n], acc[:, 0:n], AF.Silu, [accn], ['xsT'])
            elif c < 6:
                act(A.BTf[:, c - 4, 0:n], acc[:, 0:n], AF.Silu, [accn], ['BT'])
                cp(A.BTb[:, c - 4, 0:n], A.BTf[:, c - 4, 0:n], ['BT'], ['BT'])
            else:
                act(A.CTb[:, c - 6, 0:n], acc[:, 0:n], AF.Silu, [accn], ['CT'])
        if need_tm:
            for half in range(2):
                st_, sn = next_stage()
                cp(st_[0:ntk, :], ptm[half][0][0:ntk, :], [ptm[half][1]], [sn], eng=('act' if half else 'dve'))
                if sample:
                    for b in range(NB):
                        dma(scv_o[l, b, :, half * 512:(half + 1) * 512], st_[4 * b + 1:4 * b + 4, :], [sn], [])
                else:
                    dma(pcv_o[l, :, half * 512:(half + 1) * 512], st_[125:128, :], [sn], [])
                release(ptm[half][1])
        ck('p_xbc')
        ntok = 16 if sample else 128
        for piece in ('tmD', 'tmC', 'tmA', 'tmB'):
            wt, wn = load_w(l, piece)
            W = {'tmA': 512, 'tmB': 512, 'tmC': 512, 'tmD': 12}[piece]
            for ti, tix in enumerate(tiles):
                cs = slice(ti * 128, ti * 128 + ntok)
                pb, pn = bank()
                for kc in range(8):
                    mm(pb[0:ntok, 0:W], hT[:, kc, cs], wt[:, kc * W:(kc + 1) * W], kc == 0, kc == 7, [wn, 'hT'], [pn])
                r0 = (tix * 128) if not sample else 0
                if piece in ('tmA', 'tmB'):
                    which = 'sb' if piece == 'tmA' else 'fx'
                    st_, sn = next_stage()
                    cp(st_[0:ntok, 0:512], pb[0:ntok, :], [pn], [sn])
                    if sample:
                        ko, vo = (ssbk_o, ssbv_o) if which == 'sb' else (sfxk_o, sfxv_o)
                        cp(L.Vnew[which][0:16, :], pb[0:16, 256:512], [pn], [f'Vnew{which}'], eng='act')
                    else:
                        ko, vo = (psbk_o, psbv_o) if which == 'sb' else (pfxk_o, pfxv_o)
                        cp(L.V[which][:, tix, :], pb[:, 256:512], [pn], [f'V{which}{tix}'], eng='act')
                    dma(ko[l, r0:r0 + ntok, :], st_[0:ntok, 0:256], [sn], [])
                    dma(vo[l, r0:r0 + ntok, :], st_[0:ntok, 256:512], [sn], [])
                elif piece == 'tmC':
                    act(A.zs[0:ntok, ti, :], pb[0:ntok, :], AF.Silu, [pn], ['zs'])
                else:
                    tt(tmp8[0:ntok, 0:4], pb[0:ntok, 0:4], PR('bf_rep', rows=ntok), ALU.add, [pn, 'par'], ['tmp8'])
                    act(tmp8[0:ntok, 0:4], tmp8[0:ntok, 0:4], AF.Exp, ['tmp8'], ['tmp8'], scale=-1.0)
                    act(tmp8[0:ntok, 0:4], tmp8[0:ntok, 0:4], AF.Ln, ['tmp8', 'cb16'], ['tmp8'], bias=one_t[0:ntok, 0:1])
                    ts(lf_tok[0:ntok, :], tmp8[0:ntok, 0:4], -1.0, None, ALU.mult, None, ['tmp8'], ['lf_tok'])
                    lo = slf_o if sample else plf_o
                    dma(lo[l, r0:r0 + ntok, :], lf_tok[0:ntok, :], ['lf_tok'], [])
                    if sample:
                        cp(L.lfS[0:16, :], lf_tok[0:16, :], ['lf_tok'], ['lfS'])
                    else:
                        pq, pqn = bank()
                        mm(pq[:, 0:4], C('linc'), lf_tok[:, :], True, True, ['lf_tok', 'cst'], [pqn])
                        mm(pq[:, 4:8], C('ones'), lf_tok[:, :], True, True, ['lf_tok', 'cst'], [pqn])
                        stt(L.negFk[:, tix, :], pq[:, 0:4], -1.0, carryF[:, :], ALU.mult, ALU.subtract, [pqn, 'carryF'],
                            [f'negFk{tix}'])
                        tt(carryF[:, :], carryF[:, :], pq[:, 4:8], ALU.add, [pqn, 'carryF'], ['carryF'])
                    tt(tmp8[0:ntok, 8:16], pb[0:ntok, 4:12], PR('dtb', rows=ntok), ALU.add, [pn, 'par'], ['tmp8'])
                    act(tmp8[0:ntok, 8:16], tmp8[0:ntok, 8:16], AF.Exp, ['tmp8'], ['tmp8'])
                    act(A.dts[0:ntok, ti, :], tmp8[0:ntok, 8:16], AF.Ln, ['tmp8', 'cb16'], ['dts'], bias=one_t[0:ntok, 0:1])
            ck('tm_' + piece)
        ck('p_tm')
        for ti, tix in enumerate(tiles):
            ssd_tile(l, A, L, ti, ntok, sample)
            ck(f'p_ssd{ti}')

    def attn_prompt(l, A, L, gi):
        nkb = 4 * gi + 4
        spsum, spsn = F(2)
        spsumb, spsbn = H(0)
        for hc in range(2):
            pO, pOn = bank(True)
            for hp in range(2):
                h = hc * 2 + hp
                ps_ = slice(hp * 64, hp * 64 + 64)
                kbs = list(range(nkb - 1, -1, -1))
                st = {}

                def s0(idx):
                    kb = kbs[idx]
                    di = kb - 4 * gi
                    kres = f'KTsb{kb // 4}'
                    ksl = slice(kb * 128, (kb + 1) * 128)
                    pz, pzn = bank()
                    mm(pz[:, :], L.KT['sb'][ps_, hc, ksl], A.QT['sb'][ps_, hc, :], True, True, [kres, 'QTsb'], [pzn])
                    tb, tbn = F(idx % 2)
                    sp_, spn = H(1 + idx % 3)
                    act(tb[:], pz[:, :], AF.Exp, [pzn], [tbn])
                    act(sp_[:], tb[:], AF.Ln, [tbn, 'cb16'], [spn], bias=one_t[:, 0:1])
                    if di >= 0:
                        tt(sp_[:], sp_[:], msb(di), ALU.mult, [spn] + CB, [spn])
                    st[idx] = (kb, di, kres, ksl, sp_, spn)

                def s1(idx):
                    kb, di, kres, ksl, sp_, spn = st[idx]
                    w_, wn_ = H(4 + idx % 3)
                    pe_, pen = bank()
                    mm(pe_[:, :], L.KT['sb'][ps_, hc, ksl], A.QT['sb'][ps_, hc, :], True, False, [kres, 'QTsb'], [pen])
                    mm(pe_[:, :], nuincb[:], sp_[:], False, idx == 0, [spn] + CB, [pen])
                    if idx > 0:
                        mm(pe_[:, :], nonesb[:], spsumb[:], False, True, [spsbn] + CB, [pen])
                    act(w_[:], pe_[:, :], AF.Exp, [pen], [wn_])
                    if di >= 0:
                        tt(w_[:], w_[:], msb(di), ALU.mult, [wn_] + CB, [wn_])
                    if kb > 0:
                        if idx == 0:
                            cp(spsum[:], sp_[:], [spn], [spsn])
                        else:
                            tt(spsum[:], spsum[:], sp_[:], ALU.add, [spsn, spn], [spsn])
                        cp(spsumb[:], spsum[:], [spsn], [spsbn])
                    st[idx] = st[idx] + (w_, wn_)

                def s2(idx):
                    kb = st[idx][0]
                    w_, wn_ = st[idx][6], st[idx][7]
                    mm(pO[ps_, :], L.V['sb'][:, kb, h * 64:(h + 1) * 64], w_[:], idx == 0, kb == 0, [f'Vsb{kb}', wn_], [pOn])

                for it in range(nkb + 2):
                    if it < nkb:
                        s0(it)
                    if 0 <= it - 1 < nkb:
                        s1(it - 1)
                    if 0 <= it - 2 < nkb:
                        s2(it - 2)
            cp(A.yT['sb'][:, hc, :], pO[:, :], [pOn], ['ysbT'])
            release(pOn)
        fq, fqn = F(4)
        rec, recn = F(3)
        for hc in range(2):
            pN, pNn = bank(True)
            pD, pDn = bank(True)
            for hp in range(2):
                h = hc * 2 + hp
                ps_ = slice(hp * 64, hp * 64 + 64)
                pq, pqn = bank()
                mm(pq[:, :], sel4[:, h, :], cumT[:, :], True, True, ['cb16', 'cumT'], [pqn])
                cp(fq[:], pq[:, :], [pqn], [fqn], eng='act')
                st = {}

                def f0(kb):
                    di = kb - 4 * gi
                    kres = f'KTfx{kb // 4}'
                    ksl = slice(kb * 128, (kb + 1) * 128)
                    pz, pzn = bank()
                    mm(pz[:, :], L.KT['fx'][ps_, hc, ksl], A.QT['fx'][ps_, hc, :], True, True, [kres, 'QTfx'], [pzn])
                    tb, tbn = F(kb % 2)
                    w_, wn_ = H(4 + kb % 3)
                    stt(tb[:], pz[:, :], L.negFk[:, kb, h:h + 1], fq[:], ALU.add, ALU.add, [pzn, f'negFk{kb}', fqn], [tbn])
                    if di >= 0:
                        ts(tb[:], tb[:], 80.0, None, ALU.min, None, [tbn], [tbn])
                    act(w_[:], tb[:], AF.Exp, [tbn], [wn_])
                    if di >= 0:
                        tt(w_[:], w_[:], mfx(di), ALU.mult, [wn_] + CB, [wn_])
                    st[kb] = (w_, wn_)

                def f1(kb):
                    w_, wn_ = st[kb]
                    mm(pN[ps_, :], L.V['fx'][:, kb, h * 64:(h + 1) * 64], w_[:], kb == 0, kb == nkb - 1, [f'Vfx{kb}', wn_], [pNn])
                    mm(pD[ps_, :], onesb[:, 0:64], w_[:], kb == 0, kb == nkb - 1, [wn_] + CB, [pDn])

                for it in range(nkb + 1):
                    if it < nkb:
                        f0(it)
                    if 0 <= it - 1 < nkb:
                        f1(it - 1)
                recip(rec[ps_, :], pD[ps_, :], [pDn], [recn])
                tt(A.yT['fx'][ps_, hc, :], pN[ps_, :], rec[ps_, :], ALU.mult, [pNn, recn], ['yfxT'])
            release(pNn, pDn)

    def merge(l, A, n):
        for dc in range(8):
            pbr = []
            for bi, (nm, ysrc, nk, yres) in enumerate((('sbo', A.yT['sb'], 2, 'ysbT'), ('sso', A.yssT, 4, 'yssT'),
                                                        ('fxo', A.yT['fx'], 2, 'yfxT'))):
                wt, wn = load_w(l, f'fm{16 + bi * 8 + dc}')
                pg, pgn = bank()
                for kc in range(8):
                    mm(pg[:, 0:n], wt[:, kc * 128:(kc + 1) * 128], hT[:, kc, 0:n], kc == 0, kc == 7, [wn, 'hT'], [pgn])
                g_, gn_ = F(bi)
                act(g_[:, 0:n], pg[:, 0:n], AF.Sigmoid, [pgn], [gn_])
                wt, wn = load_w(l, f'{nm}{dc}')
                pb, pn = bank(True)
                for kc in range(nk):
                    mm(pb[:, 0:n], wt[:, kc * 128:(kc + 1) * 128], ysrc[:, kc, 0:n], kc == 0, kc == nk - 1, [wn, yres], [pn])
                pbr.append((pb, pn))
            ma, man = F(3)
            mb, mbn = F(4)
            tt(ma[:, 0:n], pbr[0][0][:, 0:n], Fp[0][:, 0:n], ALU.mult, [pbr[0][1], 'F0'], [man])
            tt(mb[:, 0:n], pbr[1][0][:, 0:n], Fp[1][:, 0:n], ALU.mult, [pbr[1][1], 'F1'], [mbn])
            tt(ma[:, 0:n], ma[:, 0:n], mb[:, 0:n], ALU.add, [man, mbn], [man])
            tt(mb[:, 0:n], pbr[2][0][:, 0:n], Fp[2][:, 0:n], ALU.mult, [pbr[2][1], 'F2'], [mbn])
            tt(A.mixT[:, dc, 0:n], ma[:, 0:n], mb[:, 0:n], ALU.add, [man, mbn], ['mixT'])
            release(*[x[1] for x in pbr])
        for dc in range(8):
            wt, wn = load_w(l, f'wo{dc}')
            pb, pn = bank()
            for kc in range(8):
                mm(pb[:, 0:n], wt[:, kc * 128:(kc + 1) * 128], A.mixT[:, kc, 0:n], kc == 0, kc == 7, [wn, 'mixT'], [pn])
            tt(xg[:, dc, 0:n], xg[:, dc, 0:n], pb[:, 0:n], ALU.add, ['xg', pn], ['xg'])

    def ffn(l, Dd, n):
        rms_to_hT(n, 'g2')
        for fc in range(32):
            wt, wn = load_w(l, f'up{fc}')
            pb, pn = bank()
            for kc in range(8):
                mm(pb[:, 0:n], wt[:, kc * 128:(kc + 1) * 128], hT[:, kc, 0:n], kc == 0, kc == 7, [wn, 'hT'], [pn])
            tb, tbn = F(fc % 2)
            act(tb[:, 0:n], pb[:, 0:n], AF.Relu, [pn], [tbn])
            tt(Dd.uT[:, fc, 0:n], tb[:, 0:n], tb[:, 0:n], ALU.mult, [tbn], ['uT'])
        for dc in range(8):
            wt, wn = load_w(l, f'dn{dc}')
            pb, pn = bank()
            for kc in range(32):
                mm(pb[:, 0:n], wt[:, kc * 128:(kc + 1) * 128], Dd.uT[:, kc, 0:n], kc == 0, kc == 31, [wn, 'uT'], [pn])
            tt(xg[:, dc, 0:n], xg[:, dc, 0:n], pb[:, 0:n], ALU.add, ['xg', pn], ['xg'])

    def final_out(Dd, n, tiles, sample):
        rms_to_hT(n, 'gf', out_f32=Dd.finT, ores='finT')
        ntok = 16 if sample else 128
        for ti, tix in enumerate(tiles):
            cs = slice(ti * 128, ti * 128 + ntok)
            for half in range(2):
                st_, sn = next_stage()
                pb, pn = bank()
                for k in range(4):
                    kc = half * 4 + k
                    tr(pb[0:ntok, k * 128:(k + 1) * 128], Dd.finT[:, kc, cs], ident, ['finT', 'cst'], [pn])
                cp(st_[0:ntok, :], pb[0:ntok, :], [pn], [sn], eng=('act' if half else 'dve'))
                if sample:
                    dma(ys_o[:, half * 512:(half + 1) * 512], st_[0:16, :], [sn], [])
                else:
                    dma(yp_o[tix * 128:(tix + 1) * 128, half * 512:(half + 1) * 512], st_[:, :], [sn], [])

    def gather(out_tile, res, src3d, l, b, j):
        col = b * NPG + j
        flat = src3d.rearrange("g r c -> (g r) c")
        P.dma('pool', lambda e: e.indirect_dma_start(out=out_tile[:], out_offset=None, in_=flat,
                                                     in_offset=bass.IndirectOffsetOnAxis(ap=idx_t[:, col:col + 1], axis=0),
                                                     element_offset=l * NPOOL * 128 * 256),
              ['idx_t'], [res])

    def attn_decode(l, A, L):
        NC_ = NPG * 8
        assert NC_ <= 512
        memset(L.KTn[:].rearrange("p a b -> p (a b)"), 0.0, ['KTn'])
        memset(L.VnP[:], 0.0, ['VnP'])
        memset(L.cn4[:], 0.0, ['cn4'])
        S1, S2, S3 = L.S1, L.S2, L.S3
        for which in ('sb', 'fx'):
            kd, vd = (csk_d, csv_d) if which == 'sb' else (cfk_d, cfv_d)
            for b in range(NB):
                qc = slice(b * 4, b * 4 + 4)
                memset(L.Qbd[:].rearrange("p a b -> p (a b)"), 0.0, ['Qbd'])
                for hc in range(2):
                    for hp in range(2):
                        ps_ = slice(hp * 64, hp * 64 + 64)
                        cp(L.Qbd[ps_, hc, hp * 4:hp * 4 + 4], A.QT[which][ps_, hc, qc], [f'QT{which}', 'Qbd'], ['Qbd'])
                    cp(L.KTn[:, hc, 0:4], L.KTs[which][:, hc, qc], [f'KTs{which}', 'KTn'], ['KTn'])
                pS = [bank(True), bank(True)]
                for j in range(NPG):
                    jc = (NPG - 1 - j) if which == 'sb' else j
                    kt_, ktn = L.kpg[j % 5], f'kpg{j % 5}'
                    gather(kt_, ktn, kd, l, b, j)
                    pt_, ptn = bank()
                    for hc in range(2):
                        tr(pt_[:, hc * 128:(hc + 1) * 128], kt_[:, hc * 128:(hc + 1) * 128], ident, [ktn, 'cst'], [ptn])
                    kb_, kbn = L.ktb[j % 4], f'ktb{j % 4}'
                    cp(kb_[:], pt_[:, 0:256], [ptn], [kbn], eng=('act' if j % 2 else 'dve'))
                    for hc in range(2):
                        mm(pS[hc][0][:, jc * 8:(jc + 1) * 8], kb_[:, hc * 128:(hc + 1) * 128], L.Qbd[:, hc, :], True, True,
                           [kbn, 'Qbd'], [pS[hc][1]])
                pSn, pSnn = bank(True)
                for hc in range(2):
                    mm(pSn[:, hc * 8:(hc + 1) * 8], L.KTn[:, hc, :], L.Qbd[:, hc, :], True, True, ['KTn', 'Qbd'], [pSnn])
                mnew = C('mnsb') if which == 'sb' else C('mnfx')
                mnew2 = mnew.unsqueeze(1).to_broadcast([128, 2, 8])
                Sn1, Sn2, Wn = L.Sn1, L.Sn2, L.Wn
                Sn1v = Sn1[:, :].rearrange("p (a b) -> p a b", a=2)
                Wnv = Wn[:, :].rearrange("p (a b) -> p a b", a=2)
                if which == 'sb':
                    for hc in range(2):
                        act(S1[:, hc * NC_:(hc + 1) * NC_], pS[hc][0][:, 0:NC_], AF.Exp, [pS[hc][1]], ['S1'])
                    act(S1[:, 0:2 * NC_], S1[:, 0:2 * NC_], AF.Ln, ['S1', 'cb16'], ['S1'], bias=one_t[:, 0:1])
                    act(Sn1[:, :], pSn[:, 0:16], AF.Exp, [pSnn], ['Sn1'])
                    act(Sn1[:, :], Sn1[:, :], AF.Ln, ['Sn1', 'cb16'], ['Sn1'], bias=one_t[:, 0:1])
                    tt(Sn1v, Sn1v, mnew2, ALU.mult, ['Sn1', 'cst'], ['Sn1'])
                    pcs = [bank(True), bank(True)]
                    for hc in range(2):
                        mm(pcs[hc][0][:, 0:NC_], C('uincr'), S1[:, hc * NC_:(hc + 1) * NC_], True, True, ['S1', 'cst'], [pcs[hc][1]])
                    pn_, pnn = bank(True)
                    mm(pn_[:, 0:16], C('uincr'), Sn1[:, :], True, True, ['Sn1', 'cst'], [pnn])
                    mm(pn_[:, 16:32], C('ones'), Sn1[:, :], True, True, ['Sn1', 'cst'], [pnn])
                    cp(Sn2[:, :], pn_[:, 16:32], [pnn], ['Sn2'])
                    for hc in range(2):
                        ptt_, pttn = bank()
                        mm(ptt_[:, 0:NC_], C('ones'), S1[:, hc * NC_:(hc + 1) * NC_], True, True, ['S1', 'cst'], [pttn])
                        cp(S2[:, hc * NC_:(hc + 1) * NC_], ptt_[:, 0:NC_], [pttn], ['S2'], eng='act')
                    for hc in range(2):
                        for c8 in range(8):
                            colv = slice(hc * NC_ + c8, (hc + 1) * NC_, 8)
                            scan(S3[:, colv], ones_f[:, 0:NPG], S2[:, colv], Sn2[:, hc * 8 + c8:hc * 8 + c8 + 1],
                                 ['S2', 'Sn2', 'cb16'], ['S3'])
                    for hc in range(2):
                        sl = slice(hc * NC_, (hc + 1) * NC_)
                        tt(S3[:, sl], S3[:, sl], S2[:, sl], ALU.subtract, ['S3', 'S2'], ['S3'])
                        tt(S3[:, sl], S3[:, sl], pcs[hc][0][:, 0:NC_], ALU.add, ['S3', pcs[hc][1]], ['S3'])
                        tt(S3[:, sl], pS[hc][0][:, 0:NC_], S3[:, sl], ALU.subtract, [pS[hc][1], 'S3'], ['S3'])
                    act(S3[:, 0:2 * NC_], S3[:, 0:2 * NC_], AF.Exp, ['S3'], ['S3'])
                    cp(L.S3b[:, 0:2 * NC_], S3[:, 0:2 * NC_], ['S3'], ['S3b'])
                    cp(Sn1[:, :], pn_[:, 0:16], [pnn], ['Sn1'])
                    tt(Wn[:, :], pSn[:, 0:16], Sn1[:, :], ALU.subtract, [pSnn, 'Sn1'], ['Wn'])
                    act(Wn[:, :], Wn[:, :], AF.Exp, ['Wn'], ['Wn'])
                    tt(Wnv, Wnv, mnew2, ALU.mult, ['Wn', 'cst'], ['Wn'])
                    release(pcs[0][1], pcs[1][1], pnn)
                else:
                    lfp, lfw, tot64, Fkd = L.lfp, L.lfw, L.tot64, L.Fkd
                    P.dma('pool', lambda e, b=b: e.indirect_dma_start(
                        out=lfp[0:NPG, :], out_offset=None, in_=clf_d.rearrange("l p c -> (l p) c"),
                        in_offset=bass.IndirectOffsetOnAxis(ap=pcol[0:NPG, b:b + 1], axis=0),
                        element_offset=l * NPOOL * 512), ['pcol'], ['lfp'])
                    for h in range(4):
                        colv = slice(h, 512, 4)
                        scan(lfw[0:NPG, colv], ones_f[0:NPG, 0:128], lfp[0:NPG, colv], 0.0, ['lfp', 'cb16'], ['lfw'])
                    cp(tot64[0:NPG, :], lfw[0:NPG, 508:512], ['lfw'], ['tot64'])
                    pq, pqn = bank()
                    mm(pq[0:NPG, 0:4], C('linc', rows=NPG, c1=NPG), tot64[0:NPG, :], True, True, ['tot64', 'cst'], [pqn])
                    mm(pq[:, 8:12], C('ones', rows=NPG), tot64[0:NPG, :], True, True, ['tot64', 'cst'], [pqn])
                    tt(tot64[0:NPG, :], pq[0:NPG, 0:4], tot64[0:NPG, :], ALU.subtract, [pqn, 'tot64'], ['tot64'])
                    lfw3 = lfw[0:NPG, :].rearrange("p (r h) -> p r h", h=4)
                    tt(lfw3, lfw3, tot64[0:NPG, :].unsqueeze(1).to_broadcast([NPG, 128, 4]), ALU.add, ['lfw', 'tot64'], ['lfw'])
                    cp(L.ftot[:, :], pq[:, 8:12], [pqn], ['ftot'])
                    for h in range(4):
                        pt_, ptn = bank()
                        tr(pt_[:, 0:NPG], lfw[0:NPG, h:512:4], C('ident', rows=NPG, c1=NPG), ['lfw', 'cst'], [ptn])
                        ts(Fkd[:, 0:NPG, h], pt_[:, 0:NPG], -1.0, L.ftot[:, h:h + 1], ALU.mult, ALU.add, [ptn, 'ftot'], ['Fkd'])
                    for hc in range(2):
                        sl = slice(hc * NC_, (hc + 1) * NC_)
                        o3 = S3[:, sl].rearrange("p (j a q) -> p j a q", a=2, q=4)
                        i3 = pS[hc][0][:, 0:NC_].rearrange("p (j a q) -> p j a q", a=2, q=4)
                        f3 = Fkd[:, 0:NPG, hc * 2:hc * 2 + 2].unsqueeze(3).to_broadcast([128, NPG, 2, 4])
                        tt(o3, i3, f3, ALU.add, [pS[hc][1], 'Fkd'], ['S3'])
                    act(S3[:, 0:2 * NC_], S3[:, 0:2 * NC_], AF.Exp, ['S3'], ['S3'])
                    cp(L.S3b[:, 0:2 * NC_], S3[:, 0:2 * NC_], ['S3'], ['S3b'])
                    pn_, pnn = bank()
                    mm(pn_[0:16, 0:4], C('lincS', rows=16), L.lfS[0:16, :], True, True, ['lfS', 'cst'], [pnn])
                    cp(L.cnS[0:16, :], pn_[0:16, 0:4], [pnn], ['cnS'])
                    mm(pn_[0:4, 8:12], C('ident', rows=16, c0=b * 4, c1=b * 4 + 4), L.cnS[0:16, :], True, True, ['cnS', 'cst'], [pnn])
                    cp(L.cn4[0:4, :], pn_[0:4, 8:12], [pnn, 'cn4'], ['cn4'])
                    o3 = Wn[:, :].rearrange("p (h q) -> p h q", q=4)
                    i3 = pSn[:, 0:16].rearrange("p (h q) -> p h q", q=4)
                    tt(o3, i3, L.cn4[:, :].unsqueeze(2).to_broadcast([128, 4, 4]), ALU.subtract, [pSnn, 'cn4'], ['Wn'])
                    act(Wn[:, :], Wn[:, :], AF.Exp, ['Wn'], ['Wn'])
                    tt(Wnv, Wnv, mnew2, ALU.mult, ['Wn', 'cst'], ['Wn'])
                release(pS[0][1], pS[1][1], pSnn)
                pO = [bank(True), bank(True)]
                for j in range(NPG):
                    jc = (NPG - 1 - j) if which == 'sb' else j
                    vt_, vtn = L.vpg[j % 5], f'vpg{j % 5}'
                    gather(vt_, vtn, vd, l, b, j)
                    vb_, vbn = L.ktb[j % 4], f'ktb{j % 4}'
                    cp(vb_[:], vt_[:], [vtn], [vbn], eng=('act' if j % 2 else 'dve'))
                    for hc in range(2):
                        mm(pO[hc][0][:, 0:8], vb_[:, hc * 128:(hc + 1) * 128], L.S3b[:, hc * NC_ + jc * 8:hc * NC_ + jc * 8 + 8],
                           j == 0, False, [vbn, 'S3b'], [pO[hc][1]])
                dma(L.VnP[0:4, :], L.Vnew[which][b * 4:b * 4 + 4, :], [f'Vnew{which}', 'VnP'], ['VnP'])
                cp(L.VnPb[:], L.VnP[:], ['VnP'], ['VnPb'])
                cp(L.Wnb[:], Wn[:, :], ['Wn'], ['Wnb'])
                for hc in range(2):
                    mm(pO[hc][0][:, 0:8], L.VnPb[:, hc * 128:(hc + 1) * 128], L.Wnb[:, hc * 8:(hc + 1) * 8], False, True,
                       ['VnPb', 'Wnb'], [pO[hc][1]])
                if which == 'fx':
                    for hc in range(2):
                        sl = slice(hc * NC_, (hc + 1) * NC_)
                        P.op('dve', lambda e, sl=sl, hc=hc: e.tensor_reduce(
                            L.ydec[:, hc, :], S3[:, sl].rearrange("p (j c) -> p c j", c=8), AX.X, ALU.add), ['S3'], ['ydec'])
                        tt(L.ydec[:, hc, :], L.ydec[:, hc, :], Wn[:, hc * 8:(hc + 1) * 8], ALU.add, ['ydec', 'Wn'], ['ydec'])
                        pd_, pdn = bank()
                        mm(pd_[:, 0:8], C('ones'), L.ydec[:, hc, :], True, True, ['ydec', 'cst'], [pdn])
                        recip(L.rden[:, hc, :], pd_[:, 0:8], [pdn], ['rden'])
                for hc in range(2):
                    for hp in range(2):
                        ps_ = slice(hp * 64, hp * 64 + 64)
                        src = pO[hc][0][ps_, hp * 4:hp * 4 + 4]
                        if which == 'fx':
                            tt(A.yT['fx'][ps_, hc, qc], src, L.rden[ps_, hc, hp * 4:hp * 4 + 4], ALU.mult, [pO[hc][1], 'rden'], ['yfxT'])
                        else:
                            cp(A.yT['sb'][ps_, hc, qc], src, [pO[hc][1]], ['ysbT'])
                release(pO[0][1], pO[1][1])

    def alloc_A(S):
        A = Bag()
        A.QT = {'sb': S.sb("QTsb", [128, 2, 512], BF16), 'fx': S.sb("QTfx", [128, 2, 512], BF16)}
        A.xsT = S.sb("xsT", [128, 4, 512])
        A.BTf = S.sb("BTf", [128, 2, 512])
        A.BTb = S.sb("BTb", [128, 2, 512], BF16)
        A.CTb = S.sb("CTb", [128, 2, 512], BF16)
        A.zs = S.sb("zs", [128, 4, 512])
        A.dts = S.sb("dts", [128, 4, 8])
        A.yT = {'sb': S.sb("ysbT", [128, 2, 512], BF16), 'fx': S.sb("yfxT", [128, 2, 512], BF16)}
        A.yssT = S.sb("yssT", [128, 4, 512], BF16)
        A.mixT = S.sb("mixT", [128, 8, 512], BF16)
        A.Xb = [S.sb(f"Xb{i}", [128, 515]) for i in range(2)]
        A.B_tok = S.sb("B_tok", [128, 256], BF16)
        for nm in ('dta', 'acs', 'ea', 'te', 'wdt', 'cd'):
            setattr(A, nm, S.sb(nm, [128, 8]))
        A.scm = S.sb("scm", [128, 2, 128])
        A.Ld = [S.sb(f"Ld{i}", [128, 128]) for i in range(2)]
        A.MT = S.sb("MT", [128, 4, 128], BF16)
        A.ssq = S.sb("ssq", [128, 2])
        A.rs2 = S.sb("rs2", [128, 2])
        return A

    groups = [(gi, 512, list(range(gi * 4, gi * 4 + 4)), False) for gi in range(NG)]
    groups.append((NG, NS, [0], True))
    try:
        cast_layer(0)
        for l in range(DEPTH):
            dma(par[:], par_d[l], (), ['par'])
            act(arep[:], PR('alog'), AF.Exp, ['par'], ['par2'])
            ts(arep[:], arep[:], -1.0, None, ALU.mult, None, ['par2'], ['par2'])
            ts(NBF[:], PR('bf_col'), -1.0, None, ALU.mult, None, ['par'], ['par2'])
            memset(carryF[:], 0.0, ['carryF'])
            memset(carryFT[:], 0.0, ['carryFT'])
            memset(ccar[:].rearrange("p a b -> p (a b)"), 0.0, ['ccar'])
            memset(hst[:].rearrange("p a b -> p (a b)"), 0.0, ['hst'])
            memset(hstb[:].rearrange("p a b -> p (a b)"), 0.0, ['hstb'])
            if l + 1 < DEPTH:
                cast_layer(l + 1)
            LS = Scope()
            L = Bag()
            L.KT = {'sb': LS.sb("KTsb", [128, 2, T], BF16), 'fx': LS.sb("KTfx", [128, 2, T], BF16)}
            L.V = {'sb': LS.sb("Vsb", [128, NT, 256], BF16), 'fx': LS.sb("Vfx", [128, NT, 256], BF16)}
            L.negFk = LS.sb("negFk", [128, NT, 4])
            for (gi, n, tiles, sample) in groups:
                if sample:
                    LS.close()
                    LS = Scope()
                    L = Bag()
                    L.KTs = {'sb': LS.sb("KTssb", [128, 2, 16], BF16), 'fx': LS.sb("KTsfx", [128, 2, 16], BF16)}
                    L.Vnew = {'sb': LS.sb("Vnsb", [16, 256]), 'fx': LS.sb("Vnfx", [16, 256])}
                    L.lfS = LS.sb("lfS", [16, 4])
                    L.XS = LS.sb("XS", [128, 8, NB, 7])
                    L.h0 = [LS.sb(f"h0_{i}", [64, 8, 128]) for i in range(2)]
                    L.h0Tb = LS.sb("h0Tb", [128, NB, 8, 64], BF16)
                    L.CTm = LS.sb("CTm", [128, NB, 2, 16], BF16)
                    L.xwm = LS.sb("xwm", [16, 512], BF16)
                    L.cdS = LS.sb("cdS", [128, NB, 8])
                    L.kpg = [LS.sb(f"kpg{i}", [128, 256]) for i in range(5)]
                    L.vpg = [LS.sb(f"vpg{i}", [128, 256]) for i in range(5)]
                    L.ktb = [LS.sb(f"ktb{i}", [128, 256], BF16) for i in range(4)]
                    L.Qbd = LS.sb("Qbd", [128, 2, 8], BF16)
                    L.KTn = LS.sb("KTn", [128, 2, 128], BF16)
                    L.S1 = LS.sb("S1", [128, 1024])
                    L.S2 = LS.sb("S2", [128, 1024])
                    L.S3 = LS.sb("S3", [128, 1024])
                    L.S3b = LS.sb("S3b", [128, 1024], BF16)
                    L.VnPb = LS.sb("VnPb", [128, 256], BF16)
                    L.Wnb = LS.sb("Wnb", [128, 16], BF16)
                    L.Sn1 = LS.sb("Sn1", [128, 16])
                    L.Sn2 = LS.sb("Sn2", [128, 16])
                    L.Wn = LS.sb("Wn", [128, 16])
                    L.lfp = LS.sb("lfp", [64, 512])
                    L.lfw = LS.sb("lfw", [64, 512])
                    L.Fkd = LS.sb("Fkd", [128, 64, 4])
                    L.tot64 = LS.sb("tot64", [64, 4])
                    L.ftot = LS.sb("ftot", [128, 4])
                    L.ydec = LS.sb("ydec", [128, 2, 8])
                    L.rden = LS.sb("rden", [128, 2, 8])
                    L.VnP = LS.sb("VnP", [128, 256])
                    L.cnS = LS.sb("cnS", [16, 4])
                    L.cn4 = LS.sb("cn4", [128, 4])
                if l == 0:
                    if sample:
                        load_x(xs_d, NS, 0)
                    else:
                        for ti in range(4):
                            load_x(xp_d[(gi * 4 + ti) * 128:(gi * 4 + ti + 1) * 128, :], 128, ti * 128)
                else:
                    dma(xg[:].rearrange("p a b -> p (a b)"), xscr[gi], [f'xscr{gi}'], ['xg'])
                ck(f'load{l}.{gi}')
                AS = Scope()
                A = alloc_A(AS)
                if sample:
                    dma(L.XS[:, :, :, 0:3], scv_d[l], (), ['XS'])
                proj_group(l, A, L, gi, n, tiles, sample)
                ck(f'proj{l}.{gi}')
                if sample:
                    attn_decode(l, A, L)
                else:
                    attn_prompt(l, A, L, gi)
                    if gi == NG - 1:
                        for half in range(2):
                            st_, sn = next_stage()
                            pb, pn = bank()
                            for k in range(4):
                                h = half * 4 + k
                                tr(pb[0:64, k * 128:(k + 1) * 128], hst[:, h, :], ident, ['hst', 'cst'], [pn])
                            cp(st_[0:64, :], pb[0:64, :], [pn], [sn])
                            dma(pssm_o[l, half * 4:half * 4 + 4].rearrange("h p n -> p h n"),
                                st_[0:64, :].rearrange("p (h n) -> p h n", h=4), [sn], [])
                ck(f'attn{l}.{gi}')
                merge(l, A, n)
                ck(f'merge{l}.{gi}')
                AS.close()
                DS = Scope()
                Dd = Bag()
                Dd.uT = DS.sb("uT", [128, 32, 512], BF16)
                if l == DEPTH - 1:
                    Dd.finT = DS.sb("finT", [128, 8, 512])
                ffn(l, Dd, n)
                ck(f'ffn{l}.{gi}')
                if l == DEPTH - 1:
                    final_out(Dd, n, tiles, sample)
                else:
                    dma(xscr[gi], xg[:].rearrange("p a b -> p (a b)"), ['xg'], [f'xscr{gi}'])
                DS.close()
            LS.close()
    except _Stop:
        pass
    for sc in reversed(list(open_scopes)):
        if sc is not G:
            sc.close()
    P.emit()
    G.es.close()
    pes.close()
    return nc


def host_inputs(inp, cfg, core):
    T, NB, NPG, NPOOL, DEPTH = cfg['T'], cfg['NB'], cfg['NPG'], cfg['NPOOL'], cfg['DEPTH']
    b0 = core * NB
    m = {}
    m['xp'] = np.ascontiguousarray(inp['x_prompt'][core])
    m['xs'] = np.ascontiguousarray(inp['x_sample'][b0:b0 + NB].reshape(NB * 4, D))
    m['csk'] = inp['cache_sb_k'].reshape(DEPTH * NPOOL, 128, 256)
    m['csv'] = inp['cache_sb_v'].reshape(DEPTH * NPOOL, 128, 256)
    m['cfk'] = inp['cache_fox_k'].reshape(DEPTH * NPOOL, 128, 256)
    m['cfv'] = inp['cache_fox_v'].reshape(DEPTH * NPOOL, 128, 256)
    m['clf'] = inp['cache_fox_logf'].reshape(DEPTH, NPOOL, 512)
    m['sst'] = np.ascontiguousarray(inp['state_ssm'][:, b0:b0 + NB])
    sc = inp['state_conv'][:, b0:b0 + NB]
    m['scv'] = np.ascontiguousarray(sc.reshape(DEPTH, NB, 3, 8, 128).transpose(0, 4, 3, 1, 2))
    m['ptab'] = np.ascontiguousarray(inp['page_table'][b0:b0 + NB].reshape(1, NB * NPG).astype(np.int32))
    return m


_CACHE = {}


def run(inp, cfg, ncores, stop_at=None):
    inp = {k: np.asarray(v) for k, v in inp.items()}
    DEPTH = cfg['DEPTH']
    key = tuple(sorted(cfg.items()))
    if key not in _CACHE:
        _CACHE[key] = build(cfg, stop_at)
    nc = _CACHE[key]
    wall = prep_weights(inp, DEPTH)
    par = prep_params(inp, DEPTH)
    cst = prep_consts()
    cmask = prep_cmask()
    in_maps = []
    for c in range(ncores):
        m = host_inputs(inp, cfg, c)
        m['wall'] = wall
        m['par'] = par
        m['cst'] = cst
        m['cmask'] = cmask
        in_maps.append(m)
    res = run_bass_kernel_spmd(nc, in_maps, core_ids=list(range(ncores)))
    R = res.results
    T, NB = cfg['T'], cfg['NB']

    def cat(name, shape_per_core, axis):
        return np.concatenate([R[c][name].reshape(shape_per_core) for c in range(ncores)], axis=axis)
    y_prompt = cat('yp', (1, T, D), 0)
    y_sample = cat('ys', (NB, 4, D), 0)
    outs = [y_prompt, y_sample]
    for nm in ('psbk', 'psbv', 'pfxk', 'pfxv'):
        outs.append(cat(nm, (DEPTH, 1, T, 4, 64), 1))
    outs.append(cat('plf', (DEPTH, 1, T, 4), 1))
    outs.append(cat('pssm', (DEPTH, 1, 8, 64, 128), 1))
    outs.append(cat('pcv', (DEPTH, 1, 3, 1024), 1))
    for nm in ('ssbk', 'ssbv', 'sfxk', 'sfxv'):
        outs.append(cat(nm, (DEPTH, NB, 4, 4, 64), 1))
    outs.append(cat('slf', (DEPTH, NB, 4, 4), 1))
    outs.append(cat('sssm', (DEPTH, NB, 8, 64, 128), 1))
    outs.append(cat('scvo', (DEPTH, NB, 3, 1024), 1))
    return tuple(np.ascontiguousarray(o.astype(np.float32)) for o in outs)


def kernel(**inputs):
    return run(inputs, CFG_FULL, NCORES)
```

```python
import numpy as np
from contextlib import ExitStack
import concourse.bass as bass
import concourse.mybir as mybir
from concourse.bass_utils import run_bass_kernel_spmd

F32 = mybir.dt.float32
BF16 = mybir.dt.bfloat16
I32 = mybir.dt.int32
AF = mybir.ActivationFunctionType
ALU = mybir.AluOpType
AX = mybir.AxisListType

ENG = ['pe', 'act', 'dve', 'pool', 'sp']
D = 1024
NCORES = 8
EPS = 1e-6
CFG_FULL = dict(T=2048, NB=4, NPG=64, NPOOL=2560, DEPTH=2)


class Prog:
    def __init__(self, nc, n_dma_sems=8):
        self.nc = nc
        self.ops = {e: [] for e in ENG}
        self.cnt = {e: 0 for e in ENG}
        self.lastw = {}
        self.readers = {}
        self.dma_cnt = {}
        self.dma_rr = {q: 0 for q in ENG}
        self.nd = n_dma_sems

    def _deps(self, reads, writes, eng=None):
        deps = {}

        def add(k, v):
            if deps.get(k, 0) < v:
                deps[k] = v
        for r in reads:
            if r in self.lastw:
                add(*self.lastw[r])
            if r.startswith('ps'):
                for k, v in self.readers.get(r, {}).items():
                    if k != ('c', eng):
                        add(k, v)
        for w in writes:
            if w in self.lastw:
                add(*self.lastw[w])
            for k, v in self.readers.get(w, {}).items():
                add(k, v)
        return deps

    def _commit(self, tok, reads, writes):
        k, v = tok
        for r in reads:
            d = self.readers.setdefault(r, {})
            if d.get(k, 0) < v:
                d[k] = v
        for w in writes:
            self.lastw[w] = tok
            self.readers[w] = {}

    def op(self, eng, fn, reads=(), writes=()):
        deps = self._deps(reads, writes, eng)
        self.cnt[eng] += 1
        tok = (('c', eng), self.cnt[eng])
        self.ops[eng].append((fn, deps, tok))
        self._commit(tok, reads, writes)

    def dma(self, q, fn, reads=(), writes=()):
        deps = self._deps(reads, writes)
        k = self.dma_rr[q]
        self.dma_rr[q] = (k + 1) % self.nd
        c = self.dma_cnt.get((q, k), 0)
        if c > 0:
            deps[('d', q, k)] = max(deps.get(('d', q, k), 0), c)
        self.dma_cnt[(q, k)] = c + 1
        tok = (('d', q, k), c + 1)
        self.ops[q].append((fn, deps, tok))
        self._commit(tok, reads, writes)

    def barrier(self):
        deps = {('c', e): self.cnt[e] for e in ENG if self.cnt[e] > 0}
        for (q, k), c in self.dma_cnt.items():
            deps[('d', q, k)] = c
        for e in ENG:
            self.ops[e].append((None, dict(deps), None))

    def emit(self):
        nc = self.nc
        sem_c = {e: nc.alloc_semaphore(name=f"c_{e}") for e in ENG}
        sem_d = {}
        for (q, k) in sorted(self.dma_cnt):
            sem_d[(q, k)] = nc.alloc_semaphore(name=f"d_{q}{k}")

        def semof(key):
            if key[0] == 'c':
                return sem_c[key[1]], 1
            return sem_d[(key[1], key[2])], 16

        final_waits = list(self.dma_cnt.items())

        def run(ename, eng):
            seen = {}
            for fn, deps, tok in self.ops[ename]:
                for key, val in deps.items():
                    if key == ('c', 'pe') and ename == 'pe':
                        continue
                    if seen.get(key, 0) >= val:
                        continue
                    s, mul = semof(key)
                    eng.wait_ge(s, val * mul)
                    seen[key] = val
                if fn is None:
                    continue
                ins = fn(eng)
                s, mul = semof(tok[0])
                ins.then_inc(s, mul)
            if ename == 'sp':
                for (q, k), c in final_waits:
                    eng.wait_ge(sem_d[(q, k)], 16 * c)

        with nc.Block() as block:
            @block.sync
            def _(e):
                run('sp', e)

            @block.scalar
            def _(e):
                run('act', e)

            @block.vector
            def _(e):
                run('dve', e)

            @block.gpsimd
            def _(e):
                run('pool', e)

            @block.tensor
            def _(e):
                run('pe', e)


SB_W, FX_W = 256, 256
OFF_QSB, OFF_KSB, OFF_VSB = 0, 256, 512
OFF_QFX, OFF_KFX, OFF_VFX = 768, 1024, 1280
OFF_F = 1536
OFF_Z = 1540
OFF_XBC = 2052
OFF_DT = 3076
OFF_G = 3084


def wblocks():
    bl = []
    for cb in range(40):
        bl.append((f'fm{cb}', 1024))
    bl.append(('f4', 32))
    bl += [('tmA', 4096), ('tmB', 4096), ('tmC', 4096), ('tmD', 96)]
    for cb in range(8):
        bl.append((f'sbo{cb}', 256))
    for cb in range(8):
        bl.append((f'sso{cb}', 512))
    for cb in range(8):
        bl.append((f'fxo{cb}', 256))
    for cb in range(8):
        bl.append((f'wo{cb}', 1024))
    for cb in range(32):
        bl.append((f'up{cb}', 1024))
    for cb in range(8):
        bl.append((f'dn{cb}', 4096))
    return bl


def wlayout():
    off = {}
    r = 0
    for name, E in wblocks():
        off[name] = (r, E)
        r += 128 * E // 1024
    nrows = ((r + 1023) // 1024) * 1024
    return off, nrows


def _blk(W, cols):
    K = W.shape[0]
    sub = W[:, cols]
    return np.ascontiguousarray(sub.reshape(K // 128, 128, len(cols)).transpose(1, 0, 2))


def prep_weights(inp, depth):
    off, nrows = wlayout()
    wall = np.zeros((depth, nrows * 1024), np.float32)
    fm_cols = (list(range(OFF_QSB, OFF_QSB + 256)) + list(range(OFF_KSB, OFF_KSB + 256)) +
               list(range(OFF_QFX, OFF_QFX + 256)) + list(range(OFF_KFX, OFF_KFX + 256)) +
               list(range(OFF_XBC, OFF_XBC + 1024)) + list(range(OFF_G, OFF_G + 3072)))
    for l in range(depth):
        win = inp['w_in'][l]

        def put(name, arr):
            r, E = off[name]
            a = arr.reshape(128, -1)
            assert a.shape[1] == E, (name, a.shape, E)
            wall[l, r * 1024: r * 1024 + 128 * E] = a.reshape(-1)
        for cb in range(40):
            put(f'fm{cb}', _blk(win, fm_cols[cb * 128:(cb + 1) * 128]))
        put('f4', _blk(win, list(range(OFF_F, OFF_F + 4))))
        put('tmA', _blk(win, list(range(OFF_KSB, OFF_KSB + 512))))
        put('tmB', _blk(win, list(range(OFF_KFX, OFF_KFX + 512))))
        put('tmC', _blk(win, list(range(OFF_Z, OFF_Z + 512))))
        put('tmD', _blk(win, list(range(OFF_F, OFF_F + 4)) + list(range(OFF_DT, OFF_DT + 8))))
        for cb in range(8):
            cs = list(range(cb * 128, (cb + 1) * 128))
            put(f'sbo{cb}', _blk(inp['w_sb_out'][l], cs))
            put(f'sso{cb}', _blk(inp['w_ssm_out'][l], cs))
            put(f'fxo{cb}', _blk(inp['w_fox_out'][l], cs))
            put(f'wo{cb}', _blk(inp['w_o'][l], cs))
            put(f'dn{cb}', _blk(inp['w_down'][l], cs))
        for cb in range(32):
            put(f'up{cb}', _blk(inp['w_up'][l], list(range(cb * 128, (cb + 1) * 128))))
    return wall.reshape(depth, nrows, 1024)


PAR = {}
_o = 0
for _n, _w in [('g1', 8), ('g2', 8), ('gf', 8), ('bf_rep', 4), ('bf_col', 1), ('cw', 32), ('cb', 8),
               ('dtb', 8), ('alog', 8), ('dsk', 8), ('gn', 512)]:
    PAR[_n] = (_o, _w)
    _o += _w
NPAR = _o


def prep_params(inp, depth):
    par = np.zeros((depth, 128, NPAR), np.float32)

    def col(v):
        return v.reshape(8, 128).T
    for l in range(depth):
        def put(n, a):
            o, w = PAR[n]
            par[l, :, o:o + w] = a
        put('g1', col(inp['norm1_g'][l]))
        put('g2', col(inp['norm2_g'][l]))
        put('gf', col(inp['final_norm_g']))
        put('bf_rep', np.broadcast_to(inp['b_forget'][l][None, :], (128, 4)))
        bc = np.zeros((128, 1), np.float32)
        bc[0:4, 0] = inp['b_forget'][l]
        put('bf_col', bc)
        put('cw', inp['conv_w'][l].reshape(4, 8, 128).transpose(2, 1, 0).reshape(128, 32))
        put('cb', col(inp['conv_b'][l]))
        put('dtb', np.broadcast_to(inp['dt_bias'][l][None, :], (128, 8)))
        put('alog', np.broadcast_to(inp['a_log'][l][None, :], (128, 8)))
        put('dsk', np.broadcast_to(inp['d_skip'][l][None, :], (128, 8)))
        put('gn', np.broadcast_to(inp['ssm_norm_g'][l][None, :], (128, 512)))
    return par


CST = {}
_o = 0
for _n, _w in [('ident', 128), ('linc', 128), ('ustr', 128), ('ones', 128), ('uincr', 128),
               ('lincS', 16), ('ustrS', 16), ('onesS', 16), ('selB', 512), ('rowB', 4), ('colB', 64),
               ('mnsb', 8), ('mnfx', 8), ('iota', 1)]:
    CST[_n] = (_o, _w)
    _o += _w
NCST = _o


def prep_cmask():
    j = np.arange(128)[:, None]
    q = np.arange(512)[None, :]
    m = np.zeros((128, 4, 513), np.float32)
    for i in range(4):
        m[:, i, 1:] = ((i * 128 + j) <= q)
    return m.reshape(128, 4 * 513)


def prep_consts():
    c = np.zeros((128, NCST), np.float32)
    j = np.arange(128)[:, None]
    l = np.arange(128)[None, :]

    def put(n, a):
        o, w = CST[n]
        c[:a.shape[0], o:o + w] = a
    put('ident', (j == l).astype(np.float32))
    put('linc', (j <= l).astype(np.float32))
    put('ustr', (j > l).astype(np.float32))
    put('ones', np.ones((128, 128), np.float32))
    put('uincr', (j >= l).astype(np.float32))
    j16 = np.arange(16)[:, None]
    l16 = np.arange(16)[None, :]
    same = (j16 // 4 == l16 // 4)
    put('lincS', (same & (j16 <= l16)).astype(np.float32))
    put('ustrS', (same & (j16 > l16)).astype(np.float32))
    put('onesS', same.astype(np.float32))
    selB = np.zeros((16, 4, 128), np.float32)
    for b in range(4):
        selB[4 * b:4 * b + 4, b, :] = 1.0
    put('selB', selB.reshape(16, 512))
    rowB = np.zeros((16, 4), np.float32)
    for b in range(4):
        rowB[4 * b:4 * b + 4, b] = 1.0
    put('rowB', rowB)
    colB = np.zeros((128, 4, 16), np.float32)
    for b in range(4):
        colB[:, b, 4 * b:4 * b + 4] = 1.0
    put('colB', colB.reshape(128, 64))
    qq = np.tile(np.arange(4), 2)[None, :]
    t4 = np.arange(128)[:, None]
    put('mnsb', ((t4 < qq) & (t4 < 4)).astype(np.float32))
    put('mnfx', ((t4 <= qq) & (t4 < 4)).astype(np.float32))
    put('iota', np.arange(128, dtype=np.float32)[:, None])
    return c


class Bag:
    pass


class _Stop(Exception):
    pass


def build(cfg, stop_at=None):
    T, NB, NPG, NPOOL, DEPTH = cfg['T'], cfg['NB'], cfg['NPG'], cfg['NPOOL'], cfg['DEPTH']
    NS = NB * 4
    NT = T // 128
    NG = T // 512
    woff, wrows = wlayout()

    nc = bass.Bass("TRN2", target_bir_lowering=False)
    P = Prog(nc)

    def din(name, shape, dt=F32):
        return nc.dram_tensor(name, shape, dt, kind="ExternalInput").ap()

    def dout(name, shape, dt=F32):
        return nc.dram_tensor(name, shape, dt, kind="ExternalOutput").ap()

    xp_d = din("xp", [T, D])
    xs_d = din("xs", [NS, D])
    ckk_d = din("ckk", [DEPTH * NPOOL, 128, 512])
    cvv_d = din("cvv", [DEPTH * NPOOL, 128, 512])
    clf_d = din("clf", [DEPTH, NPOOL, 512])
    sst_d = din("sst", [DEPTH, NB, 8, 64, 128])
    scv_d = din("scv", [DEPTH, 128, 8, NB, 3])
    ptab_d = din("ptab", [1, NB * NPG], I32)
    wall_d = din("wall", [DEPTH, wrows, 1024])
    par_d = din("par", [DEPTH, 128, NPAR])
    cst_d = din("cst", [128, NCST])
    cmask_d = din("cmask", [128, 4 * 513])
    wscr = nc.dram_tensor("wscr", [DEPTH, wrows, 1024], BF16, kind="Internal").ap()
    xscr = nc.dram_tensor("xscr", [NG + 1, 128, 8 * 512], F32, kind="Internal").ap()

    yp_o = dout("yp", [T, D])
    ys_o = dout("ys", [NS, D])
    psbk_o = dout("psbk", [DEPTH, T, 256])
    psbv_o = dout("psbv", [DEPTH, T, 256])
    pfxk_o = dout("pfxk", [DEPTH, T, 256])
    pfxv_o = dout("pfxv", [DEPTH, T, 256])
    plf_o = dout("plf", [DEPTH, T, 4])
    pssm_o = dout("pssm", [DEPTH, 8, 64, 128])
    pcv_o = dout("pcv", [DEPTH, 3, 1024])
    ssbk_o = dout("ssbk", [DEPTH, NS, 256])
    ssbv_o = dout("ssbv", [DEPTH, NS, 256])
    sfxk_o = dout("sfxk", [DEPTH, NS, 256])
    sfxv_o = dout("sfxv", [DEPTH, NS, 256])
    slf_o = dout("slf", [DEPTH, NS, 4])
    sssm_o = dout("sssm", [DEPTH, NB, 8, 64, 128])
    scv_o = dout("scvo", [DEPTH, NB, 3, 1024])

    uid = {'n': 0}

    open_scopes = []

    def ck(name):
        if stop_at is not None and name == stop_at:
            raise _Stop()

    class Scope:
        def __init__(self):
            self.es = ExitStack()
            open_scopes.append(self)

        def sb(self, name, shape, dt=F32):
            uid['n'] += 1
            return self.es.enter_context(nc.sbuf_tensor(f"{name}_{uid['n']}", shape, dt))

        def close(self):
            P.barrier()
            self.es.close()
            open_scopes.remove(self)

    G = Scope()
    sb = G.sb

    def mm(out, lhsT, rhs, start, stop, r, w):
        P.op('pe', lambda e: e.matmul(out, lhsT, rhs, start=start, stop=stop), r, w)

    def tr(out, in_, ident, r, w):
        P.op('pe', lambda e: e.transpose(out, in_, ident), r, w)

    def act(out, in_, func, r, w, bias=None, scale=None, accum=None):
        kw = {}
        if bias is not None:
            kw['bias'] = bias
        if scale is not None:
            kw['scale'] = scale
        if accum is not None:
            kw['accum_out'] = accum
        P.op('act', lambda e: e.activation(out, in_, func, **kw), r, w)

    def tt(out, a, b, op, r, w, eng='dve'):
        P.op(eng, lambda e: e.tensor_tensor(out, a, b, op), r, w)

    def ts(out, a, s1, s2, op0, op1, r, w, eng='dve'):
        if op1 is None:
            P.op(eng, lambda e: e.tensor_scalar(out, a, s1, None, op0), r, w)
        else:
            P.op(eng, lambda e: e.tensor_scalar(out, a, s1, s2, op0, op1), r, w)

    def stt(out, a, s, b, op0, op1, r, w):
        P.op('dve', lambda e: e.scalar_tensor_tensor(out, a, s, b, op0, op1), r, w)

    def cp(out, in_, r, w, eng='dve'):
        if eng == 'act':
            P.op('act', lambda e: e.copy(out, in_), r, w)
        else:
            P.op(eng, lambda e: e.tensor_copy(out, in_), r, w)

    def memset(ap, v, w, eng='dve'):
        P.op(eng, lambda e: e.memset(ap, v), (), w)

    def dma(out, in_, r, w, q='sp'):
        P.dma(q, lambda e: e.dma_start(out=out, in_=in_), r, w)

    def scan(out, d0, d1, init, r, w):
        P.op('dve', lambda e: e.tensor_tensor_scan(out, d0, d1, init, ALU.mult, ALU.add), r, w)

    def recip(out, in_, r, w):
        P.op('dve', lambda e: e.reciprocal(out, in_), r, w)

    pes = ExitStack()
    banks = [pes.enter_context(nc.psum_tensor(f"ps{i}", [128, 512], F32)) for i in range(8)]
    bstate = {'i': 0}

    reserved = set()

    def bank(hold=False):
        for _ in range(8):
            i = bstate['i']
            bstate['i'] = (i + 1) % 8
            if i not in reserved:
                break
        else:
            raise RuntimeError("all PSUM banks reserved")
        if hold:
            reserved.add(i)
        return banks[i], f'ps{i}'

    def release(*names):
        for n_ in names:
            reserved.discard(int(n_[2:]))

    Fp = [sb(f"F{i}", [128, 512]) for i in range(8)]
    Hp = [sb(f"H{i}", [128, 512], BF16) for i in range(8)]

    def F(i):
        return Fp[i], f'F{i}'

    def H(i):
        return Hp[i], f'H{i}'

    stg = {'i': 0}

    def next_stage():
        i = 6 + stg['i']
        stg['i'] = 1 - stg['i']
        return Fp[i], f'F{i}'

    cst = sb("cst", [128, NCST])
    dma(cst[:], cst_d[:, :], (), ['cst'])

    def C(n, rows=128, c0=0, c1=None):
        o, w = CST[n]
        if c1 is None:
            c1 = w
        return cst[0:rows, o + c0:o + c1]

    onesb = sb("onesb", [128, 128], BF16)
    nuincb = sb("nuincb", [128, 128], BF16)
    nonesb = sb("nonesb", [128, 128], BF16)
    mext = sb("mext", [128, 4, 513], BF16)
    cp(onesb[:], C('ones'), ['cst'], ['cb16'])
    ts(nuincb[:], C('uincr'), -1.0, None, ALU.mult, None, ['cst'], ['cb16'])
    ts(nonesb[:], C('ones'), -1.0, None, ALU.mult, None, ['cst'], ['cb16'])
    for i in range(4):
        f_, fn = F(i)
        dma(f_[:, 0:512], cmask_d[:, i * 513:i * 513 + 512], (), [fn])
        cp(mext[:, i, 0:512], f_[:, 0:512], [fn], ['cb16'])
        memset(mext[:, i, 512:513], 1.0, ['cb16'])
    CB = ['cst', 'cb16']
    ident = C('ident')

    def msb(di):
        return mext[:, di, 0:512]

    def mfx(di):
        return mext[:, di, 1:513]

    epst = sb("epst", [128, 1])
    one_t = sb("one_t", [128, 1])
    ones_f = sb("ones_f", [128, 128])
    memset(epst[:], EPS, ['cb16'])
    memset(one_t[:], 1.0, ['cb16'])
    memset(ones_f[:], 1.0, ['cb16'])
    onesT = sb("onesT", [4, 512])
    sel4 = sb("sel4", [4, 4, 128])
    memset(onesT[:], 1.0, ['cb16'])
    for h in range(4):
        cp(sel4[:, h, :], C('ident', rows=4, c0=h, c1=h + 1).to_broadcast([4, 128]), ['cst'], ['cb16'])

    par = sb("par", [128, NPAR])
    arep = sb("arep", [128, 8])
    NBF = sb("NBF", [128, 1])

    def PR(n, rows=128, c0=0, c1=None):
        o, w = PAR[n]
        if c1 is None:
            c1 = w
        return par[0:rows, o + c0:o + c1]

    def cast_layer(l):
        for ch in range(wrows // 1024):
            P.dma('pool', lambda e, l=l, ch=ch: e.dma_start(out=wscr[l, ch * 1024:(ch + 1) * 1024, :],
                                                         in_=wall_d[l, ch * 1024:(ch + 1) * 1024, :]),
                  (), [f'scr{l}.{ch}'])

    wbig = [sb(f"wbig{i}", [128, 4096], BF16) for i in range(2)]
    wsm = [sb(f"wsm{i}", [128, 1024], BF16) for i in range(4)]
    wstate = {'b': 0, 's': 0}

    def load_w(l, name):
        r0, E = woff[name]
        if E > 1024:
            i = wstate['b']
            wstate['b'] = (i + 1) % 2
            tile_, res = wbig[i], f'wb{i}'
        else:
            i = wstate['s']
            wstate['s'] = (i + 1) % 4
            tile_, res = wsm[i], f'ws{i}'
        nr = 128 * E // 1024
        chs = sorted(set([r0 // 1024, (r0 + nr - 1) // 1024]))
        src = wscr[l, r0:r0 + nr, :].rearrange("r c -> (r c)").rearrange("(p e) -> p e", p=128)
        dma(tile_[:, 0:E], src, [f'scr{l}.{c}' for c in chs], [res])
        return tile_, res

    xg = sb("xg", [128, 8, 512])
    hT = sb("hT", [128, 8, 512], BF16)
    rstd = sb("rstd", [128, 512])
    carryF = sb("carryF", [128, 4])
    carryFT = sb("carryFT", [4, 1])
    ccar = sb("ccar", [128, 8, 3])
    hst = sb("hst", [128, 8, 64])
    hstb = sb("hstb", [128, 8, 64], BF16)
    lfT = sb("lfT", [4, 512])
    cumT = sb("cumT", [4, 512])
    tmp8 = sb("tmp8", [128, 16])
    lf_tok = sb("lf_tok", [128, 4])
    idx_t = sb("idx_t", [128, NB * NPG], I32)
    ptr_t = sb("ptr_t", [128, NB * NPG], I32)
    iota_i = sb("iota_i", [128, 1], I32)
    pcol = sb("pcol", [64, NB], I32)

    dma(ptr_t[:], ptab_d[0:1, :].partition_broadcast(128), (), ['ptr_t'])
    cp(iota_i[:], C('iota'), ['cst'], ['iota_i'])
    ts(idx_t[:], ptr_t[:], 128, iota_i[:, 0:1], ALU.mult, ALU.add, ['ptr_t', 'iota_i'], ['idx_t'], eng='pool')
    for b in range(NB):
        dma(pcol[0:NPG, b:b + 1], ptab_d[0:1, b * NPG:(b + 1) * NPG].rearrange("a j -> j a"), (), ['pcol'])

    def load_x(src_rows, ntok, c0):
        for half in range(2):
            st_, sn = next_stage()
            dma(st_[0:ntok, :], src_rows[:, half * 512:(half + 1) * 512], (), [sn])
            pb, pn = bank()
            for k in range(4):
                tr(pb[:, k * 128:k * 128 + ntok], st_[0:ntok, k * 128:(k + 1) * 128], C('ident', rows=ntok, c1=ntok),
                   [sn, 'cst'], [pn])
            outv = xg[:, half * 4:half * 4 + 4, c0:c0 + ntok]
            inv = pb[:, :].rearrange("p (k t) -> p k t", k=4)[:, :, 0:ntok]
            cp(outv, inv, [pn], ['xg'], eng=('dve' if half == 0 else 'act'))

    def rms_to_hT(n, gname, out_f32=None, ores='hT'):
        pb, pn = bank()
        for kc in range(8):
            s, sn = H(kc % 2)
            act(s[:, 0:n], xg[:, kc, 0:n], AF.Square, ['xg'], [sn])
            mm(pb[:, 0:n], onesb[:], s[:, 0:n], kc == 0, kc == 7, [sn] + CB, [pn])
        act(rstd[:, 0:n], pb[:, 0:n], AF.Ln, [pn, 'cb16'], ['rstd'], bias=epst[:, 0:1], scale=1.0 / D)
        act(rstd[:, 0:n], rstd[:, 0:n], AF.Exp, ['rstd'], ['rstd'], scale=-0.5)
        for kc in range(8):
            o = hT[:, kc, 0:n] if out_f32 is None else out_f32[:, kc, 0:n]
            stt(o, xg[:, kc, 0:n], PR(gname, c0=kc, c1=kc + 1), rstd[:, 0:n], ALU.mult, ALU.mult,
                ['xg', 'rstd', 'par'], [ores])

    def ssd_tile(l, A, L, ti, ntok, sample):
        cs = slice(ti * 128, ti * 128 + ntok)
        linc = C('lincS', rows=16) if sample else C('linc')
        ustr = C('ustrS', rows=16) if sample else C('ustr')
        ones_ = C('onesS', rows=16) if sample else C('ones')
        idn = C('ident', rows=ntok, c1=ntok)
        xs_tok, xsn = F(0)
        yo_sb, yon = F(1)
        yv, yvn = F(2)
        y2, y2n = F(3)
        dec, decn = F(4)
        xdt, xdtn = H(2)
        xw, xwn = H(3)
        pa, pan = bank()
        for c in range(4):
            tr(pa[0:ntok, c * 128:(c + 1) * 128], A.xsT[:, c, cs], ident, ['xsT', 'cst'], [pan])
        cp(xs_tok[0:ntok, :], pa[0:ntok, :], [pan], [xsn])
        pb_, pbn = bank()
        for g2 in range(2):
            tr(pb_[0:ntok, g2 * 128:(g2 + 1) * 128], A.BTf[:, g2, cs], ident, ['BT', 'cst'], [pbn])
        cp(A.B_tok[0:ntok, :], pb_[0:ntok, 0:256], [pbn], ['B_tok'], eng='act')
        tt(A.dta[0:ntok, :], A.dts[0:ntok, ti, :], arep[0:ntok, :], ALU.mult, ['dts', 'par2'], ['dta'])
        pc, pcn = bank()
        mm(pc[0:ntok, 0:8], linc, A.dta[0:ntok, :], True, True, ['dta', 'cst'], [pcn])
        mm(pc[0:ntok, 8:16], ones_, A.dta[0:ntok, :], True, True, ['dta', 'cst'], [pcn])
        cp(A.acs[0:ntok, :], pc[0:ntok, 0:8], [pcn], ['acs'], eng='act')
        act(A.ea[0:ntok, :], pc[0:ntok, 0:8], AF.Exp, [pcn], ['ea'])
        tt(A.te[0:ntok, :], pc[0:ntok, 8:16], A.acs[0:ntok, :], ALU.subtract, [pcn, 'acs'], ['te'])
        act(A.te[0:ntok, :], A.te[0:ntok, :], AF.Exp, ['te'], ['te'])
        tt(A.wdt[0:ntok, :], A.te[0:ntok, :], A.dts[0:ntok, ti, :], ALU.mult, ['te', 'dts'], ['wdt'])
        if not sample:
            act(A.cd[:, :], pc[:, 8:16], AF.Exp, [pcn], ['cd'])
        xs3 = xs_tok[0:ntok, :].rearrange("p (h d) -> p h d", h=8)
        tt(xdt[0:ntok, :].rearrange("p (h d) -> p h d", h=8), xs3,
           A.dts[0:ntok, ti, :].unsqueeze(2).to_broadcast([ntok, 8, 64]), ALU.mult, [xsn, 'dts'], [xdtn])
        tt(xw[0:ntok, :].rearrange("p (h d) -> p h d", h=8), xs3,
           A.wdt[0:ntok, :].unsqueeze(2).to_broadcast([ntok, 8, 64]), ALU.mult, [xsn, 'wdt'], [xwn])
        psc, pscn = bank()
        for g2 in range(2):
            mm(psc[0:ntok, g2 * 128:g2 * 128 + ntok], A.BTb[:, g2, cs], A.CTb[:, g2, cs], True, True, ['BT', 'CT'], [pscn])
        for g2 in range(2):
            tt(A.scm[0:ntok, g2, 0:ntok], psc[0:ntok, g2 * 128:g2 * 128 + ntok], linc, ALU.mult, [pscn, 'cst'], ['scm'])
        pY, pYn = bank(True)
        pYo, pYon = bank(True)
        if not sample:
            pSt, pStn = bank(True)
        for g2 in range(2):
            pseg, psegn = bank()
            for hh in range(4):
                h = g2 * 4 + hh
                ld = A.Ld[hh % 2]
                ts(ld[0:ntok, 0:ntok], ustr, A.dta[0:ntok, h:h + 1], None, ALU.mult, None, ['cst', 'dta'], [f'Ld{hh % 2}'])
                mm(pseg[0:ntok, hh * 128:hh * 128 + ntok], ld[0:ntok, 0:ntok], linc, True, True,
                   [f'Ld{hh % 2}', 'cst'], [psegn])
            segv = pseg[0:ntok, :].rearrange("p (h s) -> p h s", h=4)[:, :, 0:ntok]
            decv = dec[0:ntok, :].rearrange("p (h s) -> p h s", h=4)[:, :, 0:ntok]
            act(decv, segv, AF.Exp, [psegn], [decn])
            tt(A.MT[0:ntok, :, 0:ntok], decv, A.scm[0:ntok, g2, 0:ntok].unsqueeze(1).to_broadcast([ntok, 4, ntok]),
               ALU.mult, [decn, 'scm'], ['MT'])
            for hh in range(4):
                h = g2 * 4 + hh
                hsl = slice(h * 64, (h + 1) * 64)
                mm(pY[0:ntok, hsl], A.MT[0:ntok, hh, 0:ntok], xdt[0:ntok, hsl], True, True, ['MT', xdtn], [pYn])
                if not sample:
                    mm(pYo[0:ntok, hsl], A.CTb[:, g2, cs], hstb[:, h, :], True, True, ['CT', 'hstb'], [pYon])
                    mm(pSt[:, hsl], A.B_tok[0:ntok, g2 * 128:(g2 + 1) * 128], xw[0:ntok, hsl], True, True,
                       ['B_tok', xwn], [pStn])
        if sample:
            sample_state(l, A, L, pYo, pYon, xw, xwn)
        tt(yo_sb[0:ntok, :].rearrange("p (h d) -> p h d", h=8), pYo[0:ntok, :].rearrange("p (h d) -> p h d", h=8),
           A.ea[0:ntok, :].unsqueeze(2).to_broadcast([ntok, 8, 64]), ALU.mult, [pYon, 'ea'], [yon])
        tt(yv[0:ntok, :], pY[0:ntok, :], yo_sb[0:ntok, :], ALU.add, [pYn, yon], [yvn])
        release(pYn, pYon)
        tt(y2[0:ntok, :].rearrange("p (h d) -> p h d", h=8), xs3,
           PR('dsk', rows=ntok).unsqueeze(2).to_broadcast([ntok, 8, 64]), ALU.mult, [xsn, 'par'], [y2n])
        tt(yv[0:ntok, :], yv[0:ntok, :], y2[0:ntok, :], ALU.add, [yvn, y2n], [yvn])
        tt(yv[0:ntok, :], yv[0:ntok, :], A.zs[0:ntok, ti, :], ALU.mult, [yvn, 'zs'], [yvn])
        for g2 in range(2):
            act(y2[0:ntok, g2 * 256:(g2 + 1) * 256], yv[0:ntok, g2 * 256:(g2 + 1) * 256], AF.Square, [yvn], [y2n, 'ssq'],
                accum=A.ssq[0:ntok, g2:g2 + 1])
        act(A.rs2[0:ntok, :], A.ssq[0:ntok, :], AF.Ln, ['ssq', 'cb16'], ['rs2'], bias=epst[0:ntok, 0:1], scale=1.0 / 256)
        act(A.rs2[0:ntok, :], A.rs2[0:ntok, :], AF.Exp, ['rs2'], ['rs2'], scale=-0.5)
        for g2 in range(2):
            gsl = slice(g2 * 256, (g2 + 1) * 256)
            stt(y2[0:ntok, gsl], yv[0:ntok, gsl], A.rs2[0:ntok, g2:g2 + 1], PR('gn', rows=ntok, c0=g2 * 256, c1=(g2 + 1) * 256),
                ALU.mult, ALU.mult, [yvn, 'rs2', 'par', y2n], [y2n])
        pT, pTn = bank()
        for c in range(4):
            tr(pT[:, c * 128:c * 128 + ntok], y2[0:ntok, c * 128:(c + 1) * 128], idn, [y2n, 'cst'], [pTn])
        cp(A.yssT[:, :, cs], pT[:, :].rearrange("p (c t) -> p c t", c=4)[:, :, 0:ntok], [pTn], ['yssT'], eng='act')
        if not sample:
            for h in range(8):
                stt(hst[:, h, :], hst[:, h, :], A.cd[:, h:h + 1], pSt[:, h * 64:(h + 1) * 64], ALU.mult, ALU.add,
                    ['hst', 'cd', pStn], ['hst'])
            cp(hstb[:].rearrange("p a b -> p (a b)"), hst[:].rearrange("p a b -> p (a b)"), ['hst'], ['hstb'])
            release(pStn)

    def sample_state(l, A, L, pYo, pYon, xw, xwn):
        for b in range(NB):
            tt(L.CTm[:, b, :, :], A.CTb[:, :, 0:16], C('colB', c0=b * 16, c1=(b + 1) * 16).unsqueeze(1).to_broadcast([128, 2, 16]),
               ALU.mult, ['CT', 'cst'], ['CTm'])
        pc, pcn = bank()
        for b in range(NB):
            mm(pc[:, b * 8:(b + 1) * 8], C('selB', rows=16, c0=b * 128, c1=(b + 1) * 128), A.dta[0:16, :], True, True,
               ['dta', 'cst'], [pcn])
        act(L.cdS[:].rearrange("p a b -> p (a b)"), pc[:, 0:NB * 8], AF.Exp, [pcn], ['cdS'])
        for b in range(NB):
            h0 = L.h0[0]
            h0n = 'h0_0'
            dma(h0[:, :, :], sst_d[l, b].rearrange("h p n -> p h n"), (), [h0n])
            for half in range(2):
                pb, pn = bank()
                for k in range(4):
                    h = half * 4 + k
                    tr(pb[:, k * 64:(k + 1) * 64], h0[:, h, :], C('ident', rows=64, c1=64), [h0n, 'cst'], [pn])
                cp(L.h0Tb[:, b, half * 4:half * 4 + 4, :].rearrange("p a b -> p (a b)"), pb[:, 0:256], [pn], ['h0Tb'])
            ts(L.xwm[:, :], xw[0:16, :], C('rowB', rows=16, c0=b, c1=b + 1), None, ALU.mult, None, [xwn, 'cst'], ['xwm'])
            for half in range(2):
                st_, sn = next_stage()
                pb, pn = bank()
                for k in range(4):
                    h = half * 4 + k
                    g2 = h // 4
                    mm(pb[0:64, k * 128:(k + 1) * 128], L.xwm[:, h * 64:(h + 1) * 64], A.B_tok[0:16, g2 * 128:(g2 + 1) * 128],
                       True, True, ['xwm', 'B_tok'], [pn])
                for k in range(4):
                    h = half * 4 + k
                    stt(st_[0:64, k * 128:(k + 1) * 128], h0[:, h, :], L.cdS[0:64, b, h:h + 1], pb[0:64, k * 128:(k + 1) * 128],
                        ALU.mult, ALU.add, [h0n, 'cdS', pn], [sn])
                dma(sssm_o[l, b, half * 4:half * 4 + 4].rearrange("h p n -> p h n"),
                    st_[0:64, :].rearrange("p (h n) -> p h n", h=4), [sn], [])
        for h in range(8):
            g2 = h // 4
            for b in range(NB):
                mm(pYo[0:16, h * 64:(h + 1) * 64], L.CTm[:, b, g2, :], L.h0Tb[:, b, h, :], b == 0, b == NB - 1,
                   ['CTm', 'h0Tb'], [pYon])

    def proj_group(l, A, L, gi, n, tiles, sample):
        rms_to_hT(n, 'g1')
        ck('p_rms')
        c0 = gi * 512
        for which, base in (('sb', 0), ('fx', 4)):
            for j in range(2):
                for kind, cbi in (('q', base + j), ('k', base + 2 + j)):
                    wt, wn = load_w(l, f'fm{cbi}')
                    pb, pn = bank()
                    for kc in range(8):
                        mm(pb[:, 0:n], wt[:, kc * 128:(kc + 1) * 128], hT[:, kc, 0:n], kc == 0, kc == 7, [wn, 'hT'], [pn])
                    if kind == 'q':
                        act(A.QT[which][:, j, 0:n], pb[:, 0:n], AF.Copy, [pn], [f'QT{which}'], scale=0.125)
                    elif sample:
                        cp(L.KTs[which][:, j, 0:n], pb[:, 0:n], [pn], [f'KTs{which}'])
                    else:
                        cp(L.KT[which][:, j, c0:c0 + n], pb[:, 0:n], [pn], [f'KT{which}{gi}'])
        ck('p_qk')
        if not sample:
            wt, wn = load_w(l, 'f4')
            pb, pn = bank()
            for kc in range(8):
                mm(pb[0:4, 0:n], wt[:, kc * 4:(kc + 1) * 4], hT[:, kc, 0:n], kc == 0, kc == 7, [wn, 'hT'], [pn])
            ck('f4a')
            act(lfT[:, 0:n], pb[0:4, 0:n], AF.Exp, [pn, 'par2'], ['lfT'], bias=NBF[0:4, 0:1], scale=-1.0)
            act(lfT[:, 0:n], lfT[:, 0:n], AF.Ln, ['lfT', 'cb16'], ['lfT'], bias=one_t[0:4, 0:1])
            ts(lfT[:, 0:n], lfT[:, 0:n], -1.0, None, ALU.mult, None, ['lfT'], ['lfT'])
            ck('f4b')
            scan(cumT[:, 0:n], onesT[:, 0:n], lfT[:, 0:n], carryFT[:, 0:1], ['lfT', 'cb16', 'carryFT'], ['cumT'])
            ck('f4c')
            cp(carryFT[:, 0:1], cumT[:, n - 1:n], ['cumT'], ['carryFT'])
        ck('p_f4')
        need_tm = sample or (gi == NG - 1)
        if need_tm:
            ntk = 16 if sample else 128
            tcs = slice(0, 16) if sample else slice(384, 512)
            ptm = [bank(True), bank(True)]
        for c in range(8):
            wt, wn = load_w(l, f'fm{8 + c}')
            pb, pn = bank()
            for kc in range(8):
                mm(pb[:, 0:n], wt[:, kc * 128:(kc + 1) * 128], hT[:, kc, 0:n], kc == 0, kc == 7, [wn, 'hT'], [pn])
            if need_tm:
                tb, tbn = ptm[c // 4]
                for kc in range(8):
                    mm(tb[0:ntk, (c % 4) * 128:(c % 4 + 1) * 128], hT[:, kc, tcs], wt[:, kc * 128:(kc + 1) * 128],
                       kc == 0, kc == 7, [wn, 'hT'], [tbn])
            acc, accn = F(5)
            if sample:
                cp(L.XS[:, c, :, 3:7], pb[:, 0:16].rearrange("p (b t) -> p b t", b=NB), [pn], ['XS'])
                xv = lambda i: L.XS[:, c, :, i:i + 4]
                accv = acc[:, 0:16].rearrange("p (b t) -> p b t", b=NB)
                xres = 'XS'
            else:
                Xb = A.Xb[c % 2]
                xres = f'Xb{c % 2}'
                cp(Xb[:, 0:3], ccar[:, c, :], ['ccar'], [xres])
                cp(Xb[:, 3:3 + n], pb[:, 0:n], [pn], [xres], eng='act')
                cp(ccar[:, c, :], Xb[:, n:n + 3], [xres], ['ccar'])
                xv = lambda i: Xb[:, i:i + n]
                accv = acc[:, 0:n]
            ts(accv, xv(0), PR('cw', c0=c * 4, c1=c * 4 + 1), PR('cb', c0=c, c1=c + 1), ALU.mult, ALU.add,
               [xres, 'par'], [accn])
            for i in range(1, 4):
                stt(accv, xv(i), PR('cw', c0=c * 4 + i, c1=c * 4 + i + 1), accv, ALU.mult, ALU.add,
                    [xres, 'par', accn], [accn])
            if c < 4:
                act(A.xsT[:, c, 0:n], acc[:, 0:n], AF.Silu, [accn], ['xsT'])
            elif c < 6:
                act(A.BTf[:, c - 4, 0:n], acc[:, 0:n], AF.Silu, [accn], ['BT'])
                cp(A.BTb[:, c - 4, 0:n], A.BTf[:, c - 4, 0:n], ['BT'], ['BT'])
            else:
                act(A.CTb[:, c - 6, 0:n], acc[:, 0:n], AF.Silu, [accn], ['CT'])
        if need_tm:
            for half in range(2):
                st_, sn = next_stage()
                cp(st_[0:ntk, :], ptm[half][0][0:ntk, :], [ptm[half][1]], [sn], eng=('act' if half else 'dve'))
                if sample:
                    for b in range(NB):
                        dma(scv_o[l, b, :, half * 512:(half + 1) * 512], st_[4 * b + 1:4 * b + 4, :], [sn], [])
                else:
                    dma(pcv_o[l, :, half * 512:(half + 1) * 512], st_[125:128, :], [sn], [])
                release(ptm[half][1])
        ck('p_xbc')
        ntok = 16 if sample else 128
        for piece in ('tmD', 'tmC', 'tmA', 'tmB'):
            wt, wn = load_w(l, piece)
            W = {'tmA': 512, 'tmB': 512, 'tmC': 512, 'tmD': 12}[piece]
            for ti, tix in enumerate(tiles):
                cs = slice(ti * 128, ti * 128 + ntok)
                pb, pn = bank()
                for kc in range(8):
                    mm(pb[0:ntok, 0:W], hT[:, kc, cs], wt[:, kc * W:(kc + 1) * W], kc == 0, kc == 7, [wn, 'hT'], [pn])
                r0 = (tix * 128) if not sample else 0
                if piece in ('tmA', 'tmB'):
                    which = 'sb' if piece == 'tmA' else 'fx'
                    st_, sn = next_stage()
                    cp(st_[0:ntok, 0:512], pb[0:ntok, :], [pn], [sn])
                    if sample:
                        ko, vo = (ssbk_o, ssbv_o) if which == 'sb' else (sfxk_o, sfxv_o)
                        cp(L.Vnew[which][0:16, :], pb[0:16, 256:512], [pn], [f'Vnew{which}'], eng='act')
                    else:
                        ko, vo = (psbk_o, psbv_o) if which == 'sb' else (pfxk_o, pfxv_o)
                        cp(L.V[which][:, tix, :], pb[:, 256:512], [pn], [f'V{which}{tix}'], eng='act')
                    dma(ko[l, r0:r0 + ntok, :], st_[0:ntok, 0:256], [sn], [])
                    dma(vo[l, r0:r0 + ntok, :], st_[0:ntok, 256:512], [sn], [])
                elif piece == 'tmC':
                    act(A.zs[0:ntok, ti, :], pb[0:ntok, :], AF.Silu, [pn], ['zs'])
                else:
                    tt(tmp8[0:ntok, 0:4], pb[0:ntok, 0:4], PR('bf_rep', rows=ntok), ALU.add, [pn, 'par'], ['tmp8'])
                    act(tmp8[0:ntok, 0:4], tmp8[0:ntok, 0:4], AF.Exp, ['tmp8'], ['tmp8'], scale=-1.0)
                    act(tmp8[0:ntok, 0:4], tmp8[0:ntok, 0:4], AF.Ln, ['tmp8', 'cb16'], ['tmp8'], bias=one_t[0:ntok, 0:1])
                    ts(lf_tok[0:ntok, :], tmp8[0:ntok, 0:4], -1.0, None, ALU.mult, None, ['tmp8'], ['lf_tok'])
                    lo = slf_o if sample else plf_o
                    dma(lo[l, r0:r0 + ntok, :], lf_tok[0:ntok, :], ['lf_tok'], [])
                    if sample:
                        cp(L.lfS[0:16, :], lf_tok[0:16, :], ['lf_tok'], ['lfS'])
                    else:
                        pq, pqn = bank()
                        mm(pq[:, 0:4], C('linc'), lf_tok[:, :], True, True, ['lf_tok', 'cst'], [pqn])
                        mm(pq[:, 4:8], C('ones'), lf_tok[:, :], True, True, ['lf_tok', 'cst'], [pqn])
                        stt(L.negFk[:, tix, :], pq[:, 0:4], -1.0, carryF[:, :], ALU.mult, ALU.subtract, [pqn, 'carryF'],
                            [f'negFk{tix}'])
                        tt(carryF[:, :], carryF[:, :], pq[:, 4:8], ALU.add, [pqn, 'carryF'], ['carryF'])
                    tt(tmp8[0:ntok, 8:16], pb[0:ntok, 4:12], PR('dtb', rows=ntok), ALU.add, [pn, 'par'], ['tmp8'])
                    act(tmp8[0:ntok, 8:16], tmp8[0:ntok, 8:16], AF.Exp, ['tmp8'], ['tmp8'])
                    act(A.dts[0:ntok, ti, :], tmp8[0:ntok, 8:16], AF.Ln, ['tmp8', 'cb16'], ['dts'], bias=one_t[0:ntok, 0:1])
            ck('tm_' + piece)
        ck('p_tm')
        for ti, tix in enumerate(tiles):
            ssd_tile(l, A, L, ti, ntok, sample)
            ck(f'p_ssd{ti}')

    def attn_prompt(l, A, L, gi):
        nkb = 4 * gi + 4
        spsum, spsn = F(2)
        spsumb, spsbn = H(0)
        for hc in range(2):
            pO, pOn = bank(True)
            for hp in range(2):
                h = hc * 2 + hp
                ps_ = slice(hp * 64, hp * 64 + 64)
                kbs = list(range(nkb - 1, -1, -1))
                st = {}

                def s0(idx):
                    kb = kbs[idx]
                    di = kb - 4 * gi
                    kres = f'KTsb{kb // 4}'
                    ksl = slice(kb * 128, (kb + 1) * 128)
                    pz, pzn = bank()
                    mm(pz[:, :], L.KT['sb'][ps_, hc, ksl], A.QT['sb'][ps_, hc, :], True, True, [kres, 'QTsb'], [pzn])
                    tb, tbn = F(idx % 2)
                    sp_, spn = H(1 + idx % 3)
                    act(tb[:], pz[:, :], AF.Exp, [pzn], [tbn])
                    act(sp_[:], tb[:], AF.Ln, [tbn, 'cb16'], [spn], bias=one_t[:, 0:1])
                    if di >= 0:
                        tt(sp_[:], sp_[:], msb(di), ALU.mult, [spn] + CB, [spn])
                    st[idx] = (kb, di, kres, ksl, sp_, spn)

                def s1(idx):
                    kb, di, kres, ksl, sp_, spn = st[idx]
                    w_, wn_ = H(4 + idx % 3)
                    pe_, pen = bank()
                    mm(pe_[:, :], L.KT['sb'][ps_, hc, ksl], A.QT['sb'][ps_, hc, :], True, False, [kres, 'QTsb'], [pen])
                    mm(pe_[:, :], nuincb[:], sp_[:], False, idx == 0, [spn] + CB, [pen])
                    if idx > 0:
                        mm(pe_[:, :], nonesb[:], spsumb[:], False, True, [spsbn] + CB, [pen])
                    act(w_[:], pe_[:, :], AF.Exp, [pen], [wn_])
                    if di >= 0:
                        tt(w_[:], w_[:], msb(di), ALU.mult, [wn_] + CB, [wn_])
                    if kb > 0:
                        if idx == 0:
                            cp(spsum[:], sp_[:], [spn], [spsn])
                        else:
                            tt(spsum[:], spsum[:], sp_[:], ALU.add, [spsn, spn], [spsn])
                        cp(spsumb[:], spsum[:], [spsn], [spsbn])
                    st[idx] = st[idx] + (w_, wn_)

                def s2(idx):
                    kb = st[idx][0]
                    w_, wn_ = st[idx][6], st[idx][7]
                    mm(pO[ps_, :], L.V['sb'][:, kb, h * 64:(h + 1) * 64], w_[:], idx == 0, kb == 0, [f'Vsb{kb}', wn_], [pOn])

                for it in range(nkb + 2):
                    if it < nkb:
                        s0(it)
                    if 0 <= it - 1 < nkb:
                        s1(it - 1)
                    if 0 <= it - 2 < nkb:
                        s2(it - 2)
            cp(A.yT['sb'][:, hc, :], pO[:, :], [pOn], ['ysbT'])
            release(pOn)
        fq, fqn = F(4)
        rec, recn = F(3)
        for hc in range(2):
            pN, pNn = bank(True)
            pD, pDn = bank(True)
            for hp in range(2):
                h = hc * 2 + hp
                ps_ = slice(hp * 64, hp * 64 + 64)
                pq, pqn = bank()
                mm(pq[:, :], sel4[:, h, :], cumT[:, :], True, True, ['cb16', 'cumT'], [pqn])
                cp(fq[:], pq[:, :], [pqn], [fqn], eng='act')
                st = {}

                def f0(kb):
                    di = kb - 4 * gi
                    kres = f'KTfx{kb // 4}'
                    ksl = slice(kb * 128, (kb + 1) * 128)
                    pz, pzn = bank()
                    mm(pz[:, :], L.KT['fx'][ps_, hc, ksl], A.QT['fx'][ps_, hc, :], True, True, [kres, 'QTfx'], [pzn])
                    tb, tbn = F(kb % 2)
                    w_, wn_ = H(4 + kb % 3)
                    stt(tb[:], pz[:, :], L.negFk[:, kb, h:h + 1], fq[:], ALU.add, ALU.add, [pzn, f'negFk{kb}', fqn], [tbn])
                    if di >= 0:
                        ts(tb[:], tb[:], 80.0, None, ALU.min, None, [tbn], [tbn])
                    act(w_[:], tb[:], AF.Exp, [tbn], [wn_])
                    if di >= 0:
                        tt(w_[:], w_[:], mfx(di), ALU.mult, [wn_] + CB, [wn_])
                    st[kb] = (w_, wn_)

                def f1(kb):
                    w_, wn_ = st[kb]
                    mm(pN[ps_, :], L.V['fx'][:, kb, h * 64:(h + 1) * 64], w_[:], kb == 0, kb == nkb - 1, [f'Vfx{kb}', wn_], [pNn])
                    mm(pD[ps_, :], onesb[:, 0:64], w_[:], kb == 0, kb == nkb - 1, [wn_] + CB, [pDn])

                for it in range(nkb + 1):
                    if it < nkb:
                        f0(it)
                    if 0 <= it - 1 < nkb:
                        f1(it - 1)
                recip(rec[ps_, :], pD[ps_, :], [pDn], [recn])
                tt(A.yT['fx'][ps_, hc, :], pN[ps_, :], rec[ps_, :], ALU.mult, [pNn, recn], ['yfxT'])
            release(pNn, pDn)

    def merge(l, A, n):
        for dc in range(8):
            pbr = []
            for bi, (nm, ysrc, nk, yres) in enumerate((('sbo', A.yT['sb'], 2, 'ysbT'), ('sso', A.yssT, 4, 'yssT'),
                                                        ('fxo', A.yT['fx'], 2, 'yfxT'))):
                wt, wn = load_w(l, f'fm{16 + bi * 8 + dc}')
                pg, pgn = bank()
                for kc in range(8):
                    mm(pg[:, 0:n], wt[:, kc * 128:(kc + 1) * 128], hT[:, kc, 0:n], kc == 0, kc == 7, [wn, 'hT'], [pgn])
                g_, gn_ = F(bi)
                act(g_[:, 0:n], pg[:, 0:n], AF.Sigmoid, [pgn], [gn_])
                wt, wn = load_w(l, f'{nm}{dc}')
                pb, pn = bank(True)
                for kc in range(nk):
                    mm(pb[:, 0:n], wt[:, kc * 128:(kc + 1) * 128], ysrc[:, kc, 0:n], kc == 0, kc == nk - 1, [wn, yres], [pn])
                pbr.append((pb, pn))
            ma, man = F(3)
            mb, mbn = F(4)
            tt(ma[:, 0:n], pbr[0][0][:, 0:n], Fp[0][:, 0:n], ALU.mult, [pbr[0][1], 'F0'], [man])
            tt(mb[:, 0:n], pbr[1][0][:, 0:n], Fp[1][:, 0:n], ALU.mult, [pbr[1][1], 'F1'], [mbn])
            tt(ma[:, 0:n], ma[:, 0:n], mb[:, 0:n], ALU.add, [man, mbn], [man])
            tt(mb[:, 0:n], pbr[2][0][:, 0:n], Fp[2][:, 0:n], ALU.mult, [pbr[2][1], 'F2'], [mbn])
            tt(A.mixT[:, dc, 0:n], ma[:, 0:n], mb[:, 0:n], ALU.add, [man, mbn], ['mixT'])
            release(*[x[1] for x in pbr])
        for dc in range(8):
            wt, wn = load_w(l, f'wo{dc}')
            pb, pn = bank()
            for kc in range(8):
                mm(pb[:, 0:n], wt[:, kc * 128:(kc + 1) * 128], A.mixT[:, kc, 0:n], kc == 0, kc == 7, [wn, 'mixT'], [pn])
            tt(xg[:, dc, 0:n], xg[:, dc, 0:n], pb[:, 0:n], ALU.add, ['xg', pn], ['xg'])

    def ffn(l, Dd, n):
        rms_to_hT(n, 'g2')
        for fc in range(32):
            wt, wn = load_w(l, f'up{fc}')
            pb, pn = bank()
            for kc in range(8):
                mm(pb[:, 0:n], wt[:, kc * 128:(kc + 1) * 128], hT[:, kc, 0:n], kc == 0, kc == 7, [wn, 'hT'], [pn])
            tb, tbn = F(fc % 2)
            act(tb[:, 0:n], pb[:, 0:n], AF.Relu, [pn], [tbn])
            tt(Dd.uT[:, fc, 0:n], tb[:, 0:n], tb[:, 0:n], ALU.mult, [tbn], ['uT'])
        for dc in range(8):
            wt, wn = load_w(l, f'dn{dc}')
            pb, pn = bank()
            for kc in range(32):
                mm(pb[:, 0:n], wt[:, kc * 128:(kc + 1) * 128], Dd.uT[:, kc, 0:n], kc == 0, kc == 31, [wn, 'uT'], [pn])
            tt(xg[:, dc, 0:n], xg[:, dc, 0:n], pb[:, 0:n], ALU.add, ['xg', pn], ['xg'])

    def final_out(Dd, n, tiles, sample):
        rms_to_hT(n, 'gf', out_f32=Dd.finT, ores='finT')
        ntok = 16 if sample else 128
        for ti, tix in enumerate(tiles):
            cs = slice(ti * 128, ti * 128 + ntok)
            for half in range(2):
                st_, sn = next_stage()
                pb, pn = bank()
                for k in range(4):
                    kc = half * 4 + k
                    tr(pb[0:ntok, k * 128:(k + 1) * 128], Dd.finT[:, kc, cs], ident, ['finT', 'cst'], [pn])
                cp(st_[0:ntok, :], pb[0:ntok, :], [pn], [sn], eng=('act' if half else 'dve'))
                if sample:
                    dma(ys_o[:, half * 512:(half + 1) * 512], st_[0:16, :], [sn], [])
                else:
                    dma(yp_o[tix * 128:(tix + 1) * 128, half * 512:(half + 1) * 512], st_[:, :], [sn], [])

    def gather(out_tile, res, src3d, l, b, j):
        col = b * NPG + j
        flat = src3d.rearrange("g r c -> (g r) c")
        P.dma('pool', lambda e: e.indirect_dma_start(out=out_tile[:], out_offset=None, in_=flat,
                                                     in_offset=bass.IndirectOffsetOnAxis(ap=idx_t[:, col:col + 1], axis=0),
                                                     element_offset=l * NPOOL * 128 * 512),
              ['idx_t'], [res])

    def attn_decode(l, A, L):
        NC_ = NPG * 8
        assert NC_ <= 512
        WH = ('sb', 'fx')
        for wi, which in enumerate(WH):
            memset(L.KTn[which][:].rearrange("p a b -> p (a b)"), 0.0, [f'KTn{which}'])
        memset(L.VnP[:], 0.0, ['VnP'])
        memset(L.cn4[:], 0.0, ['cn4'])
        S1, S2, S3 = L.S1, L.S2, L.S3
        Sn1, Sn2, Wn = L.Sn1, L.Sn2, L.Wn
        Sn1v = Sn1[:, :].rearrange("p (a b) -> p a b", a=2)
        Wnv = Wn[:, :].rearrange("p (a b) -> p a b", a=2)

        def jcol(which, j):
            return (NPG - 1 - j) if which == 'sb' else j
        for b in range(NB):
            qc = slice(b * 4, b * 4 + 4)
            for which in WH:
                Qb = L.Qbd[which]
                memset(Qb[:].rearrange("p a b -> p (a b)"), 0.0, [f'Qbd{which}'])
                for hc in range(2):
                    for hp in range(2):
                        ps_ = slice(hp * 64, hp * 64 + 64)
                        cp(Qb[ps_, hc, hp * 4:hp * 4 + 4], A.QT[which][ps_, hc, qc], [f'QT{which}', f'Qbd{which}'], [f'Qbd{which}'])
                    cp(L.KTn[which][:, hc, 0:4], L.KTs[which][:, hc, qc], [f'KTs{which}', f'KTn{which}'], [f'KTn{which}'])
            pS = {w_: [bank(True), bank(True)] for w_ in WH}
            for j in range(NPG):
                kt_, ktn = L.kpg[j % 3], f'kpg{j % 3}'
                gather(kt_, ktn, ckk_d, l, b, j)
                pt_, ptn = bank()
                for q4 in range(4):
                    tr(pt_[:, q4 * 128:(q4 + 1) * 128], kt_[:, q4 * 128:(q4 + 1) * 128], ident, [ktn, 'cst'], [ptn])
                kb_, kbn = L.ktb[j % 3], f'ktb{j % 3}'
                cp(kb_[:], pt_[:, :], [ptn], [kbn], eng=('act' if j % 2 else 'dve'))
                for wi, which in enumerate(WH):
                    jc = jcol(which, j)
                    for hc in range(2):
                        mm(pS[which][hc][0][:, jc * 8:(jc + 1) * 8], kb_[:, wi * 256 + hc * 128:wi * 256 + (hc + 1) * 128],
                           L.Qbd[which][:, hc, :], True, True, [kbn, f'Qbd{which}'], [pS[which][hc][1]])
            pSn, pSnn = bank(True)
            for wi, which in enumerate(WH):
                for hc in range(2):
                    mm(pSn[:, wi * 16 + hc * 8:wi * 16 + (hc + 1) * 8], L.KTn[which][:, hc, :], L.Qbd[which][:, hc, :], True, True,
                       [f'KTn{which}', f'Qbd{which}'], [pSnn])
            which = 'fx'
            mnew2 = C('mnfx').unsqueeze(1).to_broadcast([128, 2, 8])
            lfp, lfw, tot64, Fkd = L.lfp, L.lfw, L.tot64, L.Fkd
            P.dma('pool', lambda e, b=b: e.indirect_dma_start(
                out=lfp[0:NPG, :], out_offset=None, in_=clf_d.rearrange("l p c -> (l p) c"),
                in_offset=bass.IndirectOffsetOnAxis(ap=pcol[0:NPG, b:b + 1], axis=0),
                element_offset=l * NPOOL * 512), ['pcol'], ['lfp'])
            for h in range(4):
                colv = slice(h, 512, 4)
                scan(lfw[0:NPG, colv], ones_f[0:NPG, 0:128], lfp[0:NPG, colv], 0.0, ['lfp', 'cb16'], ['lfw'])
            cp(tot64[0:NPG, :], lfw[0:NPG, 508:512], ['lfw'], ['tot64'])
            pq, pqn = bank()
            mm(pq[0:NPG, 0:4], C('linc', rows=NPG, c1=NPG), tot64[0:NPG, :], True, True, ['tot64', 'cst'], [pqn])
            mm(pq[:, 8:12], C('ones', rows=NPG), tot64[0:NPG, :], True, True, ['tot64', 'cst'], [pqn])
            tt(tot64[0:NPG, :], pq[0:NPG, 0:4], tot64[0:NPG, :], ALU.subtract, [pqn, 'tot64'], ['tot64'])
            lfw3 = lfw[0:NPG, :].rearrange("p (r h) -> p r h", h=4)
            tt(lfw3, lfw3, tot64[0:NPG, :].unsqueeze(1).to_broadcast([NPG, 128, 4]), ALU.add, ['lfw', 'tot64'], ['lfw'])
            cp(L.ftot[:, :], pq[:, 8:12], [pqn], ['ftot'])
            for h in range(4):
                pt_, ptn = bank()
                tr(pt_[:, 0:NPG], lfw[0:NPG, h:512:4], C('ident', rows=NPG, c1=NPG), ['lfw', 'cst'], [ptn])
                ts(Fkd[:, 0:NPG, h], pt_[:, 0:NPG], -1.0, L.ftot[:, h:h + 1], ALU.mult, ALU.add, [ptn, 'ftot'], ['Fkd'])
            for hc in range(2):
                sl = slice(hc * NC_, (hc + 1) * NC_)
                o3 = S3[:, sl].rearrange("p (j a q) -> p j a q", a=2, q=4)
                i3 = pS[which][hc][0][:, 0:NC_].rearrange("p (j a q) -> p j a q", a=2, q=4)
                f3 = Fkd[:, 0:NPG, hc * 2:hc * 2 + 2].unsqueeze(3).to_broadcast([128, NPG, 2, 4])
                tt(o3, i3, f3, ALU.add, [pS[which][hc][1], 'Fkd'], ['S3'])
            release(pS['fx'][0][1], pS['fx'][1][1])
            act(S3[:, 0:2 * NC_], S3[:, 0:2 * NC_], AF.Exp, ['S3'], ['S3'])
            cp(L.S3b[which][:, 0:2 * NC_], S3[:, 0:2 * NC_], ['S3'], [f'S3b{which}'])
            pn_, pnn = bank()
            mm(pn_[0:16, 0:4], C('lincS', rows=16), L.lfS[0:16, :], True, True, ['lfS', 'cst'], [pnn])
            cp(L.cnS[0:16, :], pn_[0:16, 0:4], [pnn], ['cnS'])
            mm(pn_[0:4, 8:12], C('ident', rows=16, c0=b * 4, c1=b * 4 + 4), L.cnS[0:16, :], True, True, ['cnS', 'cst'], [pnn])
            cp(L.cn4[0:4, :], pn_[0:4, 8:12], [pnn, 'cn4'], ['cn4'])
            o3 = Wn[:, :].rearrange("p (h q) -> p h q", q=4)
            i3 = pSn[:, 16:32].rearrange("p (h q) -> p h q", q=4)
            tt(o3, i3, L.cn4[:, :].unsqueeze(2).to_broadcast([128, 4, 4]), ALU.subtract, [pSnn, 'cn4'], ['Wn'])
            act(Wn[:, :], Wn[:, :], AF.Exp, ['Wn'], ['Wn'])
            tt(Wnv, Wnv, mnew2, ALU.mult, ['Wn', 'cst'], ['Wn'])
            cp(L.Wnb[which][:], Wn[:, :], ['Wn'], [f'Wnb{which}'])
            for hc in range(2):
                sl = slice(hc * NC_, (hc + 1) * NC_)
                P.op('dve', lambda e, sl=sl, hc=hc: e.tensor_reduce(
                    L.ydec[:, hc, :], S3[:, sl].rearrange("p (j c) -> p c j", c=8), AX.X, ALU.add), ['S3'], ['ydec'])
                tt(L.ydec[:, hc, :], L.ydec[:, hc, :], Wn[:, hc * 8:(hc + 1) * 8], ALU.add, ['ydec', 'Wn'], ['ydec'])
                pd_, pdn = bank()
                mm(pd_[:, 0:8], C('ones'), L.ydec[:, hc, :], True, True, ['ydec', 'cst'], [pdn])
                recip(L.rden[:, hc, :], pd_[:, 0:8], [pdn], ['rden'])
            which = 'sb'
            mnew2 = C('mnsb').unsqueeze(1).to_broadcast([128, 2, 8])
            for hc in range(2):
                act(S1[:, hc * NC_:(hc + 1) * NC_], pS[which][hc][0][:, 0:NC_], AF.Exp, [pS[which][hc][1]], ['S1'])
            act(S1[:, 0:2 * NC_], S1[:, 0:2 * NC_], AF.Ln, ['S1', 'cb16'], ['S1'], bias=one_t[:, 0:1])
            act(Sn1[:, :], pSn[:, 0:16], AF.Exp, [pSnn], ['Sn1'])
            act(Sn1[:, :], Sn1[:, :], AF.Ln, ['Sn1', 'cb16'], ['Sn1'], bias=one_t[:, 0:1])
            tt(Sn1v, Sn1v, mnew2, ALU.mult, ['Sn1', 'cst'], ['Sn1'])
            pcs = [bank(True), bank(True)]
            for hc in range(2):
                mm(pcs[hc][0][:, 0:NC_], C('uincr'), S1[:, hc * NC_:(hc + 1) * NC_], True, True, ['S1', 'cst'], [pcs[hc][1]])
            pn_, pnn = bank(True)
            mm(pn_[:, 0:16], C('uincr'), Sn1[:, :], True, True, ['Sn1', 'cst'], [pnn])
            mm(pn_[:, 16:32], C('ones'), Sn1[:, :], True, True, ['Sn1', 'cst'], [pnn])
            cp(Sn2[:, :], pn_[:, 16:32], [pnn], ['Sn2'])
            for hc in range(2):
                ptt_, pttn = bank()
                mm(ptt_[:, 0:NC_], C('ones'), S1[:, hc * NC_:(hc + 1) * NC_], True, True, ['S1', 'cst'], [pttn])
                cp(S2[:, hc * NC_:(hc + 1) * NC_], ptt_[:, 0:NC_], [pttn], ['S2'], eng='act')
            for hc in range(2):
                for c8 in range(8):
                    colv = slice(hc * NC_ + c8, (hc + 1) * NC_, 8)
                    scan(S3[:, colv], ones_f[:, 0:NPG], S2[:, colv], Sn2[:, hc * 8 + c8:hc * 8 + c8 + 1],
                         ['S2', 'Sn2', 'cb16'], ['S3'])
            for hc in range(2):
                sl = slice(hc * NC_, (hc + 1) * NC_)
                tt(S3[:, sl], S3[:, sl], S2[:, sl], ALU.subtract, ['S3', 'S2'], ['S3'])
                tt(S3[:, sl], S3[:, sl], pcs[hc][0][:, 0:NC_], ALU.add, ['S3', pcs[hc][1]], ['S3'])
                tt(S3[:, sl], pS[which][hc][0][:, 0:NC_], S3[:, sl], ALU.subtract, [pS[which][hc][1], 'S3'], ['S3'])
            act(S3[:, 0:2 * NC_], S3[:, 0:2 * NC_], AF.Exp, ['S3'], ['S3'])
            cp(L.S3b[which][:, 0:2 * NC_], S3[:, 0:2 * NC_], ['S3'], [f'S3b{which}'])
            cp(Sn1[:, :], pn_[:, 0:16], [pnn], ['Sn1'])
            tt(Wn[:, :], pSn[:, 0:16], Sn1[:, :], ALU.subtract, [pSnn, 'Sn1'], ['Wn'])
            act(Wn[:, :], Wn[:, :], AF.Exp, ['Wn'], ['Wn'])
            tt(Wnv, Wnv, mnew2, ALU.mult, ['Wn', 'cst'], ['Wn'])
            cp(L.Wnb[which][:], Wn[:, :], ['Wn'], [f'Wnb{which}'])
            release(pcs[0][1], pcs[1][1], pnn, pS['sb'][0][1], pS['sb'][1][1], pSnn)
            pO = {w_: [bank(True), bank(True)] for w_ in WH}
            for j in range(NPG):
                vt_, vtn = L.vpg[j % 3], f'vpg{j % 3}'
                gather(vt_, vtn, cvv_d, l, b, j)
                vb_, vbn = L.ktb[j % 3], f'ktb{j % 3}'
                cp(vb_[:], vt_[:], [vtn], [vbn], eng=('act' if j % 2 else 'dve'))
                for wi, which in enumerate(WH):
                    jc = jcol(which, j)
                    for hc in range(2):
                        mm(pO[which][hc][0][:, 0:8], vb_[:, wi * 256 + hc * 128:wi * 256 + (hc + 1) * 128],
                           L.S3b[which][:, hc * NC_ + jc * 8:hc * NC_ + jc * 8 + 8],
                           j == 0, False, [vbn, f'S3b{which}'], [pO[which][hc][1]])
            for wi, which in enumerate(WH):
                dma(L.VnP[0:4, :], L.Vnew[which][b * 4:b * 4 + 4, :], [f'Vnew{which}', 'VnP'], ['VnP'])
                cp(L.VnPb[:], L.VnP[:], ['VnP'], ['VnPb'])
                for hc in range(2):
                    mm(pO[which][hc][0][:, 0:8], L.VnPb[:, hc * 128:(hc + 1) * 128], L.Wnb[which][:, hc * 8:(hc + 1) * 8], False, True,
                       ['VnPb', f'Wnb{which}'], [pO[which][hc][1]])
                for hc in range(2):
                    for hp in range(2):
                        ps_ = slice(hp * 64, hp * 64 + 64)
                        src = pO[which][hc][0][ps_, hp * 4:hp * 4 + 4]
                        if which == 'fx':
                            tt(A.yT['fx'][ps_, hc, qc], src, L.rden[ps_, hc, hp * 4:hp * 4 + 4], ALU.mult,
                               [pO[which][hc][1], 'rden'], ['yfxT'])
                        else:
                            cp(A.yT['sb'][ps_, hc, qc], src, [pO[which][hc][1]], ['ysbT'])
                release(pO[which][0][1], pO[which][1][1])

    def alloc_A(S):
        A = Bag()
        A.QT = {'sb': S.sb("QTsb", [128, 2, 512], BF16), 'fx': S.sb("QTfx", [128, 2, 512], BF16)}
        A.xsT = S.sb("xsT", [128, 4, 512])
        A.BTf = S.sb("BTf", [128, 2, 512])
        A.BTb = S.sb("BTb", [128, 2, 512], BF16)
        A.CTb = S.sb("CTb", [128, 2, 512], BF16)
        A.zs = S.sb("zs", [128, 4, 512])
        A.dts = S.sb("dts", [128, 4, 8])
        A.yT = {'sb': S.sb("ysbT", [128, 2, 512], BF16), 'fx': S.sb("yfxT", [128, 2, 512], BF16)}
        A.yssT = S.sb("yssT", [128, 4, 512], BF16)
        A.mixT = S.sb("mixT", [128, 8, 512], BF16)
        A.Xb = [S.sb(f"Xb{i}", [128, 515]) for i in range(2)]
        A.B_tok = S.sb("B_tok", [128, 256], BF16)
        for nm in ('dta', 'acs', 'ea', 'te', 'wdt', 'cd'):
            setattr(A, nm, S.sb(nm, [128, 8]))
        A.scm = S.sb("scm", [128, 2, 128])
        A.Ld = [S.sb(f"Ld{i}", [128, 128]) for i in range(2)]
        A.MT = S.sb("MT", [128, 4, 128], BF16)
        A.ssq = S.sb("ssq", [128, 2])
        A.rs2 = S.sb("rs2", [128, 2])
        return A

    groups = [(gi, 512, list(range(gi * 4, gi * 4 + 4)), False) for gi in range(NG)]
    groups.append((NG, NS, [0], True))
    try:
        cast_layer(0)
        for l in range(DEPTH):
            dma(par[:], par_d[l], (), ['par'])
            act(arep[:], PR('alog'), AF.Exp, ['par'], ['par2'])
            ts(arep[:], arep[:], -1.0, None, ALU.mult, None, ['par2'], ['par2'])
            ts(NBF[:], PR('bf_col'), -1.0, None, ALU.mult, None, ['par'], ['par2'])
            memset(carryF[:], 0.0, ['carryF'])
            memset(carryFT[:], 0.0, ['carryFT'])
            memset(ccar[:].rearrange("p a b -> p (a b)"), 0.0, ['ccar'])
            memset(hst[:].rearrange("p a b -> p (a b)"), 0.0, ['hst'])
            memset(hstb[:].rearrange("p a b -> p (a b)"), 0.0, ['hstb'])
            LS = Scope()
            L = Bag()
            L.KT = {'sb': LS.sb("KTsb", [128, 2, T], BF16), 'fx': LS.sb("KTfx", [128, 2, T], BF16)}
            L.V = {'sb': LS.sb("Vsb", [128, NT, 256], BF16), 'fx': LS.sb("Vfx", [128, NT, 256], BF16)}
            L.negFk = LS.sb("negFk", [128, NT, 4])
            for (gi, n, tiles, sample) in groups:
                if sample:
                    LS.close()
                    LS = Scope()
                    L = Bag()
                    L.KTs = {'sb': LS.sb("KTssb", [128, 2, 16], BF16), 'fx': LS.sb("KTsfx", [128, 2, 16], BF16)}
                    L.Vnew = {'sb': LS.sb("Vnsb", [16, 256]), 'fx': LS.sb("Vnfx", [16, 256])}
                    L.lfS = LS.sb("lfS", [16, 4])
                    L.XS = LS.sb("XS", [128, 8, NB, 7])
                    L.h0 = [LS.sb(f"h0_{i}", [64, 8, 128]) for i in range(1)]
                    L.h0Tb = LS.sb("h0Tb", [128, NB, 8, 64], BF16)
                    L.CTm = LS.sb("CTm", [128, NB, 2, 16], BF16)
                    L.xwm = LS.sb("xwm", [16, 512], BF16)
                    L.cdS = LS.sb("cdS", [128, NB, 8])
                    L.kpg = [LS.sb(f"kpg{i}", [128, 512]) for i in range(3)]
                    L.vpg = [LS.sb(f"vpg{i}", [128, 512]) for i in range(3)]
                    L.ktb = [LS.sb(f"ktb{i}", [128, 512], BF16) for i in range(3)]
                    L.Qbd = {w_: LS.sb(f"Qbd{w_}", [128, 2, 8], BF16) for w_ in ('sb', 'fx')}
                    L.KTn = {w_: LS.sb(f"KTn{w_}", [128, 2, 128], BF16) for w_ in ('sb', 'fx')}
                    L.S1 = LS.sb("S1", [128, 1024])
                    L.S2 = LS.sb("S2", [128, 1024])
                    L.S3 = LS.sb("S3", [128, 1024])
                    L.S3b = {w_: LS.sb(f"S3b{w_}", [128, 1024], BF16) for w_ in ('sb', 'fx')}
                    L.VnPb = LS.sb("VnPb", [128, 256], BF16)
                    L.Wnb = {w_: LS.sb(f"Wnb{w_}", [128, 16], BF16) for w_ in ('sb', 'fx')}
                    L.Sn1 = LS.sb("Sn1", [128, 16])
                    L.Sn2 = LS.sb("Sn2", [128, 16])
                    L.Wn = LS.sb("Wn", [128, 16])
                    L.lfp = LS.sb("lfp", [64, 512])
                    L.lfw = LS.sb("lfw", [64, 512])
                    L.Fkd = LS.sb("Fkd", [128, 64, 4])
                    L.tot64 = LS.sb("tot64", [64, 4])
                    L.ftot = LS.sb("ftot", [128, 4])
                    L.ydec = LS.sb("ydec", [128, 2, 8])
                    L.rden = LS.sb("rden", [128, 2, 8])
                    L.VnP = LS.sb("VnP", [128, 256])
                    L.cnS = LS.sb("cnS", [16, 4])
                    L.cn4 = LS.sb("cn4", [128, 4])
                if l == 0:
                    if sample:
                        load_x(xs_d, NS, 0)
                    else:
                        for ti in range(4):
                            load_x(xp_d[(gi * 4 + ti) * 128:(gi * 4 + ti + 1) * 128, :], 128, ti * 128)
                else:
                    dma(xg[:].rearrange("p a b -> p (a b)"), xscr[gi], [f'xscr{gi}'], ['xg'])
                ck(f'load{l}.{gi}')
                AS = Scope()
                A = alloc_A(AS)
                if sample:
                    dma(L.XS[:, :, :, 0:3], scv_d[l], (), ['XS'])
                proj_group(l, A, L, gi, n, tiles, sample)
                ck(f'proj{l}.{gi}')
                if gi == 0 and l + 1 < DEPTH:
                    cast_layer(l + 1)
                if sample:
                    attn_decode(l, A, L)
                else:
                    attn_prompt(l, A, L, gi)
                    if gi == NG - 1:
                        for half in range(2):
                            st_, sn = next_stage()
                            pb, pn = bank()
                            for k in range(4):
                                h = half * 4 + k
                                tr(pb[0:64, k * 128:(k + 1) * 128], hst[:, h, :], ident, ['hst', 'cst'], [pn])
                            cp(st_[0:64, :], pb[0:64, :], [pn], [sn])
                            dma(pssm_o[l, half * 4:half * 4 + 4].rearrange("h p n -> p h n"),
                                st_[0:64, :].rearrange("p (h n) -> p h n", h=4), [sn], [])
                ck(f'attn{l}.{gi}')
                merge(l, A, n)
                ck(f'merge{l}.{gi}')
                AS.close()
                DS = Scope()
                Dd = Bag()
                Dd.uT = DS.sb("uT", [128, 32, 512], BF16)
                if l == DEPTH - 1:
                    Dd.finT = DS.sb("finT", [128, 8, 512])
                ffn(l, Dd, n)
                ck(f'ffn{l}.{gi}')
                if l == DEPTH - 1:
                    final_out(Dd, n, tiles, sample)
                else:
                    dma(xscr[gi], xg[:].rearrange("p a b -> p (a b)"), ['xg'], [f'xscr{gi}'])
                DS.close()
            LS.close()
    except _Stop:
        pass
    for sc in reversed(list(open_scopes)):
        if sc is not G:
            sc.close()
    P.emit()
    G.es.close()
    pes.close()
    return nc


def host_inputs(inp, cfg, core, kk, vv):
    T, NB, NPG, NPOOL, DEPTH = cfg['T'], cfg['NB'], cfg['NPG'], cfg['NPOOL'], cfg['DEPTH']
    b0 = core * NB
    m = {}
    m['xp'] = np.ascontiguousarray(inp['x_prompt'][core])
    m['xs'] = np.ascontiguousarray(inp['x_sample'][b0:b0 + NB].reshape(NB * 4, D))
    m['ckk'] = kk
    m['cvv'] = vv
    m['clf'] = inp['cache_fox_logf'].reshape(DEPTH, NPOOL, 512)
    m['sst'] = np.ascontiguousarray(inp['state_ssm'][:, b0:b0 + NB])
    sc = inp['state_conv'][:, b0:b0 + NB]
    m['scv'] = np.ascontiguousarray(sc.reshape(DEPTH, NB, 3, 8, 128).transpose(0, 4, 3, 1, 2))
    m['ptab'] = np.ascontiguousarray(inp['page_table'][b0:b0 + NB].reshape(1, NB * NPG).astype(np.int32))
    return m


_CACHE = {}


def run(inp, cfg, ncores, stop_at=None):
    inp = {k: np.asarray(v) for k, v in inp.items()}
    DEPTH = cfg['DEPTH']
    key = tuple(sorted(cfg.items()))
    if key not in _CACHE:
        _CACHE[key] = build(cfg, stop_at)
    nc = _CACHE[key]
    wall = prep_weights(inp, DEPTH)
    par = prep_params(inp, DEPTH)
    cst = prep_consts()
    cmask = prep_cmask()
    NPOOL_ = cfg['NPOOL']
    kk = np.concatenate([inp['cache_sb_k'].reshape(DEPTH * NPOOL_, 128, 256),
                         inp['cache_fox_k'].reshape(DEPTH * NPOOL_, 128, 256)], axis=2)
    vv = np.concatenate([inp['cache_sb_v'].reshape(DEPTH * NPOOL_, 128, 256),
                         inp['cache_fox_v'].reshape(DEPTH * NPOOL_, 128, 256)], axis=2)
    in_maps = []
    for c in range(ncores):
        m = host_inputs(inp, cfg, c, kk, vv)
        m['wall'] = wall
        m['par'] = par
        m['cst'] = cst
        m['cmask'] = cmask
        in_maps.append(m)
    res = run_bass_kernel_spmd(nc, in_maps, core_ids=list(range(ncores)))
    R = res.results
    T, NB = cfg['T'], cfg['NB']

    def cat(name, shape_per_core, axis):
        return np.concatenate([R[c][name].reshape(shape_per_core) for c in range(ncores)], axis=axis)
    y_prompt = cat('yp', (1, T, D), 0)
    y_sample = cat('ys', (NB, 4, D), 0)
    outs = [y_prompt, y_sample]
    for nm in ('psbk', 'psbv', 'pfxk', 'pfxv'):
        outs.append(cat(nm, (DEPTH, 1, T, 4, 64), 1))
    outs.append(cat('plf', (DEPTH, 1, T, 4), 1))
    outs.append(cat('pssm', (DEPTH, 1, 8, 64, 128), 1))
    outs.append(cat('pcv', (DEPTH, 1, 3, 1024), 1))
    for nm in ('ssbk', 'ssbv', 'sfxk', 'sfxv'):
        outs.append(cat(nm, (DEPTH, NB, 4, 4, 64), 1))
    outs.append(cat('slf', (DEPTH, NB, 4, 4), 1))
    outs.append(cat('sssm', (DEPTH, NB, 8, 64, 128), 1))
    outs.append(cat('scvo', (DEPTH, NB, 3, 1024), 1))
    return tuple(np.ascontiguousarray(o.astype(np.float32)) for o in outs)


def kernel(**inputs):
    return run(inputs, CFG_FULL, NCORES)
```

```python
import numpy as np
from contextlib import ExitStack
import concourse.bass as bass
import concourse.mybir as mybir
from concourse.bass_utils import run_bass_kernel_spmd

F32 = mybir.dt.float32
BF16 = mybir.dt.bfloat16
I32 = mybir.dt.int32
AF = mybir.ActivationFunctionType
ALU = mybir.AluOpType
AX = mybir.AxisListType

ENG = ['pe', 'act', 'dve', 'pool', 'sp']
D = 1024
NCORES = 8
EPS = 1e-6
CFG_FULL = dict(T=2048, NB=4, NPG=64, NPOOL=2560, DEPTH=2)


class Prog:
    def __init__(self, nc, n_dma_sems=8):
        self.nc = nc
        self.ops = {e: [] for e in ENG}
        self.cnt = {e: 0 for e in ENG}
        self.lastw = {}
        self.readers = {}
        self.dma_cnt = {}
        self.dma_rr = {q: 0 for q in ENG}
        self.nd = n_dma_sems

    def _deps(self, reads, writes, eng=None):
        deps = {}

        def add(k, v):
            if deps.get(k, 0) < v:
                deps[k] = v
        for r in reads:
            if r in self.lastw:
                add(*self.lastw[r])
            if r.startswith('ps'):
                for k, v in self.readers.get(r, {}).items():
                    if k != ('c', eng):
                        add(k, v)
        for w in writes:
            if w in self.lastw:
                add(*self.lastw[w])
            for k, v in self.readers.get(w, {}).items():
                add(k, v)
        return deps

    def _commit(self, tok, reads, writes):
        k, v = tok
        for r in reads:
            d = self.readers.setdefault(r, {})
            if d.get(k, 0) < v:
                d[k] = v
        for w in writes:
            self.lastw[w] = tok
            self.readers[w] = {}

    def op(self, eng, fn, reads=(), writes=()):
        deps = self._deps(reads, writes, eng)
        self.cnt[eng] += 1
        tok = (('c', eng), self.cnt[eng])
        self.ops[eng].append((fn, deps, tok))
        self._commit(tok, reads, writes)

    def dma(self, q, fn, reads=(), writes=()):
        deps = self._deps(reads, writes)
        k = self.dma_rr[q]
        self.dma_rr[q] = (k + 1) % self.nd
        c = self.dma_cnt.get((q, k), 0)
        if c > 0:
            deps[('d', q, k)] = max(deps.get(('d', q, k), 0), c)
        self.dma_cnt[(q, k)] = c + 1
        tok = (('d', q, k), c + 1)
        self.ops[q].append((fn, deps, tok))
        self._commit(tok, reads, writes)

    def barrier(self):
        deps = {('c', e): self.cnt[e] for e in ENG if self.cnt[e] > 0}
        for (q, k), c in self.dma_cnt.items():
            deps[('d', q, k)] = c
        for e in ENG:
            self.ops[e].append((None, dict(deps), None))

    def emit(self):
        nc = self.nc
        sem_c = {e: nc.alloc_semaphore(name=f"c_{e}") for e in ENG}
        sem_d = {}
        for (q, k) in sorted(self.dma_cnt):
            sem_d[(q, k)] = nc.alloc_semaphore(name=f"d_{q}{k}")

        def semof(key):
            if key[0] == 'c':
                return sem_c[key[1]], 1
            return sem_d[(key[1], key[2])], 16

        final_waits = list(self.dma_cnt.items())

        def run(ename, eng):
            seen = {}
            for fn, deps, tok in self.ops[ename]:
                for key, val in deps.items():
                    if key == ('c', 'pe') and ename == 'pe':
                        continue
                    if seen.get(key, 0) >= val:
                        continue
                    s, mul = semof(key)
                    eng.wait_ge(s, val * mul)
                    seen[key] = val
                if fn is None:
                    continue
                ins = fn(eng)
                s, mul = semof(tok[0])
                ins.then_inc(s, mul)
            if ename == 'sp':
                for (q, k), c in final_waits:
                    eng.wait_ge(sem_d[(q, k)], 16 * c)

        with nc.Block() as block:
            @block.sync
            def _(e):
                run('sp', e)

            @block.scalar
            def _(e):
                run('act', e)

            @block.vector
            def _(e):
                run('dve', e)

            @block.gpsimd
            def _(e):
                run('pool', e)

            @block.tensor
            def _(e):
                run('pe', e)


SB_W, FX_W = 256, 256
OFF_QSB, OFF_KSB, OFF_VSB = 0, 256, 512
OFF_QFX, OFF_KFX, OFF_VFX = 768, 1024, 1280
OFF_F = 1536
OFF_Z = 1540
OFF_XBC = 2052
OFF_DT = 3076
OFF_G = 3084


def wblocks():
    bl = []
    for cb in range(40):
        bl.append((f'fm{cb}', 1024))
    bl.append(('f4', 32))
    bl += [('tmA', 4096), ('tmB', 4096), ('tmC', 4096), ('tmD', 96)]
    for cb in range(8):
        bl.append((f'sbo{cb}', 256))
    for cb in range(8):
        bl.append((f'sso{cb}', 512))
    for cb in range(8):
        bl.append((f'fxo{cb}', 256))
    for cb in range(8):
        bl.append((f'wo{cb}', 1024))
    for cb in range(32):
        bl.append((f'up{cb}', 1024))
    for cb in range(8):
        bl.append((f'dn{cb}', 4096))
    return bl


def wlayout():
    off = {}
    r = 0
    for name, E in wblocks():
        off[name] = (r, E)
        r += 128 * E // 1024
    nrows = ((r + 1023) // 1024) * 1024
    return off, nrows


def _blk(W, cols):
    K = W.shape[0]
    sub = W[:, cols]
    return np.ascontiguousarray(sub.reshape(K // 128, 128, len(cols)).transpose(1, 0, 2))


def prep_weights(inp, depth):
    off, nrows = wlayout()
    wall = np.zeros((depth, nrows * 1024), np.float32)
    fm_cols = (list(range(OFF_QSB, OFF_QSB + 256)) + list(range(OFF_KSB, OFF_KSB + 256)) +
               list(range(OFF_QFX, OFF_QFX + 256)) + list(range(OFF_KFX, OFF_KFX + 256)) +
               list(range(OFF_XBC, OFF_XBC + 1024)) + list(range(OFF_G, OFF_G + 3072)))
    for l in range(depth):
        win = inp['w_in'][l]

        def put(name, arr):
            r, E = off[name]
            a = arr.reshape(128, -1)
            assert a.shape[1] == E, (name, a.shape, E)
            wall[l, r * 1024: r * 1024 + 128 * E] = a.reshape(-1)
        for cb in range(40):
            put(f'fm{cb}', _blk(win, fm_cols[cb * 128:(cb + 1) * 128]))
        put('f4', _blk(win, list(range(OFF_F, OFF_F + 4))))
        put('tmA', _blk(win, list(range(OFF_KSB, OFF_KSB + 512))))
        put('tmB', _blk(win, list(range(OFF_KFX, OFF_KFX + 512))))
        put('tmC', _blk(win, list(range(OFF_Z, OFF_Z + 512))))
        put('tmD', _blk(win, list(range(OFF_F, OFF_F + 4)) + list(range(OFF_DT, OFF_DT + 8))))
        for cb in range(8):
            cs = list(range(cb * 128, (cb + 1) * 128))
            put(f'sbo{cb}', _blk(inp['w_sb_out'][l], cs))
            put(f'sso{cb}', _blk(inp['w_ssm_out'][l], cs))
            put(f'fxo{cb}', _blk(inp['w_fox_out'][l], cs))
            put(f'wo{cb}', _blk(inp['w_o'][l], cs))
            put(f'dn{cb}', _blk(inp['w_down'][l], cs))
        for cb in range(32):
            put(f'up{cb}', _blk(inp['w_up'][l], list(range(cb * 128, (cb + 1) * 128))))
    return wall.reshape(depth, nrows, 1024)


PAR = {}
_o = 0
for _n, _w in [('g1', 8), ('g2', 8), ('gf', 8), ('bf_rep', 4), ('bf_col', 1), ('cw', 32), ('cb', 8),
               ('dtb', 8), ('alog', 8), ('dsk', 8), ('gn', 512)]:
    PAR[_n] = (_o, _w)
    _o += _w
NPAR = _o


def prep_params(inp, depth):
    par = np.zeros((depth, 128, NPAR), np.float32)

    def col(v):
        return v.reshape(8, 128).T
    for l in range(depth):
        def put(n, a):
            o, w = PAR[n]
            par[l, :, o:o + w] = a
        put('g1', col(inp['norm1_g'][l]))
        put('g2', col(inp['norm2_g'][l]))
        put('gf', col(inp['final_norm_g']))
        put('bf_rep', np.broadcast_to(inp['b_forget'][l][None, :], (128, 4)))
        bc = np.zeros((128, 1), np.float32)
        bc[0:4, 0] = inp['b_forget'][l]
        put('bf_col', bc)
        put('cw', inp['conv_w'][l].reshape(4, 8, 128).transpose(2, 1, 0).reshape(128, 32))
        put('cb', col(inp['conv_b'][l]))
        put('dtb', np.broadcast_to(inp['dt_bias'][l][None, :], (128, 8)))
        put('alog', np.broadcast_to(inp['a_log'][l][None, :], (128, 8)))
        put('dsk', np.broadcast_to(inp['d_skip'][l][None, :], (128, 8)))
        put('gn', np.broadcast_to(inp['ssm_norm_g'][l][None, :], (128, 512)))
    return par


CST = {}
_o = 0
for _n, _w in [('ident', 128), ('linc', 128), ('ustr', 128), ('ones', 128), ('uincr', 128),
               ('lincS', 16), ('ustrS', 16), ('onesS', 16), ('selB', 512), ('rowB', 4), ('colB', 64),
               ('mnsb', 8), ('mnfx', 8), ('iota', 1)]:
    CST[_n] = (_o, _w)
    _o += _w
NCST = _o


def prep_cmask():
    j = np.arange(128)[:, None]
    q = np.arange(512)[None, :]
    m = np.zeros((128, 4, 513), np.float32)
    for i in range(4):
        m[:, i, 1:] = ((i * 128 + j) <= q)
    return m.reshape(128, 4 * 513)


def prep_consts():
    c = np.zeros((128, NCST), np.float32)
    j = np.arange(128)[:, None]
    l = np.arange(128)[None, :]

    def put(n, a):
        o, w = CST[n]
        c[:a.shape[0], o:o + w] = a
    put('ident', (j == l).astype(np.float32))
    put('linc', (j <= l).astype(np.float32))
    put('ustr', (j > l).astype(np.float32))
    put('ones', np.ones((128, 128), np.float32))
    put('uincr', (j >= l).astype(np.float32))
    j16 = np.arange(16)[:, None]
    l16 = np.arange(16)[None, :]
    same = (j16 // 4 == l16 // 4)
    put('lincS', (same & (j16 <= l16)).astype(np.float32))
    put('ustrS', (same & (j16 > l16)).astype(np.float32))
    put('onesS', same.astype(np.float32))
    selB = np.zeros((16, 4, 128), np.float32)
    for b in range(4):
        selB[4 * b:4 * b + 4, b, :] = 1.0
    put('selB', selB.reshape(16, 512))
    rowB = np.zeros((16, 4), np.float32)
    for b in range(4):
        rowB[4 * b:4 * b + 4, b] = 1.0
    put('rowB', rowB)
    colB = np.zeros((128, 4, 16), np.float32)
    for b in range(4):
        colB[:, b, 4 * b:4 * b + 4] = 1.0
    put('colB', colB.reshape(128, 64))
    qq = np.tile(np.arange(4), 2)[None, :]
    t4 = np.arange(128)[:, None]
    put('mnsb', ((t4 < qq) & (t4 < 4)).astype(np.float32))
    put('mnfx', ((t4 <= qq) & (t4 < 4)).astype(np.float32))
    put('iota', np.arange(128, dtype=np.float32)[:, None])
    return c


class Bag:
    pass


class _Stop(Exception):
    pass


def build(cfg, stop_at=None):
    T, NB, NPG, NPOOL, DEPTH = cfg['T'], cfg['NB'], cfg['NPG'], cfg['NPOOL'], cfg['DEPTH']
    NS = NB * 4
    NT = T // 128
    NG = T // 512
    woff, wrows = wlayout()

    nc = bass.Bass("TRN2", target_bir_lowering=False)
    P = Prog(nc)

    def din(name, shape, dt=F32):
        return nc.dram_tensor(name, shape, dt, kind="ExternalInput").ap()

    def dout(name, shape, dt=F32):
        return nc.dram_tensor(name, shape, dt, kind="ExternalOutput").ap()

    xp_d = din("xp", [T, D])
    xs_d = din("xs", [NS, D])
    ckk_d = din("ckk", [DEPTH * NPOOL, 128, 512])
    cvv_d = din("cvv", [DEPTH * NPOOL, 128, 512])
    clf_d = din("clf", [DEPTH, NPOOL, 512])
    sst_d = din("sst", [DEPTH, NB, 8, 64, 128])
    scv_d = din("scv", [DEPTH, 128, 8, NB, 3])
    ptab_d = din("ptab", [1, NB * NPG], I32)
    wall_d = din("wall", [DEPTH, wrows, 1024])
    par_d = din("par", [DEPTH, 128, NPAR])
    cst_d = din("cst", [128, NCST])
    cmask_d = din("cmask", [128, 4 * 513])
    wscr = nc.dram_tensor("wscr", [DEPTH, wrows, 1024], BF16, kind="Internal").ap()
    xscr = nc.dram_tensor("xscr", [NG + 1, 128, 8 * 512], F32, kind="Internal").ap()

    yp_o = dout("yp", [T, D])
    ys_o = dout("ys", [NS, D])
    psbk_o = dout("psbk", [DEPTH, T, 256])
    psbv_o = dout("psbv", [DEPTH, T, 256])
    pfxk_o = dout("pfxk", [DEPTH, T, 256])
    pfxv_o = dout("pfxv", [DEPTH, T, 256])
    plf_o = dout("plf", [DEPTH, T, 4])
    pssm_o = dout("pssm", [DEPTH, 8, 64, 128])
    pcv_o = dout("pcv", [DEPTH, 3, 1024])
    ssbk_o = dout("ssbk", [DEPTH, NS, 256])
    ssbv_o = dout("ssbv", [DEPTH, NS, 256])
    sfxk_o = dout("sfxk", [DEPTH, NS, 256])
    sfxv_o = dout("sfxv", [DEPTH, NS, 256])
    slf_o = dout("slf", [DEPTH, NS, 4])
    sssm_o = dout("sssm", [DEPTH, NB, 8, 64, 128])
    scv_o = dout("scvo", [DEPTH, NB, 3, 1024])

    uid = {'n': 0}

    open_scopes = []

    def ck(name):
        if stop_at is not None and name == stop_at:
            raise _Stop()

    class Scope:
        def __init__(self):
            self.es = ExitStack()
            open_scopes.append(self)

        def sb(self, name, shape, dt=F32):
            uid['n'] += 1
            return self.es.enter_context(nc.sbuf_tensor(f"{name}_{uid['n']}", shape, dt))

        def close(self):
            P.barrier()
            self.es.close()
            open_scopes.remove(self)

    G = Scope()
    sb = G.sb

    def mm(out, lhsT, rhs, start, stop, r, w):
        P.op('pe', lambda e: e.matmul(out, lhsT, rhs, start=start, stop=stop), r, w)

    def tr(out, in_, ident, r, w):
        P.op('pe', lambda e: e.transpose(out, in_, ident), r, w)

    def act(out, in_, func, r, w, bias=None, scale=None, accum=None):
        kw = {}
        if bias is not None:
            kw['bias'] = bias
        if scale is not None:
            kw['scale'] = scale
        if accum is not None:
            kw['accum_out'] = accum
        P.op('act', lambda e: e.activation(out, in_, func, **kw), r, w)

    def tt(out, a, b, op, r, w, eng='dve'):
        P.op(eng, lambda e: e.tensor_tensor(out, a, b, op), r, w)

    def ts(out, a, s1, s2, op0, op1, r, w, eng='dve'):
        if op1 is None:
            P.op(eng, lambda e: e.tensor_scalar(out, a, s1, None, op0), r, w)
        else:
            P.op(eng, lambda e: e.tensor_scalar(out, a, s1, s2, op0, op1), r, w)

    def stt(out, a, s, b, op0, op1, r, w):
        P.op('dve', lambda e: e.scalar_tensor_tensor(out, a, s, b, op0, op1), r, w)

    def cp(out, in_, r, w, eng='dve'):
        if eng == 'act':
            P.op('act', lambda e: e.copy(out, in_), r, w)
        else:
            P.op(eng, lambda e: e.tensor_copy(out, in_), r, w)

    def memset(ap, v, w, eng='dve'):
        P.op(eng, lambda e: e.memset(ap, v), (), w)

    def dma(out, in_, r, w, q='sp'):
        P.dma(q, lambda e: e.dma_start(out=out, in_=in_), r, w)

    def scan(out, d0, d1, init, r, w):
        P.op('dve', lambda e: e.tensor_tensor_scan(out, d0, d1, init, ALU.mult, ALU.add), r, w)

    def recip(out, in_, r, w):
        P.op('dve', lambda e: e.reciprocal(out, in_), r, w)

    pes = ExitStack()
    banks = [pes.enter_context(nc.psum_tensor(f"ps{i}", [128, 512], F32)) for i in range(8)]
    bstate = {'i': 0}

    reserved = set()

    def bank(hold=False):
        for _ in range(8):
            i = bstate['i']
            bstate['i'] = (i + 1) % 8
            if i not in reserved:
                break
        else:
            raise RuntimeError("all PSUM banks reserved")
        if hold:
            reserved.add(i)
        return banks[i], f'ps{i}'

    def release(*names):
        for n_ in names:
            reserved.discard(int(n_[2:]))

    Fp = [sb(f"F{i}", [128, 512]) for i in range(8)]
    Hp = [sb(f"H{i}", [128, 512], BF16) for i in range(8)]

    def F(i):
        return Fp[i], f'F{i}'

    def H(i):
        return Hp[i], f'H{i}'

    stg = {'i': 0}

    def next_stage():
        i = 6 + stg['i']
        stg['i'] = 1 - stg['i']
        return Fp[i], f'F{i}'

    cst = sb("cst", [128, NCST])
    dma(cst[:], cst_d[:, :], (), ['cst'])

    def C(n, rows=128, c0=0, c1=None):
        o, w = CST[n]
        if c1 is None:
            c1 = w
        return cst[0:rows, o + c0:o + c1]

    onesb = sb("onesb", [128, 128], BF16)
    nuincb = sb("nuincb", [128, 128], BF16)
    nonesb = sb("nonesb", [128, 128], BF16)
    mext = sb("mext", [128, 4, 513], BF16)
    cp(onesb[:], C('ones'), ['cst'], ['cb16'])
    ts(nuincb[:], C('uincr'), -1.0, None, ALU.mult, None, ['cst'], ['cb16'])
    ts(nonesb[:], C('ones'), -1.0, None, ALU.mult, None, ['cst'], ['cb16'])
    for i in range(4):
        f_, fn = F(i)
        dma(f_[:, 0:512], cmask_d[:, i * 513:i * 513 + 512], (), [fn])
        cp(mext[:, i, 0:512], f_[:, 0:512], [fn], ['cb16'])
        memset(mext[:, i, 512:513], 1.0, ['cb16'])
    CB = ['cst', 'cb16']
    ident = C('ident')

    def msb(di):
        return mext[:, di, 0:512]

    def mfx(di):
        return mext[:, di, 1:513]

    epst = sb("epst", [128, 1])
    one_t = sb("one_t", [128, 1])
    ones_f = sb("ones_f", [128, 128])
    memset(epst[:], EPS, ['cb16'])
    memset(one_t[:], 1.0, ['cb16'])
    memset(ones_f[:], 1.0, ['cb16'])
    onesT = sb("onesT", [4, 512])
    sel4 = sb("sel4", [4, 4, 128])
    memset(onesT[:], 1.0, ['cb16'])
    for h in range(4):
        cp(sel4[:, h, :], C('ident', rows=4, c0=h, c1=h + 1).to_broadcast([4, 128]), ['cst'], ['cb16'])

    par = sb("par", [128, NPAR])
    arep = sb("arep", [128, 8])
    NBF = sb("NBF", [128, 1])

    def PR(n, rows=128, c0=0, c1=None):
        o, w = PAR[n]
        if c1 is None:
            c1 = w
        return par[0:rows, o + c0:o + c1]

    def cast_layer(l):
        for ch in range(wrows // 1024):
            P.dma('pool', lambda e, l=l, ch=ch: e.dma_start(out=wscr[l, ch * 1024:(ch + 1) * 1024, :],
                                                         in_=wall_d[l, ch * 1024:(ch + 1) * 1024, :]),
                  (), [f'scr{l}.{ch}'])

    wbig = [sb(f"wbig{i}", [128, 4096], BF16) for i in range(2)]
    wsm = [sb(f"wsm{i}", [128, 1024], BF16) for i in range(4)]
    wstate = {'b': 0, 's': 0}

    def load_w(l, name):
        r0, E = woff[name]
        if E > 1024:
            i = wstate['b']
            wstate['b'] = (i + 1) % 2
            tile_, res = wbig[i], f'wb{i}'
        else:
            i = wstate['s']
            wstate['s'] = (i + 1) % 4
            tile_, res = wsm[i], f'ws{i}'
        nr = 128 * E // 1024
        chs = sorted(set([r0 // 1024, (r0 + nr - 1) // 1024]))
        src = wscr[l, r0:r0 + nr, :].rearrange("r c -> (r c)").rearrange("(p e) -> p e", p=128)
        dma(tile_[:, 0:E], src, [f'scr{l}.{c}' for c in chs], [res])
        return tile_, res

    xg = sb("xg", [128, 8, 512])
    hT = sb("hT", [128, 8, 512], BF16)
    rstd = sb("rstd", [128, 512])
    carryF = sb("carryF", [128, 4])
    carryFT = sb("carryFT", [4, 1])
    ccar = sb("ccar", [128, 8, 3])
    hst = sb("hst", [128, 8, 64])
    hstb = sb("hstb", [128, 8, 64], BF16)
    lfT = sb("lfT", [4, 512])
    cumT = sb("cumT", [4, 512])
    tmp8 = sb("tmp8", [128, 16])
    lf_tok = sb("lf_tok", [128, 4])
    idx_t = sb("idx_t", [128, NB * NPG], I32)
    ptr_t = sb("ptr_t", [128, NB * NPG], I32)
    iota_i = sb("iota_i", [128, 1], I32)
    pcol = sb("pcol", [64, NB], I32)

    dma(ptr_t[:], ptab_d[0:1, :].partition_broadcast(128), (), ['ptr_t'])
    cp(iota_i[:], C('iota'), ['cst'], ['iota_i'])
    ts(idx_t[:], ptr_t[:], 128, iota_i[:, 0:1], ALU.mult, ALU.add, ['ptr_t', 'iota_i'], ['idx_t'], eng='pool')
    for b in range(NB):
        dma(pcol[0:NPG, b:b + 1], ptab_d[0:1, b * NPG:(b + 1) * NPG].rearrange("a j -> j a"), (), ['pcol'])

    def load_x(src_rows, ntok, c0):
        for half in range(2):
            st_, sn = next_stage()
            dma(st_[0:ntok, :], src_rows[:, half * 512:(half + 1) * 512], (), [sn])
            pb, pn = bank()
            for k in range(4):
                tr(pb[:, k * 128:k * 128 + ntok], st_[0:ntok, k * 128:(k + 1) * 128], C('ident', rows=ntok, c1=ntok),
                   [sn, 'cst'], [pn])
            outv = xg[:, half * 4:half * 4 + 4, c0:c0 + ntok]
            inv = pb[:, :].rearrange("p (k t) -> p k t", k=4)[:, :, 0:ntok]
            cp(outv, inv, [pn], ['xg'], eng=('dve' if half == 0 else 'act'))

    def rms_to_hT(n, gname, out_f32=None, ores='hT'):
        pb, pn = bank()
        for kc in range(8):
            s, sn = H(kc % 2)
            act(s[:, 0:n], xg[:, kc, 0:n], AF.Square, ['xg'], [sn])
            mm(pb[:, 0:n], onesb[:], s[:, 0:n], kc == 0, kc == 7, [sn] + CB, [pn])
        act(rstd[:, 0:n], pb[:, 0:n], AF.Ln, [pn, 'cb16'], ['rstd'], bias=epst[:, 0:1], scale=1.0 / D)
        act(rstd[:, 0:n], rstd[:, 0:n], AF.Exp, ['rstd'], ['rstd'], scale=-0.5)
        for kc in range(8):
            o = hT[:, kc, 0:n] if out_f32 is None else out_f32[:, kc, 0:n]
            stt(o, xg[:, kc, 0:n], PR(gname, c0=kc, c1=kc + 1), rstd[:, 0:n], ALU.mult, ALU.mult,
                ['xg', 'rstd', 'par'], [ores])

    def ssd_tile(l, A, L, ti, ntok, sample):
        cs = slice(ti * 128, ti * 128 + ntok)
        linc = C('lincS', rows=16) if sample else C('linc')
        ustr = C('ustrS', rows=16) if sample else C('ustr')
        ones_ = C('onesS', rows=16) if sample else C('ones')
        idn = C('ident', rows=ntok, c1=ntok)
        if sample:
            xs_tok, xsn = F(0)
            yo_sb, yon = F(1)
            yv, yvn = F(2)
            y2, y2n = F(3)
            dec, decn = F(4)
            xdt, xdtn = H(2)
            xw, xwn = H(3)
        else:
            (xs_tok, xsn), (yo_sb, yon), (yv, yvn), (y2, y2n), (dec, decn) = [(A.sF[i], f'sF{i}') for i in range(5)]
            (xdt, xdtn), (xw, xwn) = [(A.sH[i], f'sH{i}') for i in range(2)]
        pa, pan = bank()
        for c in range(4):
            tr(pa[0:ntok, c * 128:(c + 1) * 128], A.xsT[:, c, cs], ident, ['xsT', 'cst'], [pan])
        cp(xs_tok[0:ntok, :], pa[0:ntok, :], [pan], [xsn])
        pb_, pbn = bank()
        for g2 in range(2):
            tr(pb_[0:ntok, g2 * 128:(g2 + 1) * 128], A.BTf[:, g2, cs], ident, ['BT', 'cst'], [pbn])
        cp(A.B_tok[0:ntok, :], pb_[0:ntok, 0:256], [pbn], ['B_tok'], eng='act')
        yield
        tt(A.dta[0:ntok, :], A.dts[0:ntok, ti, :], arep[0:ntok, :], ALU.mult, ['dts', 'par2'], ['dta'])
        pc, pcn = bank()
        mm(pc[0:ntok, 0:8], linc, A.dta[0:ntok, :], True, True, ['dta', 'cst'], [pcn])
        mm(pc[0:ntok, 8:16], ones_, A.dta[0:ntok, :], True, True, ['dta', 'cst'], [pcn])
        cp(A.acs[0:ntok, :], pc[0:ntok, 0:8], [pcn], ['acs'], eng='act')
        act(A.ea[0:ntok, :], pc[0:ntok, 0:8], AF.Exp, [pcn], ['ea'])
        tt(A.te[0:ntok, :], pc[0:ntok, 8:16], A.acs[0:ntok, :], ALU.subtract, [pcn, 'acs'], ['te'])
        act(A.te[0:ntok, :], A.te[0:ntok, :], AF.Exp, ['te'], ['te'])
        tt(A.wdt[0:ntok, :], A.te[0:ntok, :], A.dts[0:ntok, ti, :], ALU.mult, ['te', 'dts'], ['wdt'])
        if not sample:
            act(A.cd[:, :], pc[:, 8:16], AF.Exp, [pcn], ['cd'])
        yield
        xs3 = xs_tok[0:ntok, :].rearrange("p (h d) -> p h d", h=8)
        tt(xdt[0:ntok, :].rearrange("p (h d) -> p h d", h=8), xs3,
           A.dts[0:ntok, ti, :].unsqueeze(2).to_broadcast([ntok, 8, 64]), ALU.mult, [xsn, 'dts'], [xdtn])
        tt(xw[0:ntok, :].rearrange("p (h d) -> p h d", h=8), xs3,
           A.wdt[0:ntok, :].unsqueeze(2).to_broadcast([ntok, 8, 64]), ALU.mult, [xsn, 'wdt'], [xwn])
        psc, pscn = bank()
        for g2 in range(2):
            mm(psc[0:ntok, g2 * 128:g2 * 128 + ntok], A.BTb[:, g2, cs], A.CTb[:, g2, cs], True, True, ['BT', 'CT'], [pscn])
        yield
        for g2 in range(2):
            tt(A.scm[0:ntok, g2, 0:ntok], psc[0:ntok, g2 * 128:g2 * 128 + ntok], linc, ALU.mult, [pscn, 'cst'], ['scm'])
        pY, pYn = bank(True)
        pYo, pYon = bank(True)
        if not sample:
            pSt, pStn = bank(True)
        for g2 in range(2):
            pseg, psegn = bank()
            for hh in range(4):
                h = g2 * 4 + hh
                ld = A.Ld[hh % 2]
                ts(ld[0:ntok, 0:ntok], ustr, A.dta[0:ntok, h:h + 1], None, ALU.mult, None, ['cst', 'dta'], [f'Ld{hh % 2}'])
                mm(pseg[0:ntok, hh * 128:hh * 128 + ntok], ld[0:ntok, 0:ntok], linc, True, True,
                   [f'Ld{hh % 2}', 'cst'], [psegn])
            yield
            segv = pseg[0:ntok, :].rearrange("p (h s) -> p h s", h=4)[:, :, 0:ntok]
            decv = dec[0:ntok, :].rearrange("p (h s) -> p h s", h=4)[:, :, 0:ntok]
            act(decv, segv, AF.Exp, [psegn], [decn])
            tt(A.MT[0:ntok, :, 0:ntok], decv, A.scm[0:ntok, g2, 0:ntok].unsqueeze(1).to_broadcast([ntok, 4, ntok]),
               ALU.mult, [decn, 'scm'], ['MT'])
            yield
            for hh in range(4):
                h = g2 * 4 + hh
                hsl = slice(h * 64, (h + 1) * 64)
                mm(pY[0:ntok, hsl], A.MT[0:ntok, hh, 0:ntok], xdt[0:ntok, hsl], True, True, ['MT', xdtn], [pYn])
                if not sample:
                    mm(pYo[0:ntok, hsl], A.CTb[:, g2, cs], hstb[:, h, :], True, True, ['CT', 'hstb'], [pYon])
                    mm(pSt[:, hsl], A.B_tok[0:ntok, g2 * 128:(g2 + 1) * 128], xw[0:ntok, hsl], True, True,
                       ['B_tok', xwn], [pStn])
        if sample:
            sample_state(l, A, L, pYo, pYon, xw, xwn)
        yield
        tt(yo_sb[0:ntok, :].rearrange("p (h d) -> p h d", h=8), pYo[0:ntok, :].rearrange("p (h d) -> p h d", h=8),
           A.ea[0:ntok, :].unsqueeze(2).to_broadcast([ntok, 8, 64]), ALU.mult, [pYon, 'ea'], [yon])
        tt(yv[0:ntok, :], pY[0:ntok, :], yo_sb[0:ntok, :], ALU.add, [pYn, yon], [yvn])
        release(pYn, pYon)
        tt(y2[0:ntok, :].rearrange("p (h d) -> p h d", h=8), xs3,
           PR('dsk', rows=ntok).unsqueeze(2).to_broadcast([ntok, 8, 64]), ALU.mult, [xsn, 'par'], [y2n])
        tt(yv[0:ntok, :], yv[0:ntok, :], y2[0:ntok, :], ALU.add, [yvn, y2n], [yvn])
        tt(yv[0:ntok, :], yv[0:ntok, :], A.zs[0:ntok, ti, :], ALU.mult, [yvn, 'zs'], [yvn])
        for g2 in range(2):
            act(y2[0:ntok, g2 * 256:(g2 + 1) * 256], yv[0:ntok, g2 * 256:(g2 + 1) * 256], AF.Square, [yvn], [y2n, 'ssq'],
                accum=A.ssq[0:ntok, g2:g2 + 1])
        act(A.rs2[0:ntok, :], A.ssq[0:ntok, :], AF.Ln, ['ssq', 'cb16'], ['rs2'], bias=epst[0:ntok, 0:1], scale=1.0 / 256)
        act(A.rs2[0:ntok, :], A.rs2[0:ntok, :], AF.Exp, ['rs2'], ['rs2'], scale=-0.5)
        for g2 in range(2):
            gsl = slice(g2 * 256, (g2 + 1) * 256)
            stt(y2[0:ntok, gsl], yv[0:ntok, gsl], A.rs2[0:ntok, g2:g2 + 1], PR('gn', rows=ntok, c0=g2 * 256, c1=(g2 + 1) * 256),
                ALU.mult, ALU.mult, [yvn, 'rs2', 'par', y2n], [y2n])
        yield
        pT, pTn = bank()
        for c in range(4):
            tr(pT[:, c * 128:c * 128 + ntok], y2[0:ntok, c * 128:(c + 1) * 128], idn, [y2n, 'cst'], [pTn])
        yield
        cp(A.yssT[:, :, cs], pT[:, :].rearrange("p (c t) -> p c t", c=4)[:, :, 0:ntok], [pTn], ['yssT'], eng='act')
        if not sample:
            for h in range(8):
                stt(hst[:, h, :], hst[:, h, :], A.cd[:, h:h + 1], pSt[:, h * 64:(h + 1) * 64], ALU.mult, ALU.add,
                    ['hst', 'cd', pStn], ['hst'])
            cp(hstb[:].rearrange("p a b -> p (a b)"), hst[:].rearrange("p a b -> p (a b)"), ['hst'], ['hstb'])
            release(pStn)

    def sample_state(l, A, L, pYo, pYon, xw, xwn):
        for b in range(NB):
            tt(L.CTm[:, b, :, :], A.CTb[:, :, 0:16], C('colB', c0=b * 16, c1=(b + 1) * 16).unsqueeze(1).to_broadcast([128, 2, 16]),
               ALU.mult, ['CT', 'cst'], ['CTm'])
        pc, pcn = bank()
        for b in range(NB):
            mm(pc[:, b * 8:(b + 1) * 8], C('selB', rows=16, c0=b * 128, c1=(b + 1) * 128), A.dta[0:16, :], True, True,
               ['dta', 'cst'], [pcn])
        act(L.cdS[:].rearrange("p a b -> p (a b)"), pc[:, 0:NB * 8], AF.Exp, [pcn], ['cdS'])
        for b in range(NB):
            h0 = L.h0[0]
            h0n = 'h0_0'
            dma(h0[:, :, :], sst_d[l, b].rearrange("h p n -> p h n"), (), [h0n])
            for half in range(2):
                pb, pn = bank()
                for k in range(4):
                    h = half * 4 + k
                    tr(pb[:, k * 64:(k + 1) * 64], h0[:, h, :], C('ident', rows=64, c1=64), [h0n, 'cst'], [pn])
                cp(L.h0Tb[:, b, half * 4:half * 4 + 4, :].rearrange("p a b -> p (a b)"), pb[:, 0:256], [pn], ['h0Tb'])
            ts(L.xwm[:, :], xw[0:16, :], C('rowB', rows=16, c0=b, c1=b + 1), None, ALU.mult, None, [xwn, 'cst'], ['xwm'])
            for half in range(2):
                st_, sn = next_stage()
                pb, pn = bank()
                for k in range(4):
                    h = half * 4 + k
                    g2 = h // 4
                    mm(pb[0:64, k * 128:(k + 1) * 128], L.xwm[:, h * 64:(h + 1) * 64], A.B_tok[0:16, g2 * 128:(g2 + 1) * 128],
                       True, True, ['xwm', 'B_tok'], [pn])
                for k in range(4):
                    h = half * 4 + k
                    stt(st_[0:64, k * 128:(k + 1) * 128], h0[:, h, :], L.cdS[0:64, b, h:h + 1], pb[0:64, k * 128:(k + 1) * 128],
                        ALU.mult, ALU.add, [h0n, 'cdS', pn], [sn])
                dma(sssm_o[l, b, half * 4:half * 4 + 4].rearrange("h p n -> p h n"),
                    st_[0:64, :].rearrange("p (h n) -> p h n", h=4), [sn], [])
        for h in range(8):
            g2 = h // 4
            for b in range(NB):
                mm(pYo[0:16, h * 64:(h + 1) * 64], L.CTm[:, b, g2, :], L.h0Tb[:, b, h, :], b == 0, b == NB - 1,
                   ['CTm', 'h0Tb'], [pYon])

    def proj_group(l, A, L, gi, n, tiles, sample):
        rms_to_hT(n, 'g1')
        ck('p_rms')
        c0 = gi * 512
        for which, base in (('sb', 0), ('fx', 4)):
            for j in range(2):
                for kind, cbi in (('q', base + j), ('k', base + 2 + j)):
                    wt, wn = load_w(l, f'fm{cbi}')
                    pb, pn = bank()
                    for kc in range(8):
                        mm(pb[:, 0:n], wt[:, kc * 128:(kc + 1) * 128], hT[:, kc, 0:n], kc == 0, kc == 7, [wn, 'hT'], [pn])
                    if kind == 'q':
                        act(A.QT[which][:, j, 0:n], pb[:, 0:n], AF.Copy, [pn], [f'QT{which}'], scale=0.125)
                    elif sample:
                        cp(L.KTs[which][:, j, 0:n], pb[:, 0:n], [pn], [f'KTs{which}'])
                    else:
                        cp(L.KT[which][:, j, c0:c0 + n], pb[:, 0:n], [pn], [f'KT{which}{gi}'])
        ck('p_qk')
        if not sample:
            wt, wn = load_w(l, 'f4')
            pb, pn = bank()
            for kc in range(8):
                mm(pb[0:4, 0:n], wt[:, kc * 4:(kc + 1) * 4], hT[:, kc, 0:n], kc == 0, kc == 7, [wn, 'hT'], [pn])
            ck('f4a')
            act(lfT[:, 0:n], pb[0:4, 0:n], AF.Exp, [pn, 'par2'], ['lfT'], bias=NBF[0:4, 0:1], scale=-1.0)
            act(lfT[:, 0:n], lfT[:, 0:n], AF.Ln, ['lfT', 'cb16'], ['lfT'], bias=one_t[0:4, 0:1])
            ts(lfT[:, 0:n], lfT[:, 0:n], -1.0, None, ALU.mult, None, ['lfT'], ['lfT'])
            ck('f4b')
            scan(cumT[:, 0:n], onesT[:, 0:n], lfT[:, 0:n], carryFT[:, 0:1], ['lfT', 'cb16', 'carryFT'], ['cumT'])
            ck('f4c')
            cp(carryFT[:, 0:1], cumT[:, n - 1:n], ['cumT'], ['carryFT'])
        ck('p_f4')
        need_tm = sample or (gi == NG - 1)
        if need_tm:
            ntk = 16 if sample else 128
            tcs = slice(0, 16) if sample else slice(384, 512)
            ptm = [bank(True), bank(True)]
        for c in range(8):
            wt, wn = load_w(l, f'fm{8 + c}')
            pb, pn = bank()
            for kc in range(8):
                mm(pb[:, 0:n], wt[:, kc * 128:(kc + 1) * 128], hT[:, kc, 0:n], kc == 0, kc == 7, [wn, 'hT'], [pn])
            if need_tm:
                tb, tbn = ptm[c // 4]
                for kc in range(8):
                    mm(tb[0:ntk, (c % 4) * 128:(c % 4 + 1) * 128], hT[:, kc, tcs], wt[:, kc * 128:(kc + 1) * 128],
                       kc == 0, kc == 7, [wn, 'hT'], [tbn])
            acc, accn = F(5)
            if sample:
                cp(L.XS[:, c, :, 3:7], pb[:, 0:16].rearrange("p (b t) -> p b t", b=NB), [pn], ['XS'])
                xv = lambda i: L.XS[:, c, :, i:i + 4]
                accv = acc[:, 0:16].rearrange("p (b t) -> p b t", b=NB)
                xres = 'XS'
            else:
                Xb = A.Xb[c % 2]
                xres = f'Xb{c % 2}'
                cp(Xb[:, 0:3], ccar[:, c, :], ['ccar'], [xres])
                cp(Xb[:, 3:3 + n], pb[:, 0:n], [pn], [xres], eng='act')
                cp(ccar[:, c, :], Xb[:, n:n + 3], [xres], ['ccar'])
                xv = lambda i: Xb[:, i:i + n]
                accv = acc[:, 0:n]
            ts(accv, xv(0), PR('cw', c0=c * 4, c1=c * 4 + 1), PR('cb', c0=c, c1=c + 1), ALU.mult, ALU.add,
               [xres, 'par'], [accn])
            for i in range(1, 4):
                stt(accv, xv(i), PR('cw', c0=c * 4 + i, c1=c * 4 + i + 1), accv, ALU.mult, ALU.add,
                    [xres, 'par', accn], [accn])
            if c < 4:
                act(A.xsT[:, c, 0:n], acc[:, 0:n], AF.Silu, [accn], ['xsT'])
            elif c < 6:
                act(A.BTf[:, c - 4, 0:n], acc[:, 0:n], AF.Silu, [accn], ['BT'])
                cp(A.BTb[:, c - 4, 0:n], A.BTf[:, c - 4, 0:n], ['BT'], ['BT'])
            else:
                act(A.CTb[:, c - 6, 0:n], acc[:, 0:n], AF.Silu, [accn], ['CT'])
        if need_tm:
            for half in range(2):
                st_, sn = next_stage()
                cp(st_[0:ntk, :], ptm[half][0][0:ntk, :], [ptm[half][1]], [sn], eng=('act' if half else 'dve'))
                if sample:
                    for b in range(NB):
                        dma(scv_o[l, b, :, half * 512:(half + 1) * 512], st_[4 * b + 1:4 * b + 4, :], [sn], [])
                else:
                    dma(pcv_o[l, :, half * 512:(half + 1) * 512], st_[125:128, :], [sn], [])
                release(ptm[half][1])
        ck('p_xbc')
        ntok = 16 if sample else 128
        for piece in ('tmD', 'tmC', 'tmA', 'tmB'):
            wt, wn = load_w(l, piece)
            W = {'tmA': 512, 'tmB': 512, 'tmC': 512, 'tmD': 12}[piece]
            for ti, tix in enumerate(tiles):
                cs = slice(ti * 128, ti * 128 + ntok)
                pb, pn = bank()
                for kc in range(8):
                    mm(pb[0:ntok, 0:W], hT[:, kc, cs], wt[:, kc * W:(kc + 1) * W], kc == 0, kc == 7, [wn, 'hT'], [pn])
                r0 = (tix * 128) if not sample else 0
                if piece in ('tmA', 'tmB'):
                    which = 'sb' if piece == 'tmA' else 'fx'
                    st_, sn = next_stage()
                    cp(st_[0:ntok, 0:512], pb[0:ntok, :], [pn], [sn])
                    if sample:
                        ko, vo = (ssbk_o, ssbv_o) if which == 'sb' else (sfxk_o, sfxv_o)
                        cp(L.Vnew[which][0:16, :], pb[0:16, 256:512], [pn], [f'Vnew{which}'], eng='act')
                    else:
                        ko, vo = (psbk_o, psbv_o) if which == 'sb' else (pfxk_o, pfxv_o)
                        cp(L.V[which][:, tix, :], pb[:, 256:512], [pn], [f'V{which}{tix}'], eng='act')
                    dma(ko[l, r0:r0 + ntok, :], st_[0:ntok, 0:256], [sn], [])
                    dma(vo[l, r0:r0 + ntok, :], st_[0:ntok, 256:512], [sn], [])
                elif piece == 'tmC':
                    act(A.zs[0:ntok, ti, :], pb[0:ntok, :], AF.Silu, [pn], ['zs'])
                else:
                    tt(tmp8[0:ntok, 0:4], pb[0:ntok, 0:4], PR('bf_rep', rows=ntok), ALU.add, [pn, 'par'], ['tmp8'])
                    act(tmp8[0:ntok, 0:4], tmp8[0:ntok, 0:4], AF.Exp, ['tmp8'], ['tmp8'], scale=-1.0)
                    act(tmp8[0:ntok, 0:4], tmp8[0:ntok, 0:4], AF.Ln, ['tmp8', 'cb16'], ['tmp8'], bias=one_t[0:ntok, 0:1])
                    ts(lf_tok[0:ntok, :], tmp8[0:ntok, 0:4], -1.0, None, ALU.mult, None, ['tmp8'], ['lf_tok'])
                    lo = slf_o if sample else plf_o
                    dma(lo[l, r0:r0 + ntok, :], lf_tok[0:ntok, :], ['lf_tok'], [])
                    if sample:
                        cp(L.lfS[0:16, :], lf_tok[0:16, :], ['lf_tok'], ['lfS'])
                    else:
                        pq, pqn = bank()
                        mm(pq[:, 0:4], C('linc'), lf_tok[:, :], True, True, ['lf_tok', 'cst'], [pqn])
                        mm(pq[:, 4:8], C('ones'), lf_tok[:, :], True, True, ['lf_tok', 'cst'], [pqn])
                        stt(L.negFk[:, tix, :], pq[:, 0:4], -1.0, carryF[:, :], ALU.mult, ALU.subtract, [pqn, 'carryF'],
                            [f'negFk{tix}'])
                        tt(carryF[:, :], carryF[:, :], pq[:, 4:8], ALU.add, [pqn, 'carryF'], ['carryF'])
                    tt(tmp8[0:ntok, 8:16], pb[0:ntok, 4:12], PR('dtb', rows=ntok), ALU.add, [pn, 'par'], ['tmp8'])
                    act(tmp8[0:ntok, 8:16], tmp8[0:ntok, 8:16], AF.Exp, ['tmp8'], ['tmp8'])
                    act(A.dts[0:ntok, ti, :], tmp8[0:ntok, 8:16], AF.Ln, ['tmp8', 'cb16'], ['dts'], bias=one_t[0:ntok, 0:1])
            ck('tm_' + piece)
        ck('p_tm')

        def ssd_all():
            for ti, tix in enumerate(tiles):
                yield from ssd_tile(l, A, L, ti, ntok, sample)
                yield
        if sample:
            for _ in ssd_all():
                pass
            return None
        return ssd_all()

    def attn_prompt(l, A, L, gi, bg=None):
        nkb = 4 * gi + 4
        n_iter = 4 * (nkb + 2) + 4 * (nkb + 1)
        per = max(1, -(-40 // n_iter))

        def tick():
            if bg is not None:
                for _ in range(per):
                    next(bg, None)
        spsum, spsn = F(2)
        spsumb, spsbn = H(0)
        for hc in range(2):
            pO, pOn = bank(True)
            for hp in range(2):
                h = hc * 2 + hp
                ps_ = slice(hp * 64, hp * 64 + 64)
                kbs = list(range(nkb - 1, -1, -1))
                st = {}

                def s0(idx):
                    kb = kbs[idx]
                    di = kb - 4 * gi
                    kres = f'KTsb{kb // 4}'
                    ksl = slice(kb * 128, (kb + 1) * 128)
                    pz, pzn = bank()
                    mm(pz[:, :], L.KT['sb'][ps_, hc, ksl], A.QT['sb'][ps_, hc, :], True, True, [kres, 'QTsb'], [pzn])
                    tb, tbn = F(idx % 2)
                    sp_, spn = H(1 + idx % 3)
                    act(tb[:], pz[:, :], AF.Exp, [pzn], [tbn])
                    act(sp_[:], tb[:], AF.Ln, [tbn, 'cb16'], [spn], bias=one_t[:, 0:1])
                    if di >= 0:
                        tt(sp_[:], sp_[:], msb(di), ALU.mult, [spn] + CB, [spn])
                    st[idx] = (kb, di, kres, ksl, sp_, spn)

                def s1(idx):
                    kb, di, kres, ksl, sp_, spn = st[idx]
                    w_, wn_ = H(4 + idx % 3)
                    pe_, pen = bank()
                    mm(pe_[:, :], L.KT['sb'][ps_, hc, ksl], A.QT['sb'][ps_, hc, :], True, False, [kres, 'QTsb'], [pen])
                    mm(pe_[:, :], nuincb[:], sp_[:], False, idx == 0, [spn] + CB, [pen])
                    if idx > 0:
                        mm(pe_[:, :], nonesb[:], spsumb[:], False, True, [spsbn] + CB, [pen])
                    act(w_[:], pe_[:, :], AF.Exp, [pen], [wn_])
                    if di >= 0:
                        tt(w_[:], w_[:], msb(di), ALU.mult, [wn_] + CB, [wn_])
                    if kb > 0:
                        if idx == 0:
                            cp(spsum[:], sp_[:], [spn], [spsn])
                        else:
                            tt(spsum[:], spsum[:], sp_[:], ALU.add, [spsn, spn], [spsn])
                        cp(spsumb[:], spsum[:], [spsn], [spsbn])
                    st[idx] = st[idx] + (w_, wn_)

                def s2(idx):
                    kb = st[idx][0]
                    w_, wn_ = st[idx][6], st[idx][7]
                    mm(pO[ps_, :], L.V['sb'][:, kb, h * 64:(h + 1) * 64], w_[:], idx == 0, kb == 0, [f'Vsb{kb}', wn_], [pOn])

                for it in range(nkb + 2):
                    tick()
                    if it < nkb:
                        s0(it)
                    if 0 <= it - 1 < nkb:
                        s1(it - 1)
                    if 0 <= it - 2 < nkb:
                        s2(it - 2)
            cp(A.yT['sb'][:, hc, :], pO[:, :], [pOn], ['ysbT'])
            release(pOn)
        fq, fqn = F(4)
        rec, recn = F(3)
        for hc in range(2):
            pN, pNn = bank(True)
            pD, pDn = bank(True)
            for hp in range(2):
                h = hc * 2 + hp
                ps_ = slice(hp * 64, hp * 64 + 64)
                pq, pqn = bank()
                mm(pq[:, :], sel4[:, h, :], cumT[:, :], True, True, ['cb16', 'cumT'], [pqn])
                cp(fq[:], pq[:, :], [pqn], [fqn], eng='act')
                st = {}

                def f0(kb):
                    di = kb - 4 * gi
                    kres = f'KTfx{kb // 4}'
                    ksl = slice(kb * 128, (kb + 1) * 128)
                    pz, pzn = bank()
                    mm(pz[:, :], L.KT['fx'][ps_, hc, ksl], A.QT['fx'][ps_, hc, :], True, True, [kres, 'QTfx'], [pzn])
                    tb, tbn = F(kb % 2)
                    w_, wn_ = H(4 + kb % 3)
                    stt(tb[:], pz[:, :], L.negFk[:, kb, h:h + 1], fq[:], ALU.add, ALU.add, [pzn, f'negFk{kb}', fqn], [tbn])
                    if di >= 0:
                        ts(tb[:], tb[:], 80.0, None, ALU.min, None, [tbn], [tbn])
                    act(w_[:], tb[:], AF.Exp, [tbn], [wn_])
                    if di >= 0:
                        tt(w_[:], w_[:], mfx(di), ALU.mult, [wn_] + CB, [wn_])
                    st[kb] = (w_, wn_)

                def f1(kb):
                    w_, wn_ = st[kb]
                    mm(pN[ps_, :], L.V['fx'][:, kb, h * 64:(h + 1) * 64], w_[:], kb == 0, kb == nkb - 1, [f'Vfx{kb}', wn_], [pNn])
                    mm(pD[ps_, :], onesb[:, 0:64], w_[:], kb == 0, kb == nkb - 1, [wn_] + CB, [pDn])

                for it in range(nkb + 1):
                    tick()
                    if it < nkb:
                        f0(it)
                    if 0 <= it - 1 < nkb:
                        f1(it - 1)
                recip(rec[ps_, :], pD[ps_, :], [pDn], [recn])
                tt(A.yT['fx'][ps_, hc, :], pN[ps_, :], rec[ps_, :], ALU.mult, [pNn, recn], ['yfxT'])
            release(pNn, pDn)

    def merge(l, A, n):
        for dc in range(8):
            pbr = []
            for bi, (nm, ysrc, nk, yres) in enumerate((('sbo', A.yT['sb'], 2, 'ysbT'), ('sso', A.yssT, 4, 'yssT'),
                                                        ('fxo', A.yT['fx'], 2, 'yfxT'))):
                wt, wn = load_w(l, f'fm{16 + bi * 8 + dc}')
                pg, pgn = bank()
                for kc in range(8):
                    mm(pg[:, 0:n], wt[:, kc * 128:(kc + 1) * 128], hT[:, kc, 0:n], kc == 0, kc == 7, [wn, 'hT'], [pgn])
                g_, gn_ = F(bi)
                act(g_[:, 0:n], pg[:, 0:n], AF.Sigmoid, [pgn], [gn_])
                wt, wn = load_w(l, f'{nm}{dc}')
                pb, pn = bank(True)
                for kc in range(nk):
                    mm(pb[:, 0:n], wt[:, kc * 128:(kc + 1) * 128], ysrc[:, kc, 0:n], kc == 0, kc == nk - 1, [wn, yres], [pn])
                pbr.append((pb, pn))
            ma, man = F(3)
            mb, mbn = F(4)
            tt(ma[:, 0:n], pbr[0][0][:, 0:n], Fp[0][:, 0:n], ALU.mult, [pbr[0][1], 'F0'], [man])
            tt(mb[:, 0:n], pbr[1][0][:, 0:n], Fp[1][:, 0:n], ALU.mult, [pbr[1][1], 'F1'], [mbn])
            tt(ma[:, 0:n], ma[:, 0:n], mb[:, 0:n], ALU.add, [man, mbn], [man])
            tt(mb[:, 0:n], pbr[2][0][:, 0:n], Fp[2][:, 0:n], ALU.mult, [pbr[2][1], 'F2'], [mbn])
            tt(A.mixT[:, dc, 0:n], ma[:, 0:n], mb[:, 0:n], ALU.add, [man, mbn], ['mixT'])
            release(*[x[1] for x in pbr])
        for dc in range(8):
            wt, wn = load_w(l, f'wo{dc}')
            pb, pn = bank()
            for kc in range(8):
                mm(pb[:, 0:n], wt[:, kc * 128:(kc + 1) * 128], A.mixT[:, kc, 0:n], kc == 0, kc == 7, [wn, 'mixT'], [pn])
            tt(xg[:, dc, 0:n], xg[:, dc, 0:n], pb[:, 0:n], ALU.add, ['xg', pn], ['xg'])

    def ffn(l, Dd, n):
        rms_to_hT(n, 'g2')
        for fc in range(32):
            wt, wn = load_w(l, f'up{fc}')
            pb, pn = bank()
            for kc in range(8):
                mm(pb[:, 0:n], wt[:, kc * 128:(kc + 1) * 128], hT[:, kc, 0:n], kc == 0, kc == 7, [wn, 'hT'], [pn])
            tb, tbn = F(fc % 2)
            act(tb[:, 0:n], pb[:, 0:n], AF.Relu, [pn], [tbn])
            tt(Dd.uT[:, fc, 0:n], tb[:, 0:n], tb[:, 0:n], ALU.mult, [tbn], ['uT'])
        for dc in range(8):
            wt, wn = load_w(l, f'dn{dc}')
            pb, pn = bank()
            for kc in range(32):
                mm(pb[:, 0:n], wt[:, kc * 128:(kc + 1) * 128], Dd.uT[:, kc, 0:n], kc == 0, kc == 31, [wn, 'uT'], [pn])
            tt(xg[:, dc, 0:n], xg[:, dc, 0:n], pb[:, 0:n], ALU.add, ['xg', pn], ['xg'])

    def final_out(Dd, n, tiles, sample):
        rms_to_hT(n, 'gf', out_f32=Dd.finT, ores='finT')
        ntok = 16 if sample else 128
        for ti, tix in enumerate(tiles):
            cs = slice(ti * 128, ti * 128 + ntok)
            for half in range(2):
                st_, sn = next_stage()
                pb, pn = bank()
                for k in range(4):
                    kc = half * 4 + k
                    tr(pb[0:ntok, k * 128:(k + 1) * 128], Dd.finT[:, kc, cs], ident, ['finT', 'cst'], [pn])
                cp(st_[0:ntok, :], pb[0:ntok, :], [pn], [sn], eng=('act' if half else 'dve'))
                if sample:
                    dma(ys_o[:, half * 512:(half + 1) * 512], st_[0:16, :], [sn], [])
                else:
                    dma(yp_o[tix * 128:(tix + 1) * 128, half * 512:(half + 1) * 512], st_[:, :], [sn], [])

    def gather(out_tile, res, src3d, l, b, j):
        col = b * NPG + j
        flat = src3d.rearrange("g r c -> (g r) c")
        P.dma('pool', lambda e: e.indirect_dma_start(out=out_tile[:], out_offset=None, in_=flat,
                                                     in_offset=bass.IndirectOffsetOnAxis(ap=idx_t[:, col:col + 1], axis=0),
                                                     element_offset=l * NPOOL * 128 * 512),
              ['idx_t'], [res])

    def attn_decode(l, A, L):
        NC_ = NPG * 8
        assert NC_ <= 512
        WH = ('sb', 'fx')
        for wi, which in enumerate(WH):
            memset(L.KTn[which][:].rearrange("p a b -> p (a b)"), 0.0, [f'KTn{which}'])
        memset(L.VnP[:], 0.0, ['VnP'])
        memset(L.cn4[:], 0.0, ['cn4'])
        S1, S2, S3 = L.S1, L.S2, L.S3
        Sn1, Sn2, Wn = L.Sn1, L.Sn2, L.Wn
        Sn1v = Sn1[:, :].rearrange("p (a b) -> p a b", a=2)
        Wnv = Wn[:, :].rearrange("p (a b) -> p a b", a=2)

        def jcol(which, j):
            return (NPG - 1 - j) if which == 'sb' else j
        for b in range(NB):
            qc = slice(b * 4, b * 4 + 4)
            for which in WH:
                Qb = L.Qbd[which]
                memset(Qb[:].rearrange("p a b -> p (a b)"), 0.0, [f'Qbd{which}'])
                for hc in range(2):
                    for hp in range(2):
                        ps_ = slice(hp * 64, hp * 64 + 64)
                        cp(Qb[ps_, hc, hp * 4:hp * 4 + 4], A.QT[which][ps_, hc, qc], [f'QT{which}', f'Qbd{which}'], [f'Qbd{which}'])
                    cp(L.KTn[which][:, hc, 0:4], L.KTs[which][:, hc, qc], [f'KTs{which}', f'KTn{which}'], [f'KTn{which}'])
            pS = {w_: [bank(True), bank(True)] for w_ in WH}
            for j in range(NPG):
                kt_, ktn = L.kpg[j % 4], f'kpg{j % 4}'
                gather(kt_, ktn, ckk_d, l, b, j)
                pt_, ptn = bank()
                for q4 in range(4):
                    tr(pt_[:, q4 * 128:(q4 + 1) * 128], kt_[:, q4 * 128:(q4 + 1) * 128], ident, [ktn, 'cst'], [ptn])
                kb_, kbn = L.ktb[j % 3], f'ktb{j % 3}'
                cp(kb_[:], pt_[:, :], [ptn], [kbn], eng=('act' if j % 2 else 'dve'))
                for wi, which in enumerate(WH):
                    jc = jcol(which, j)
                    for hc in range(2):
                        mm(pS[which][hc][0][:, jc * 8:(jc + 1) * 8], kb_[:, wi * 256 + hc * 128:wi * 256 + (hc + 1) * 128],
                           L.Qbd[which][:, hc, :], True, True, [kbn, f'Qbd{which}'], [pS[which][hc][1]])
            pSn, pSnn = bank(True)
            for wi, which in enumerate(WH):
                for hc in range(2):
                    mm(pSn[:, wi * 16 + hc * 8:wi * 16 + (hc + 1) * 8], L.KTn[which][:, hc, :], L.Qbd[which][:, hc, :], True, True,
                       [f'KTn{which}', f'Qbd{which}'], [pSnn])
            which = 'fx'
            mnew2 = C('mnfx').unsqueeze(1).to_broadcast([128, 2, 8])
            lfp, lfw, tot64, Fkd = L.lfp, L.lfw, L.tot64, L.Fkd
            P.dma('pool', lambda e, b=b: e.indirect_dma_start(
                out=lfp[0:NPG, :], out_offset=None, in_=clf_d.rearrange("l p c -> (l p) c"),
                in_offset=bass.IndirectOffsetOnAxis(ap=pcol[0:NPG, b:b + 1], axis=0),
                element_offset=l * NPOOL * 512), ['pcol'], ['lfp'])
            for h in range(4):
                colv = slice(h, 512, 4)
                scan(lfw[0:NPG, colv], ones_f[0:NPG, 0:128], lfp[0:NPG, colv], 0.0, ['lfp', 'cb16'], ['lfw'])
            cp(tot64[0:NPG, :], lfw[0:NPG, 508:512], ['lfw'], ['tot64'])
            pq, pqn = bank()
            mm(pq[0:NPG, 0:4], C('linc', rows=NPG, c1=NPG), tot64[0:NPG, :], True, True, ['tot64', 'cst'], [pqn])
            mm(pq[:, 8:12], C('ones', rows=NPG), tot64[0:NPG, :], True, True, ['tot64', 'cst'], [pqn])
            tt(tot64[0:NPG, :], pq[0:NPG, 0:4], tot64[0:NPG, :], ALU.subtract, [pqn, 'tot64'], ['tot64'])
            lfw3 = lfw[0:NPG, :].rearrange("p (r h) -> p r h", h=4)
            tt(lfw3, lfw3, tot64[0:NPG, :].unsqueeze(1).to_broadcast([NPG, 128, 4]), ALU.add, ['lfw', 'tot64'], ['lfw'])
            cp(L.ftot[:, :], pq[:, 8:12], [pqn], ['ftot'])
            for h in range(4):
                pt_, ptn = bank()
                tr(pt_[:, 0:NPG], lfw[0:NPG, h:512:4], C('ident', rows=NPG, c1=NPG), ['lfw', 'cst'], [ptn])
                ts(Fkd[:, 0:NPG, h], pt_[:, 0:NPG], -1.0, L.ftot[:, h:h + 1], ALU.mult, ALU.add, [ptn, 'ftot'], ['Fkd'])
            for hc in range(2):
                sl = slice(hc * NC_, (hc + 1) * NC_)
                o3 = S3[:, sl].rearrange("p (j a q) -> p j a q", a=2, q=4)
                i3 = pS[which][hc][0][:, 0:NC_].rearrange("p (j a q) -> p j a q", a=2, q=4)
                f3 = Fkd[:, 0:NPG, hc * 2:hc * 2 + 2].unsqueeze(3).to_broadcast([128, NPG, 2, 4])
                tt(o3, i3, f3, ALU.add, [pS[which][hc][1], 'Fkd'], ['S1'])
            release(pS['fx'][0][1], pS['fx'][1][1])
            act(S3[:, 0:2 * NC_], S3[:, 0:2 * NC_], AF.Exp, ['S1'], ['S1'])
            cp(L.S3b[which][:, 0:2 * NC_], S3[:, 0:2 * NC_], ['S1'], [f'S3b{which}'])
            pn_, pnn = bank()
            mm(pn_[0:16, 0:4], C('lincS', rows=16), L.lfS[0:16, :], True, True, ['lfS', 'cst'], [pnn])
            cp(L.cnS[0:16, :], pn_[0:16, 0:4], [pnn], ['cnS'])
            mm(pn_[0:4, 8:12], C('ident', rows=16, c0=b * 4, c1=b * 4 + 4), L.cnS[0:16, :], True, True, ['cnS', 'cst'], [pnn])
            cp(L.cn4[0:4, :], pn_[0:4, 8:12], [pnn, 'cn4'], ['cn4'])
            o3 = Wn[:, :].rearrange("p (h q) -> p h q", q=4)
            i3 = pSn[:, 16:32].rearrange("p (h q) -> p h q", q=4)
            tt(o3, i3, L.cn4[:, :].unsqueeze(2).to_broadcast([128, 4, 4]), ALU.subtract, [pSnn, 'cn4'], ['Wn'])
            act(Wn[:, :], Wn[:, :], AF.Exp, ['Wn'], ['Wn'])
            tt(Wnv, Wnv, mnew2, ALU.mult, ['Wn', 'cst'], ['Wn'])
            cp(L.Wnb[which][:], Wn[:, :], ['Wn'], [f'Wnb{which}'])
            for hc in range(2):
                sl = slice(hc * NC_, (hc + 1) * NC_)
                P.op('dve', lambda e, sl=sl, hc=hc: e.tensor_reduce(
                    L.ydec[:, hc, :], S3[:, sl].rearrange("p (j c) -> p c j", c=8), AX.X, ALU.add), ['S1'], ['ydec'])
                tt(L.ydec[:, hc, :], L.ydec[:, hc, :], Wn[:, hc * 8:(hc + 1) * 8], ALU.add, ['ydec', 'Wn'], ['ydec'])
                pd_, pdn = bank()
                mm(pd_[:, 0:8], C('ones'), L.ydec[:, hc, :], True, True, ['ydec', 'cst'], [pdn])
                recip(L.rden[:, hc, :], pd_[:, 0:8], [pdn], ['rden'])
            which = 'sb'
            mnew2 = C('mnsb').unsqueeze(1).to_broadcast([128, 2, 8])
            for hc in range(2):
                act(S1[:, hc * NC_:(hc + 1) * NC_], pS[which][hc][0][:, 0:NC_], AF.Exp, [pS[which][hc][1]], ['S1'])
            act(S1[:, 0:2 * NC_], S1[:, 0:2 * NC_], AF.Ln, ['S1', 'cb16'], ['S1'], bias=one_t[:, 0:1])
            act(Sn1[:, :], pSn[:, 0:16], AF.Exp, [pSnn], ['Sn1'])
            act(Sn1[:, :], Sn1[:, :], AF.Ln, ['Sn1', 'cb16'], ['Sn1'], bias=one_t[:, 0:1])
            tt(Sn1v, Sn1v, mnew2, ALU.mult, ['Sn1', 'cst'], ['Sn1'])
            pcs = [bank(True), bank(True)]
            for hc in range(2):
                mm(pcs[hc][0][:, 0:NC_], C('uincr'), S1[:, hc * NC_:(hc + 1) * NC_], True, True, ['S1', 'cst'], [pcs[hc][1]])
            pn_, pnn = bank(True)
            mm(pn_[:, 0:16], C('uincr'), Sn1[:, :], True, True, ['Sn1', 'cst'], [pnn])
            mm(pn_[:, 16:32], C('ones'), Sn1[:, :], True, True, ['Sn1', 'cst'], [pnn])
            cp(Sn2[:, :], pn_[:, 16:32], [pnn], ['Sn2'])
            for hc in range(2):
                ptt_, pttn = bank()
                mm(ptt_[:, 0:NC_], C('ones'), S1[:, hc * NC_:(hc + 1) * NC_], True, True, ['S1', 'cst'], [pttn])
                cp(S2[:, hc * NC_:(hc + 1) * NC_], ptt_[:, 0:NC_], [pttn], ['S2'], eng='act')
            for hc in range(2):
                for c8 in range(8):
                    colv = slice(hc * NC_ + c8, (hc + 1) * NC_, 8)
                    scan(S3[:, colv], ones_f[:, 0:NPG], S2[:, colv], Sn2[:, hc * 8 + c8:hc * 8 + c8 + 1],
                         ['S2', 'Sn2', 'cb16'], ['S1'])
            for hc in range(2):
                sl = slice(hc * NC_, (hc + 1) * NC_)
                tt(S3[:, sl], S3[:, sl], S2[:, sl], ALU.subtract, ['S1', 'S2'], ['S1'])
                tt(S3[:, sl], S3[:, sl], pcs[hc][0][:, 0:NC_], ALU.add, ['S1', pcs[hc][1]], ['S1'])
                tt(S3[:, sl], pS[which][hc][0][:, 0:NC_], S3[:, sl], ALU.subtract, [pS[which][hc][1], 'S1'], ['S1'])
            act(S3[:, 0:2 * NC_], S3[:, 0:2 * NC_], AF.Exp, ['S1'], ['S1'])
            cp(L.S3b[which][:, 0:2 * NC_], S3[:, 0:2 * NC_], ['S1'], [f'S3b{which}'])
            cp(Sn1[:, :], pn_[:, 0:16], [pnn], ['Sn1'])
            tt(Wn[:, :], pSn[:, 0:16], Sn1[:, :], ALU.subtract, [pSnn, 'Sn1'], ['Wn'])
            act(Wn[:, :], Wn[:, :], AF.Exp, ['Wn'], ['Wn'])
            tt(Wnv, Wnv, mnew2, ALU.mult, ['Wn', 'cst'], ['Wn'])
            cp(L.Wnb[which][:], Wn[:, :], ['Wn'], [f'Wnb{which}'])
            release(pcs[0][1], pcs[1][1], pnn, pS['sb'][0][1], pS['sb'][1][1], pSnn)
            pO = {w_: [bank(True), bank(True)] for w_ in WH}
            for j in range(NPG):
                vt_, vtn = L.vpg[j % 4], f'vpg{j % 4}'
                gather(vt_, vtn, cvv_d, l, b, j)
                vb_, vbn = L.ktb[j % 3], f'ktb{j % 3}'
                cp(vb_[:], vt_[:], [vtn], [vbn], eng=('act' if j % 2 else 'dve'))
                for wi, which in enumerate(WH):
                    jc = jcol(which, j)
                    for hc in range(2):
                        mm(pO[which][hc][0][:, 0:8], vb_[:, wi * 256 + hc * 128:wi * 256 + (hc + 1) * 128],
                           L.S3b[which][:, hc * NC_ + jc * 8:hc * NC_ + jc * 8 + 8],
                           j == 0, False, [vbn, f'S3b{which}'], [pO[which][hc][1]])
            for wi, which in enumerate(WH):
                dma(L.VnP[0:4, :], L.Vnew[which][b * 4:b * 4 + 4, :], [f'Vnew{which}', 'VnP'], ['VnP'])
                cp(L.VnPb[:], L.VnP[:], ['VnP'], ['VnPb'])
                for hc in range(2):
                    mm(pO[which][hc][0][:, 0:8], L.VnPb[:, hc * 128:(hc + 1) * 128], L.Wnb[which][:, hc * 8:(hc + 1) * 8], False, True,
                       ['VnPb', f'Wnb{which}'], [pO[which][hc][1]])
                for hc in range(2):
                    for hp in range(2):
                        ps_ = slice(hp * 64, hp * 64 + 64)
                        src = pO[which][hc][0][ps_, hp * 4:hp * 4 + 4]
                        if which == 'fx':
                            tt(A.yT['fx'][ps_, hc, qc], src, L.rden[ps_, hc, hp * 4:hp * 4 + 4], ALU.mult,
                               [pO[which][hc][1], 'rden'], ['yfxT'])
                        else:
                            cp(A.yT['sb'][ps_, hc, qc], src, [pO[which][hc][1]], ['ysbT'])
                release(pO[which][0][1], pO[which][1][1])

    def alloc_A(S, sample=False):
        A = Bag()
        if not sample:
            A.sF = [S.sb(f"sF{i}", [128, 512]) for i in range(5)]
            A.sH = [S.sb(f"sH{i}", [128, 512], BF16) for i in range(2)]
        A.QT = {'sb': S.sb("QTsb", [128, 2, 512], BF16), 'fx': S.sb("QTfx", [128, 2, 512], BF16)}
        A.xsT = S.sb("xsT", [128, 4, 512])
        A.BTf = S.sb("BTf", [128, 2, 512])
        A.BTb = S.sb("BTb", [128, 2, 512], BF16)
        A.CTb = S.sb("CTb", [128, 2, 512], BF16)
        A.zs = S.sb("zs", [128, 4, 512])
        A.dts = S.sb("dts", [128, 4, 8])
        A.yT = {'sb': S.sb("ysbT", [128, 2, 512], BF16), 'fx': S.sb("yfxT", [128, 2, 512], BF16)}
        A.yssT = S.sb("yssT", [128, 4, 512], BF16)
        A.mixT = S.sb("mixT", [128, 8, 512], BF16)
        A.Xb = [S.sb(f"Xb{i}", [128, 515]) for i in range(2)]
        A.B_tok = S.sb("B_tok", [128, 256], BF16)
        for nm in ('dta', 'acs', 'ea', 'te', 'wdt', 'cd'):
            setattr(A, nm, S.sb(nm, [128, 8]))
        A.scm = S.sb("scm", [128, 2, 128])
        A.Ld = [S.sb(f"Ld{i}", [128, 128]) for i in range(2)]
        A.MT = S.sb("MT", [128, 4, 128], BF16)
        A.ssq = S.sb("ssq", [128, 2])
        A.rs2 = S.sb("rs2", [128, 2])
        return A

    groups = [(gi, 512, list(range(gi * 4, gi * 4 + 4)), False) for gi in range(NG)]
    groups.append((NG, NS, [0], True))
    try:
        cast_layer(0)
        for l in range(DEPTH):
            dma(par[:], par_d[l], (), ['par'])
            act(arep[:], PR('alog'), AF.Exp, ['par'], ['par2'])
            ts(arep[:], arep[:], -1.0, None, ALU.mult, None, ['par2'], ['par2'])
            ts(NBF[:], PR('bf_col'), -1.0, None, ALU.mult, None, ['par'], ['par2'])
            memset(carryF[:], 0.0, ['carryF'])
            memset(carryFT[:], 0.0, ['carryFT'])
            memset(ccar[:].rearrange("p a b -> p (a b)"), 0.0, ['ccar'])
            memset(hst[:].rearrange("p a b -> p (a b)"), 0.0, ['hst'])
            memset(hstb[:].rearrange("p a b -> p (a b)"), 0.0, ['hstb'])
            LS = Scope()
            L = Bag()
            L.KT = {'sb': LS.sb("KTsb", [128, 2, T], BF16), 'fx': LS.sb("KTfx", [128, 2, T], BF16)}
            L.V = {'sb': LS.sb("Vsb", [128, NT, 256], BF16), 'fx': LS.sb("Vfx", [128, NT, 256], BF16)}
            L.negFk = LS.sb("negFk", [128, NT, 4])
            for (gi, n, tiles, sample) in groups:
                if sample:
                    LS.close()
                    LS = Scope()
                    L = Bag()
                    L.KTs = {'sb': LS.sb("KTssb", [128, 2, 16], BF16), 'fx': LS.sb("KTsfx", [128, 2, 16], BF16)}
                    L.Vnew = {'sb': LS.sb("Vnsb", [16, 256]), 'fx': LS.sb("Vnfx", [16, 256])}
                    L.lfS = LS.sb("lfS", [16, 4])
                    L.XS = LS.sb("XS", [128, 8, NB, 7])
                    L.h0 = [LS.sb(f"h0_{i}", [64, 8, 128]) for i in range(1)]
                    L.h0Tb = LS.sb("h0Tb", [128, NB, 8, 64], BF16)
                    L.CTm = LS.sb("CTm", [128, NB, 2, 16], BF16)
                    L.xwm = LS.sb("xwm", [16, 512], BF16)
                    L.cdS = LS.sb("cdS", [128, NB, 8])
                    L.kpg = [LS.sb(f"kpg{i}", [128, 512]) for i in range(4)]
                    L.vpg = [LS.sb(f"vpg{i}", [128, 512]) for i in range(4)]
                    L.ktb = [LS.sb(f"ktb{i}", [128, 512], BF16) for i in range(3)]
                    L.Qbd = {w_: LS.sb(f"Qbd{w_}", [128, 2, 8], BF16) for w_ in ('sb', 'fx')}
                    L.KTn = {w_: LS.sb(f"KTn{w_}", [128, 2, 128], BF16) for w_ in ('sb', 'fx')}
                    L.S1 = LS.sb("S1", [128, 1024])
                    L.S2 = LS.sb("S2", [128, 1024])
                    L.S3 = L.S1
                    L.S3b = {w_: LS.sb(f"S3b{w_}", [128, 1024], BF16) for w_ in ('sb', 'fx')}
                    L.VnPb = LS.sb("VnPb", [128, 256], BF16)
                    L.Wnb = {w_: LS.sb(f"Wnb{w_}", [128, 16], BF16) for w_ in ('sb', 'fx')}
                    L.Sn1 = LS.sb("Sn1", [128, 16])
                    L.Sn2 = LS.sb("Sn2", [128, 16])
                    L.Wn = LS.sb("Wn", [128, 16])
                    L.lfp = LS.sb("lfp", [64, 512])
                    L.lfw = LS.sb("lfw", [64, 512])
                    L.Fkd = LS.sb("Fkd", [128, 64, 4])
                    L.tot64 = LS.sb("tot64", [64, 4])
                    L.ftot = LS.sb("ftot", [128, 4])
                    L.ydec = LS.sb("ydec", [128, 2, 8])
                    L.rden = LS.sb("rden", [128, 2, 8])
                    L.VnP = LS.sb("VnP", [128, 256])
                    L.cnS = LS.sb("cnS", [16, 4])
                    L.cn4 = LS.sb("cn4", [128, 4])
                if l == 0:
                    if sample:
                        load_x(xs_d, NS, 0)
                    else:
                        for ti in range(4):
                            load_x(xp_d[(gi * 4 + ti) * 128:(gi * 4 + ti + 1) * 128, :], 128, ti * 128)
                else:
                    dma(xg[:].rearrange("p a b -> p (a b)"), xscr[gi], [f'xscr{gi}'], ['xg'])
                ck(f'load{l}.{gi}')
                AS = Scope()
                A = alloc_A(AS, sample)
                if sample:
                    dma(L.XS[:, :, :, 0:3], scv_d[l], (), ['XS'])
                bg = proj_group(l, A, L, gi, n, tiles, sample)
                ck(f'proj{l}.{gi}')
                if gi == 0 and l + 1 < DEPTH:
                    cast_layer(l + 1)
                if sample:
                    attn_decode(l, A, L)
                else:
                    attn_prompt(l, A, L, gi, bg)
                    for _ in bg:
                        pass
                    if gi == NG - 1:
                        for half in range(2):
                            st_, sn = next_stage()
                            pb, pn = bank()
                            for k in range(4):
                                h = half * 4 + k
                                tr(pb[0:64, k * 128:(k + 1) * 128], hst[:, h, :], ident, ['hst', 'cst'], [pn])
                            cp(st_[0:64, :], pb[0:64, :], [pn], [sn])
                            dma(pssm_o[l, half * 4:half * 4 + 4].rearrange("h p n -> p h n"),
                                st_[0:64, :].rearrange("p (h n) -> p h n", h=4), [sn], [])
                ck(f'attn{l}.{gi}')
                merge(l, A, n)
                ck(f'merge{l}.{gi}')
                AS.close()
                DS = Scope()
                Dd = Bag()
                Dd.uT = DS.sb("uT", [128, 32, 512], BF16)
                if l == DEPTH - 1:
                    Dd.finT = DS.sb("finT", [128, 8, 512])
                ffn(l, Dd, n)
                ck(f'ffn{l}.{gi}')
                if l == DEPTH - 1:
                    final_out(Dd, n, tiles, sample)
                else:
                    dma(xscr[gi], xg[:].rearrange("p a b -> p (a b)"), ['xg'], [f'xscr{gi}'])
                DS.close()
            LS.close()
    except _Stop:
        pass
    for sc in reversed(list(open_scopes)):
        if sc is not G:
            sc.close()
    P.emit()
    G.es.close()
    pes.close()
    return nc


def host_inputs(inp, cfg, core, kk, vv):
    T, NB, NPG, NPOOL, DEPTH = cfg['T'], cfg['NB'], cfg['NPG'], cfg['NPOOL'], cfg['DEPTH']
    b0 = core * NB
    m = {}
    m['xp'] = np.ascontiguousarray(inp['x_prompt'][core])
    m['xs'] = np.ascontiguousarray(inp['x_sample'][b0:b0 + NB].reshape(NB * 4, D))
    m['ckk'] = kk
    m['cvv'] = vv
    m['clf'] = inp['cache_fox_logf'].reshape(DEPTH, NPOOL, 512)
    m['sst'] = np.ascontiguousarray(inp['state_ssm'][:, b0:b0 + NB])
    sc = inp['state_conv'][:, b0:b0 + NB]
    m['scv'] = np.ascontiguousarray(sc.reshape(DEPTH, NB, 3, 8, 128).transpose(0, 4, 3, 1, 2))
    m['ptab'] = np.ascontiguousarray(inp['page_table'][b0:b0 + NB].reshape(1, NB * NPG).astype(np.int32))
    return m


_CACHE = {}


def run(inp, cfg, ncores, stop_at=None):
    inp = {k: np.asarray(v) for k, v in inp.items()}
    DEPTH = cfg['DEPTH']
    key = tuple(sorted(cfg.items()))
    if key not in _CACHE:
        _CACHE[key] = build(cfg, stop_at)
    nc = _CACHE[key]
    wall = prep_weights(inp, DEPTH)
    par = prep_params(inp, DEPTH)
    cst = prep_consts()
    cmask = prep_cmask()
    NPOOL_ = cfg['NPOOL']
    kk = np.concatenate([inp['cache_sb_k'].reshape(DEPTH * NPOOL_, 128, 256),
                         inp['cache_fox_k'].reshape(DEPTH * NPOOL_, 128, 256)], axis=2)
    vv = np.concatenate([inp['cache_sb_v'].reshape(DEPTH * NPOOL_, 128, 256),
                         inp['cache_fox_v'].reshape(DEPTH * NPOOL_, 128, 256)], axis=2)
    in_maps = []
    for c in range(ncores):
        m = host_inputs(inp, cfg, c, kk, vv)
        m['wall'] = wall
        m['par'] = par
        m['cst'] = cst
        m['cmask'] = cmask
        in_maps.append(m)
    res = run_bass_kernel_spmd(nc, in_maps, core_ids=list(range(ncores)))
    R = res.results
    T, NB = cfg['T'], cfg['NB']

    def cat(name, shape_per_core, axis):
        return np.concatenate([R[c][name].reshape(shape_per_core) for c in range(ncores)], axis=axis)
    y_prompt = cat('yp', (1, T, D), 0)
    y_sample = cat('ys', (NB, 4, D), 0)
    outs = [y_prompt, y_sample]
    for nm in ('psbk', 'psbv', 'pfxk', 'pfxv'):
        outs.append(cat(nm, (DEPTH, 1, T, 4, 64), 1))
    outs.append(cat('plf', (DEPTH, 1, T, 4), 1))
    outs.append(cat('pssm', (DEPTH, 1, 8, 64, 128), 1))
    outs.append(cat('pcv', (DEPTH, 1, 3, 1024), 1))
    for nm in ('ssbk', 'ssbv', 'sfxk', 'sfxv'):
        outs.append(cat(nm, (DEPTH, NB, 4, 4, 64), 1))
    outs.append(cat('slf', (DEPTH, NB, 4, 4), 1))
    outs.append(cat('sssm', (DEPTH, NB, 8, 64, 128), 1))
    outs.append(cat('scvo', (DEPTH, NB, 3, 1024), 1))
    return tuple(np.ascontiguousarray(o.astype(np.float32)) for o in outs)


def kernel(**inputs):
    return run(inputs, CFG_FULL, NCORES)
```

```python
import numpy as np
from contextlib import ExitStack
import concourse.bass as bass
import concourse.mybir as mybir
from concourse.bass_utils import run_bass_kernel_spmd

F32 = mybir.dt.float32
BF16 = mybir.dt.bfloat16
I32 = mybir.dt.int32
AF = mybir.ActivationFunctionType
ALU = mybir.AluOpType
AX = mybir.AxisListType

ENG = ['pe', 'act', 'dve', 'pool', 'sp']
D = 1024
NCORES = 8
EPS = 1e-6
CFG_FULL = dict(T=2048, NB=4, NPG=64, NPOOL=2560, DEPTH=2)


class Prog:
    def __init__(self, nc, n_dma_sems=8):
        self.nc = nc
        self.ops = {e: [] for e in ENG}
        self.cnt = {e: 0 for e in ENG}
        self.lastw = {}
        self.readers = {}
        self.dma_cnt = {}
        self.dma_rr = {q: 0 for q in ENG}
        self.nd = n_dma_sems

    def _deps(self, reads, writes, eng=None):
        deps = {}

        def add(k, v):
            if deps.get(k, 0) < v:
                deps[k] = v
        for r in reads:
            if r in self.lastw:
                add(*self.lastw[r])
            if r.startswith('ps'):
                for k, v in self.readers.get(r, {}).items():
                    if k != ('c', eng):
                        add(k, v)
        for w in writes:
            if w in self.lastw:
                add(*self.lastw[w])
            for k, v in self.readers.get(w, {}).items():
                add(k, v)
        return deps

    def _commit(self, tok, reads, writes):
        k, v = tok
        for r in reads:
            d = self.readers.setdefault(r, {})
            if d.get(k, 0) < v:
                d[k] = v
        for w in writes:
            self.lastw[w] = tok
            self.readers[w] = {}

    def op(self, eng, fn, reads=(), writes=()):
        deps = self._deps(reads, writes, eng)
        self.cnt[eng] += 1
        tok = (('c', eng), self.cnt[eng])
        self.ops[eng].append((fn, deps, tok))
        self._commit(tok, reads, writes)

    def dma(self, q, fn, reads=(), writes=()):
        deps = self._deps(reads, writes)
        k = self.dma_rr[q]
        self.dma_rr[q] = (k + 1) % self.nd
        c = self.dma_cnt.get((q, k), 0)
        if c > 0:
            deps[('d', q, k)] = max(deps.get(('d', q, k), 0), c)
        self.dma_cnt[(q, k)] = c + 1
        tok = (('d', q, k), c + 1)
        self.ops[q].append((fn, deps, tok))
        self._commit(tok, reads, writes)

    def barrier(self):
        deps = {('c', e): self.cnt[e] for e in ENG if self.cnt[e] > 0}
        for (q, k), c in self.dma_cnt.items():
            deps[('d', q, k)] = c
        for e in ENG:
            self.ops[e].append((None, dict(deps), None))

    def emit(self):
        nc = self.nc
        sem_c = {e: nc.alloc_semaphore(name=f"c_{e}") for e in ENG}
        sem_d = {}
        for (q, k) in sorted(self.dma_cnt):
            sem_d[(q, k)] = nc.alloc_semaphore(name=f"d_{q}{k}")

        def semof(key):
            if key[0] == 'c':
                return sem_c[key[1]], 1
            return sem_d[(key[1], key[2])], 16

        final_waits = list(self.dma_cnt.items())

        def run(ename, eng):
            seen = {}
            for fn, deps, tok in self.ops[ename]:
                for key, val in deps.items():
                    if key == ('c', 'pe') and ename == 'pe':
                        continue
                    if seen.get(key, 0) >= val:
                        continue
                    s, mul = semof(key)
                    eng.wait_ge(s, val * mul)
                    seen[key] = val
                if fn is None:
                    continue
                ins = fn(eng)
                s, mul = semof(tok[0])
                ins.then_inc(s, mul)
            if ename == 'sp':
                for (q, k), c in final_waits:
                    eng.wait_ge(sem_d[(q, k)], 16 * c)

        with nc.Block() as block:
            @block.sync
            def _(e):
                run('sp', e)

            @block.scalar
            def _(e):
                run('act', e)

            @block.vector
            def _(e):
                run('dve', e)

            @block.gpsimd
            def _(e):
                run('pool', e)

            @block.tensor
            def _(e):
                run('pe', e)


SB_W, FX_W = 256, 256
OFF_QSB, OFF_KSB, OFF_VSB = 0, 256, 512
OFF_QFX, OFF_KFX, OFF_VFX = 768, 1024, 1280
OFF_F = 1536
OFF_Z = 1540
OFF_XBC = 2052
OFF_DT = 3076
OFF_G = 3084


def wblocks():
    bl = []
    for cb in range(40):
        bl.append((f'fm{cb}', 1024))
    bl.append(('f4', 32))
    bl += [('tmA', 4096), ('tmB', 4096), ('tmC', 4096), ('tmD', 96)]
    for cb in range(8):
        bl.append((f'sbo{cb}', 256))
    for cb in range(8):
        bl.append((f'sso{cb}', 512))
    for cb in range(8):
        bl.append((f'fxo{cb}', 256))
    for cb in range(8):
        bl.append((f'wo{cb}', 1024))
    for cb in range(32):
        bl.append((f'up{cb}', 1024))
    for cb in range(8):
        bl.append((f'dn{cb}', 4096))
    return bl


def wlayout():
    off = {}
    r = 0
    for name, E in wblocks():
        off[name] = (r, E)
        r += 128 * E // 1024
    nrows = ((r + 1023) // 1024) * 1024
    return off, nrows


def _blk(W, cols):
    K = W.shape[0]
    sub = W[:, cols]
    return np.ascontiguousarray(sub.reshape(K // 128, 128, len(cols)).transpose(1, 0, 2))


def prep_weights(inp, depth):
    off, nrows = wlayout()
    wall = np.zeros((depth, nrows * 1024), np.float32)
    fm_cols = (list(range(OFF_QSB, OFF_QSB + 256)) + list(range(OFF_KSB, OFF_KSB + 256)) +
               list(range(OFF_QFX, OFF_QFX + 256)) + list(range(OFF_KFX, OFF_KFX + 256)) +
               list(range(OFF_XBC, OFF_XBC + 1024)) + list(range(OFF_G, OFF_G + 3072)))
    for l in range(depth):
        win = inp['w_in'][l]

        def put(name, arr):
            r, E = off[name]
            a = arr.reshape(128, -1)
            assert a.shape[1] == E, (name, a.shape, E)
            wall[l, r * 1024: r * 1024 + 128 * E] = a.reshape(-1)
        for cb in range(40):
            put(f'fm{cb}', _blk(win, fm_cols[cb * 128:(cb + 1) * 128]))
        put('f4', _blk(win, list(range(OFF_F, OFF_F + 4))))
        put('tmA', _blk(win, list(range(OFF_KSB, OFF_KSB + 512))))
        put('tmB', _blk(win, list(range(OFF_KFX, OFF_KFX + 512))))
        put('tmC', _blk(win, list(range(OFF_Z, OFF_Z + 512))))
        put('tmD', _blk(win, list(range(OFF_F, OFF_F + 4)) + list(range(OFF_DT, OFF_DT + 8))))
        for cb in range(8):
            cs = list(range(cb * 128, (cb + 1) * 128))
            put(f'sbo{cb}', _blk(inp['w_sb_out'][l], cs))
            put(f'sso{cb}', _blk(inp['w_ssm_out'][l], cs))
            put(f'fxo{cb}', _blk(inp['w_fox_out'][l], cs))
            put(f'wo{cb}', _blk(inp['w_o'][l], cs))
            put(f'dn{cb}', _blk(inp['w_down'][l], cs))
        for cb in range(32):
            put(f'up{cb}', _blk(inp['w_up'][l], list(range(cb * 128, (cb + 1) * 128))))
    return wall.reshape(depth, nrows, 1024)


PAR = {}
_o = 0
for _n, _w in [('g1', 8), ('g2', 8), ('gf', 8), ('bf_rep', 4), ('bf_col', 1), ('cw', 32), ('cb', 8),
               ('dtb', 8), ('alog', 8), ('dsk', 8), ('gn', 512)]:
    PAR[_n] = (_o, _w)
    _o += _w
NPAR = _o


def prep_params(inp, depth):
    par = np.zeros((depth, 128, NPAR), np.float32)

    def col(v):
        return v.reshape(8, 128).T
    for l in range(depth):
        def put(n, a):
            o, w = PAR[n]
            par[l, :, o:o + w] = a
        put('g1', col(inp['norm1_g'][l]))
        put('g2', col(inp['norm2_g'][l]))
        put('gf', col(inp['final_norm_g']))
        put('bf_rep', np.broadcast_to(inp['b_forget'][l][None, :], (128, 4)))
        bc = np.zeros((128, 1), np.float32)
        bc[0:4, 0] = inp['b_forget'][l]
        put('bf_col', bc)
        put('cw', inp['conv_w'][l].reshape(4, 8, 128).transpose(2, 1, 0).reshape(128, 32))
        put('cb', col(inp['conv_b'][l]))
        put('dtb', np.broadcast_to(inp['dt_bias'][l][None, :], (128, 8)))
        put('alog', np.broadcast_to(inp['a_log'][l][None, :], (128, 8)))
        put('dsk', np.broadcast_to(inp['d_skip'][l][None, :], (128, 8)))
        put('gn', np.broadcast_to(inp['ssm_norm_g'][l][None, :], (128, 512)))
    return par


CST = {}
_o = 0
for _n, _w in [('ident', 128), ('linc', 128), ('ustr', 128), ('ones', 128), ('uincr', 128),
               ('lincS', 16), ('ustrS', 16), ('onesS', 16), ('selB', 512), ('rowB', 4), ('colB', 64),
               ('mnsb', 8), ('mnfx', 8), ('iota', 1)]:
    CST[_n] = (_o, _w)
    _o += _w
NCST = _o


def prep_cmask():
    j = np.arange(128)[:, None]
    q = np.arange(512)[None, :]
    m = np.zeros((128, 4, 513), np.float32)
    for i in range(4):
        m[:, i, 1:] = ((i * 128 + j) <= q)
    return m.reshape(128, 4 * 513)


def prep_consts():
    c = np.zeros((128, NCST), np.float32)
    j = np.arange(128)[:, None]
    l = np.arange(128)[None, :]

    def put(n, a):
        o, w = CST[n]
        c[:a.shape[0], o:o + w] = a
    put('ident', (j == l).astype(np.float32))
    put('linc', (j <= l).astype(np.float32))
    put('ustr', (j > l).astype(np.float32))
    put('ones', np.ones((128, 128), np.float32))
    put('uincr', (j >= l).astype(np.float32))
    j16 = np.arange(16)[:, None]
    l16 = np.arange(16)[None, :]
    same = (j16 // 4 == l16 // 4)
    put('lincS', (same & (j16 <= l16)).astype(np.float32))
    put('ustrS', (same & (j16 > l16)).astype(np.float32))
    put('onesS', same.astype(np.float32))
    selB = np.zeros((16, 4, 128), np.float32)
    for b in range(4):
        selB[4 * b:4 * b + 4, b, :] = 1.0
    put('selB', selB.reshape(16, 512))
    rowB = np.zeros((16, 4), np.float32)
    for b in range(4):
        rowB[4 * b:4 * b + 4, b] = 1.0
    put('rowB', rowB)
    colB = np.zeros((128, 4, 16), np.float32)
    for b in range(4):
        colB[:, b, 4 * b:4 * b + 4] = 1.0
    put('colB', colB.reshape(128, 64))
    qq = np.tile(np.arange(4), 2)[None, :]
    t4 = np.arange(128)[:, None]
    put('mnsb', ((t4 < qq) & (t4 < 4)).astype(np.float32))
    put('mnfx', ((t4 <= qq) & (t4 < 4)).astype(np.float32))
    put('iota', np.arange(128, dtype=np.float32)[:, None])
    return c


class Bag:
    pass


class _Stop(Exception):
    pass


def build(cfg, stop_at=None):
    T, NB, NPG, NPOOL, DEPTH = cfg['T'], cfg['NB'], cfg['NPG'], cfg['NPOOL'], cfg['DEPTH']
    NS = NB * 4
    NT = T // 128
    NG = T // 512
    woff, wrows = wlayout()

    nc = bass.Bass("TRN2", target_bir_lowering=False)
    P = Prog(nc)

    def din(name, shape, dt=F32):
        return nc.dram_tensor(name, shape, dt, kind="ExternalInput").ap()

    def dout(name, shape, dt=F32):
        return nc.dram_tensor(name, shape, dt, kind="ExternalOutput").ap()

    xp_d = din("xp", [T, D])
    xs_d = din("xs", [NS, D])
    ckk_d = din("ckk", [DEPTH * NPOOL, 128, 512])
    cvv_d = din("cvv", [DEPTH * NPOOL, 128, 512])
    clf_d = din("clf", [DEPTH, NPOOL, 512])
    sst_d = din("sst", [DEPTH, NB, 8, 64, 128])
    scv_d = din("scv", [DEPTH, 128, 8, NB, 3])
    ptab_d = din("ptab", [1, NB * NPG], I32)
    wall_d = din("wall", [DEPTH, wrows, 1024])
    par_d = din("par", [DEPTH, 128, NPAR])
    cst_d = din("cst", [128, NCST])
    cmask_d = din("cmask", [128, 4 * 513])
    wscr = nc.dram_tensor("wscr", [DEPTH, wrows, 1024], BF16, kind="Internal").ap()
    xscr = nc.dram_tensor("xscr", [NG + 1, 128, 8 * 512], F32, kind="Internal").ap()

    yp_o = dout("yp", [T, D])
    ys_o = dout("ys", [NS, D])
    psbk_o = dout("psbk", [DEPTH, T, 256])
    psbv_o = dout("psbv", [DEPTH, T, 256])
    pfxk_o = dout("pfxk", [DEPTH, T, 256])
    pfxv_o = dout("pfxv", [DEPTH, T, 256])
    plf_o = dout("plf", [DEPTH, T, 4])
    pssm_o = dout("pssm", [DEPTH, 8, 64, 128])
    pcv_o = dout("pcv", [DEPTH, 3, 1024])
    ssbk_o = dout("ssbk", [DEPTH, NS, 256])
    ssbv_o = dout("ssbv", [DEPTH, NS, 256])
    sfxk_o = dout("sfxk", [DEPTH, NS, 256])
    sfxv_o = dout("sfxv", [DEPTH, NS, 256])
    slf_o = dout("slf", [DEPTH, NS, 4])
    sssm_o = dout("sssm", [DEPTH, NB, 8, 64, 128])
    scv_o = dout("scvo", [DEPTH, NB, 3, 1024])

    uid = {'n': 0}

    open_scopes = []

    def ck(name):
        if stop_at is not None and name == stop_at:
            raise _Stop()

    class Scope:
        def __init__(self):
            self.es = ExitStack()
            open_scopes.append(self)

        def sb(self, name, shape, dt=F32):
            uid['n'] += 1
            return self.es.enter_context(nc.sbuf_tensor(f"{name}_{uid['n']}", shape, dt))

        def close(self):
            P.barrier()
            self.es.close()
            open_scopes.remove(self)

    G = Scope()
    sb = G.sb

    def mm(out, lhsT, rhs, start, stop, r, w):
        P.op('pe', lambda e: e.matmul(out, lhsT, rhs, start=start, stop=stop), r, w)

    def tr(out, in_, ident, r, w):
        P.op('pe', lambda e: e.transpose(out, in_, ident), r, w)

    def act(out, in_, func, r, w, bias=None, scale=None, accum=None):
        kw = {}
        if bias is not None:
            kw['bias'] = bias
        if scale is not None:
            kw['scale'] = scale
        if accum is not None:
            kw['accum_out'] = accum
        P.op('act', lambda e: e.activation(out, in_, func, **kw), r, w)

    def tt(out, a, b, op, r, w, eng='dve'):
        P.op(eng, lambda e: e.tensor_tensor(out, a, b, op), r, w)

    def ts(out, a, s1, s2, op0, op1, r, w, eng='dve'):
        if op1 is None:
            P.op(eng, lambda e: e.tensor_scalar(out, a, s1, None, op0), r, w)
        else:
            P.op(eng, lambda e: e.tensor_scalar(out, a, s1, s2, op0, op1), r, w)

    def stt(out, a, s, b, op0, op1, r, w):
        P.op('dve', lambda e: e.scalar_tensor_tensor(out, a, s, b, op0, op1), r, w)

    def cp(out, in_, r, w, eng='dve'):
        if eng == 'act':
            P.op('act', lambda e: e.copy(out, in_), r, w)
        else:
            P.op(eng, lambda e: e.tensor_copy(out, in_), r, w)

    def memset(ap, v, w, eng='dve'):
        P.op(eng, lambda e: e.memset(ap, v), (), w)

    def dma(out, in_, r, w, q='sp'):
        P.dma(q, lambda e: e.dma_start(out=out, in_=in_), r, w)

    def scan(out, d0, d1, init, r, w):
        P.op('dve', lambda e: e.tensor_tensor_scan(out, d0, d1, init, ALU.mult, ALU.add), r, w)

    def recip(out, in_, r, w):
        P.op('dve', lambda e: e.reciprocal(out, in_), r, w)

    pes = ExitStack()
    banks = [pes.enter_context(nc.psum_tensor(f"ps{i}", [128, 512], F32)) for i in range(8)]
    bstate = {'i': 0}

    reserved = set()

    def bank(hold=False):
        for _ in range(8):
            i = bstate['i']
            bstate['i'] = (i + 1) % 8
            if i not in reserved:
                break
        else:
            raise RuntimeError("all PSUM banks reserved")
        if hold:
            reserved.add(i)
        return banks[i], f'ps{i}'

    def release(*names):
        for n_ in names:
            reserved.discard(int(n_[2:]))

    Fp = [sb(f"F{i}", [128, 512]) for i in range(8)]
    Hp = [sb(f"H{i}", [128, 512], BF16) for i in range(8)]

    def F(i):
        return Fp[i], f'F{i}'

    def H(i):
        return Hp[i], f'H{i}'

    stg = {'i': 0}

    def next_stage():
        i = 6 + stg['i']
        stg['i'] = 1 - stg['i']
        return Fp[i], f'F{i}'

    cst = sb("cst", [128, NCST])
    dma(cst[:], cst_d[:, :], (), ['cst'])

    def C(n, rows=128, c0=0, c1=None):
        o, w = CST[n]
        if c1 is None:
            c1 = w
        return cst[0:rows, o + c0:o + c1]

    onesb = sb("onesb", [128, 128], BF16)
    nuincb = sb("nuincb", [128, 128], BF16)
    nonesb = sb("nonesb", [128, 128], BF16)
    mext = sb("mext", [128, 4, 513], BF16)
    cp(onesb[:], C('ones'), ['cst'], ['cb16'])
    ts(nuincb[:], C('uincr'), -1.0, None, ALU.mult, None, ['cst'], ['cb16'])
    ts(nonesb[:], C('ones'), -1.0, None, ALU.mult, None, ['cst'], ['cb16'])
    for i in range(4):
        f_, fn = F(i)
        dma(f_[:, 0:512], cmask_d[:, i * 513:i * 513 + 512], (), [fn])
        cp(mext[:, i, 0:512], f_[:, 0:512], [fn], ['cb16'])
        memset(mext[:, i, 512:513], 1.0, ['cb16'])
    CB = ['cst', 'cb16']
    ident = C('ident')

    def msb(di):
        return mext[:, di, 0:512]

    def mfx(di):
        return mext[:, di, 1:513]

    epst = sb("epst", [128, 1])
    one_t = sb("one_t", [128, 1])
    ones_f = sb("ones_f", [128, 128])
    memset(epst[:], EPS, ['cb16'])
    memset(one_t[:], 1.0, ['cb16'])
    memset(ones_f[:], 1.0, ['cb16'])
    onesT = sb("onesT", [4, 512])
    sel4 = sb("sel4", [4, 4, 128])
    memset(onesT[:], 1.0, ['cb16'])
    for h in range(4):
        cp(sel4[:, h, :], C('ident', rows=4, c0=h, c1=h + 1).to_broadcast([4, 128]), ['cst'], ['cb16'])

    par = sb("par", [128, NPAR])
    arep = sb("arep", [128, 8])
    NBF = sb("NBF", [128, 1])

    def PR(n, rows=128, c0=0, c1=None):
        o, w = PAR[n]
        if c1 is None:
            c1 = w
        return par[0:rows, o + c0:o + c1]

    def cast_layer(l):
        nch = wrows // 1024
        first = [c for c in (0, 1, 5, 6) if c < nch]
        for ch in first + [c for c in range(nch) if c not in first]:
            P.dma('pool', lambda e, l=l, ch=ch: e.dma_start(out=wscr[l, ch * 1024:(ch + 1) * 1024, :],
                                                         in_=wall_d[l, ch * 1024:(ch + 1) * 1024, :]),
                  (), [f'scr{l}.{ch}'])

    wbig = [sb(f"wbig{i}", [128, 4096], BF16) for i in range(2)]
    wsm = [sb(f"wsm{i}", [128, 1024], BF16) for i in range(4)]
    wstate = {'b': 0, 's': 0}

    def load_w(l, name):
        r0, E = woff[name]
        if E > 1024:
            i = wstate['b']
            wstate['b'] = (i + 1) % 2
            tile_, res = wbig[i], f'wb{i}'
        else:
            i = wstate['s']
            wstate['s'] = (i + 1) % 4
            tile_, res = wsm[i], f'ws{i}'
        nr = 128 * E // 1024
        chs = sorted(set([r0 // 1024, (r0 + nr - 1) // 1024]))
        src = wscr[l, r0:r0 + nr, :].rearrange("r c -> (r c)").rearrange("(p e) -> p e", p=128)
        dma(tile_[:, 0:E], src, [f'scr{l}.{c}' for c in chs], [res])
        return tile_, res

    xg = sb("xg", [128, 8, 512])
    hT = sb("hT", [128, 8, 512], BF16)
    rstd = sb("rstd", [128, 512])
    carryF = sb("carryF", [128, 4])
    carryFT = sb("carryFT", [4, 1])
    ccar = sb("ccar", [128, 8, 3])
    hst = sb("hst", [128, 8, 64])
    hstb = sb("hstb", [128, 8, 64], BF16)
    lfT = sb("lfT", [4, 512])
    cumT = sb("cumT", [4, 512])
    tmp8 = sb("tmp8", [128, 16])
    lf_tok = sb("lf_tok", [128, 4])
    idx_t = sb("idx_t", [128, NB * NPG], I32)
    ptr_t = sb("ptr_t", [128, NB * NPG], I32)
    iota_i = sb("iota_i", [128, 1], I32)
    pcol = sb("pcol", [64, NB], I32)

    dma(ptr_t[:], ptab_d[0:1, :].partition_broadcast(128), (), ['ptr_t'])
    cp(iota_i[:], C('iota'), ['cst'], ['iota_i'])
    ts(idx_t[:], ptr_t[:], 128, iota_i[:, 0:1], ALU.mult, ALU.add, ['ptr_t', 'iota_i'], ['idx_t'], eng='pool')
    for b in range(NB):
        dma(pcol[0:NPG, b:b + 1], ptab_d[0:1, b * NPG:(b + 1) * NPG].rearrange("a j -> j a"), (), ['pcol'])

    def load_x(src_rows, ntok, c0):
        for half in range(2):
            st_, sn = next_stage()
            dma(st_[0:ntok, :], src_rows[:, half * 512:(half + 1) * 512], (), [sn])
            pb, pn = bank()
            for k in range(4):
                tr(pb[:, k * 128:k * 128 + ntok], st_[0:ntok, k * 128:(k + 1) * 128], C('ident', rows=ntok, c1=ntok),
                   [sn, 'cst'], [pn])
            outv = xg[:, half * 4:half * 4 + 4, c0:c0 + ntok]
            inv = pb[:, :].rearrange("p (k t) -> p k t", k=4)[:, :, 0:ntok]
            cp(outv, inv, [pn], ['xg'], eng=('dve' if half == 0 else 'act'))

    def rms_to_hT(n, gname, out_f32=None, ores='hT'):
        pb, pn = bank()
        for kc in range(8):
            s, sn = H(kc % 2)
            act(s[:, 0:n], xg[:, kc, 0:n], AF.Square, ['xg'], [sn])
            mm(pb[:, 0:n], onesb[:], s[:, 0:n], kc == 0, kc == 7, [sn] + CB, [pn])
        act(rstd[:, 0:n], pb[:, 0:n], AF.Ln, [pn, 'cb16'], ['rstd'], bias=epst[:, 0:1], scale=1.0 / D)
        act(rstd[:, 0:n], rstd[:, 0:n], AF.Exp, ['rstd'], ['rstd'], scale=-0.5)
        for kc in range(8):
            o = hT[:, kc, 0:n] if out_f32 is None else out_f32[:, kc, 0:n]
            stt(o, xg[:, kc, 0:n], PR(gname, c0=kc, c1=kc + 1), rstd[:, 0:n], ALU.mult, ALU.mult,
                ['xg', 'rstd', 'par'], [ores])

    def ssd_tile(l, A, L, ti, ntok, sample):
        cs = slice(ti * 128, ti * 128 + ntok)
        linc = C('lincS', rows=16) if sample else C('linc')
        ustr = C('ustrS', rows=16) if sample else C('ustr')
        ones_ = C('onesS', rows=16) if sample else C('ones')
        idn = C('ident', rows=ntok, c1=ntok)
        if sample:
            xs_tok, xsn = F(0)
            yo_sb, yon = F(1)
            yv, yvn = F(2)
            y2, y2n = F(3)
            dec, decn = F(4)
            xdt, xdtn = H(2)
            xw, xwn = H(3)
        else:
            (xs_tok, xsn), (yo_sb, yon), (yv, yvn), (y2, y2n), (dec, decn) = [(A.sF[i], f'sF{i}') for i in range(5)]
            (xdt, xdtn), (xw, xwn) = [(A.sH[i], f'sH{i}') for i in range(2)]
        pa, pan = bank()
        for c in range(4):
            tr(pa[0:ntok, c * 128:(c + 1) * 128], A.xsT[:, c, cs], ident, ['xsT', 'cst'], [pan])
        cp(xs_tok[0:ntok, :], pa[0:ntok, :], [pan], [xsn])
        pb_, pbn = bank()
        for g2 in range(2):
            tr(pb_[0:ntok, g2 * 128:(g2 + 1) * 128], A.BTf[:, g2, cs], ident, ['BT', 'cst'], [pbn])
        cp(A.B_tok[0:ntok, :], pb_[0:ntok, 0:256], [pbn], ['B_tok'], eng='act')
        yield
        tt(A.dta[0:ntok, :], A.dts[0:ntok, ti, :], arep[0:ntok, :], ALU.mult, ['dts', 'par2'], ['dta'])
        pc, pcn = bank()
        mm(pc[0:ntok, 0:8], linc, A.dta[0:ntok, :], True, True, ['dta', 'cst'], [pcn])
        mm(pc[0:ntok, 8:16], ones_, A.dta[0:ntok, :], True, True, ['dta', 'cst'], [pcn])
        cp(A.acs[0:ntok, :], pc[0:ntok, 0:8], [pcn], ['acs'], eng='act')
        act(A.ea[0:ntok, :], pc[0:ntok, 0:8], AF.Exp, [pcn], ['ea'])
        tt(A.te[0:ntok, :], pc[0:ntok, 8:16], A.acs[0:ntok, :], ALU.subtract, [pcn, 'acs'], ['te'])
        act(A.te[0:ntok, :], A.te[0:ntok, :], AF.Exp, ['te'], ['te'])
        tt(A.wdt[0:ntok, :], A.te[0:ntok, :], A.dts[0:ntok, ti, :], ALU.mult, ['te', 'dts'], ['wdt'])
        if not sample:
            act(A.cd[:, :], pc[:, 8:16], AF.Exp, [pcn], ['cd'])
        yield
        xs3 = xs_tok[0:ntok, :].rearrange("p (h d) -> p h d", h=8)
        tt(xdt[0:ntok, :].rearrange("p (h d) -> p h d", h=8), xs3,
           A.dts[0:ntok, ti, :].unsqueeze(2).to_broadcast([ntok, 8, 64]), ALU.mult, [xsn, 'dts'], [xdtn])
        tt(xw[0:ntok, :].rearrange("p (h d) -> p h d", h=8), xs3,
           A.wdt[0:ntok, :].unsqueeze(2).to_broadcast([ntok, 8, 64]), ALU.mult, [xsn, 'wdt'], [xwn])
        psc, pscn = bank()
        for g2 in range(2):
            mm(psc[0:ntok, g2 * 128:g2 * 128 + ntok], A.BTb[:, g2, cs], A.CTb[:, g2, cs], True, True, ['BT', 'CT'], [pscn])
        yield
        for g2 in range(2):
            tt(A.scm[0:ntok, g2, 0:ntok], psc[0:ntok, g2 * 128:g2 * 128 + ntok], linc, ALU.mult, [pscn, 'cst'], ['scm'])
        pY, pYn = bank(True)
        pYo, pYon = bank(True)
        if not sample:
            pSt, pStn = bank(True)
        for g2 in range(2):
            pseg, psegn = bank()
            for hh in range(4):
                h = g2 * 4 + hh
                ld = A.Ld[hh % 2]
                ts(ld[0:ntok, 0:ntok], ustr, A.dta[0:ntok, h:h + 1], None, ALU.mult, None, ['cst', 'dta'], [f'Ld{hh % 2}'])
                mm(pseg[0:ntok, hh * 128:hh * 128 + ntok], ld[0:ntok, 0:ntok], linc, True, True,
                   [f'Ld{hh % 2}', 'cst'], [psegn])
            yield
            segv = pseg[0:ntok, :].rearrange("p (h s) -> p h s", h=4)[:, :, 0:ntok]
            decv = dec[0:ntok, :].rearrange("p (h s) -> p h s", h=4)[:, :, 0:ntok]
            act(decv, segv, AF.Exp, [psegn], [decn])
            tt(A.MT[0:ntok, :, 0:ntok], decv, A.scm[0:ntok, g2, 0:ntok].unsqueeze(1).to_broadcast([ntok, 4, ntok]),
               ALU.mult, [decn, 'scm'], ['MT'])
            yield
            for hh in range(4):
                h = g2 * 4 + hh
                hsl = slice(h * 64, (h + 1) * 64)
                mm(pY[0:ntok, hsl], A.MT[0:ntok, hh, 0:ntok], xdt[0:ntok, hsl], True, True, ['MT', xdtn], [pYn])
                if not sample:
                    mm(pYo[0:ntok, hsl], A.CTb[:, g2, cs], hstb[:, h, :], True, True, ['CT', 'hstb'], [pYon])
                    mm(pSt[:, hsl], A.B_tok[0:ntok, g2 * 128:(g2 + 1) * 128], xw[0:ntok, hsl], True, True,
                       ['B_tok', xwn], [pStn])
        if sample:
            sample_state(l, A, L, pYo, pYon, xw, xwn)
        yield
        tt(yo_sb[0:ntok, :].rearrange("p (h d) -> p h d", h=8), pYo[0:ntok, :].rearrange("p (h d) -> p h d", h=8),
           A.ea[0:ntok, :].unsqueeze(2).to_broadcast([ntok, 8, 64]), ALU.mult, [pYon, 'ea'], [yon])
        tt(yv[0:ntok, :], pY[0:ntok, :], yo_sb[0:ntok, :], ALU.add, [pYn, yon], [yvn])
        release(pYn, pYon)
        tt(y2[0:ntok, :].rearrange("p (h d) -> p h d", h=8), xs3,
           PR('dsk', rows=ntok).unsqueeze(2).to_broadcast([ntok, 8, 64]), ALU.mult, [xsn, 'par'], [y2n])
        tt(yv[0:ntok, :], yv[0:ntok, :], y2[0:ntok, :], ALU.add, [yvn, y2n], [yvn])
        tt(yv[0:ntok, :], yv[0:ntok, :], A.zs[0:ntok, ti, :], ALU.mult, [yvn, 'zs'], [yvn])
        for g2 in range(2):
            act(y2[0:ntok, g2 * 256:(g2 + 1) * 256], yv[0:ntok, g2 * 256:(g2 + 1) * 256], AF.Square, [yvn], [y2n, 'ssq'],
                accum=A.ssq[0:ntok, g2:g2 + 1])
        act(A.rs2[0:ntok, :], A.ssq[0:ntok, :], AF.Ln, ['ssq', 'cb16'], ['rs2'], bias=epst[0:ntok, 0:1], scale=1.0 / 256)
        act(A.rs2[0:ntok, :], A.rs2[0:ntok, :], AF.Exp, ['rs2'], ['rs2'], scale=-0.5)
        for g2 in range(2):
            gsl = slice(g2 * 256, (g2 + 1) * 256)
            stt(y2[0:ntok, gsl], yv[0:ntok, gsl], A.rs2[0:ntok, g2:g2 + 1], PR('gn', rows=ntok, c0=g2 * 256, c1=(g2 + 1) * 256),
                ALU.mult, ALU.mult, [yvn, 'rs2', 'par', y2n], [y2n])
        yield
        pT, pTn = bank()
        for c in range(4):
            tr(pT[:, c * 128:c * 128 + ntok], y2[0:ntok, c * 128:(c + 1) * 128], idn, [y2n, 'cst'], [pTn])
        yield
        cp(A.yssT[:, :, cs], pT[:, :].rearrange("p (c t) -> p c t", c=4)[:, :, 0:ntok], [pTn], ['yssT'], eng='act')
        if not sample:
            for h in range(8):
                stt(hst[:, h, :], hst[:, h, :], A.cd[:, h:h + 1], pSt[:, h * 64:(h + 1) * 64], ALU.mult, ALU.add,
                    ['hst', 'cd', pStn], ['hst'])
            cp(hstb[:].rearrange("p a b -> p (a b)"), hst[:].rearrange("p a b -> p (a b)"), ['hst'], ['hstb'])
            release(pStn)

    def sample_state(l, A, L, pYo, pYon, xw, xwn):
        for b in range(NB):
            tt(L.CTm[:, b, :, :], A.CTb[:, :, 0:16], C('colB', c0=b * 16, c1=(b + 1) * 16).unsqueeze(1).to_broadcast([128, 2, 16]),
               ALU.mult, ['CT', 'cst'], ['CTm'])
        pc, pcn = bank()
        for b in range(NB):
            mm(pc[:, b * 8:(b + 1) * 8], C('selB', rows=16, c0=b * 128, c1=(b + 1) * 128), A.dta[0:16, :], True, True,
               ['dta', 'cst'], [pcn])
        act(L.cdS[:].rearrange("p a b -> p (a b)"), pc[:, 0:NB * 8], AF.Exp, [pcn], ['cdS'])
        for b in range(NB):
            h0 = L.h0[0]
            h0n = 'h0_0'
            dma(h0[:, :, :], sst_d[l, b].rearrange("h p n -> p h n"), (), [h0n])
            for half in range(2):
                pb, pn = bank()
                for k in range(4):
                    h = half * 4 + k
                    tr(pb[:, k * 64:(k + 1) * 64], h0[:, h, :], C('ident', rows=64, c1=64), [h0n, 'cst'], [pn])
                cp(L.h0Tb[:, b, half * 4:half * 4 + 4, :].rearrange("p a b -> p (a b)"), pb[:, 0:256], [pn], ['h0Tb'])
            ts(L.xwm[:, :], xw[0:16, :], C('rowB', rows=16, c0=b, c1=b + 1), None, ALU.mult, None, [xwn, 'cst'], ['xwm'])
            for half in range(2):
                st_, sn = next_stage()
                pb, pn = bank()
                for k in range(4):
                    h = half * 4 + k
                    g2 = h // 4
                    mm(pb[0:64, k * 128:(k + 1) * 128], L.xwm[:, h * 64:(h + 1) * 64], A.B_tok[0:16, g2 * 128:(g2 + 1) * 128],
                       True, True, ['xwm', 'B_tok'], [pn])
                for k in range(4):
                    h = half * 4 + k
                    stt(st_[0:64, k * 128:(k + 1) * 128], h0[:, h, :], L.cdS[0:64, b, h:h + 1], pb[0:64, k * 128:(k + 1) * 128],
                        ALU.mult, ALU.add, [h0n, 'cdS', pn], [sn])
                dma(sssm_o[l, b, half * 4:half * 4 + 4].rearrange("h p n -> p h n"),
                    st_[0:64, :].rearrange("p (h n) -> p h n", h=4), [sn], [])
        for h in range(8):
            g2 = h // 4
            for b in range(NB):
                mm(pYo[0:16, h * 64:(h + 1) * 64], L.CTm[:, b, g2, :], L.h0Tb[:, b, h, :], b == 0, b == NB - 1,
                   ['CTm', 'h0Tb'], [pYon])

    def proj_group(l, A, L, gi, n, tiles, sample):
        rms_to_hT(n, 'g1')
        ck('p_rms')
        c0 = gi * 512
        for which, base in (('sb', 0), ('fx', 4)):
            for j in range(2):
                for kind, cbi in (('q', base + j), ('k', base + 2 + j)):
                    wt, wn = load_w(l, f'fm{cbi}')
                    pb, pn = bank()
                    for kc in range(8):
                        mm(pb[:, 0:n], wt[:, kc * 128:(kc + 1) * 128], hT[:, kc, 0:n], kc == 0, kc == 7, [wn, 'hT'], [pn])
                    if kind == 'q':
                        act(A.QT[which][:, j, 0:n], pb[:, 0:n], AF.Copy, [pn], [f'QT{which}'], scale=0.125)
                    elif sample:
                        cp(L.KTs[which][:, j, 0:n], pb[:, 0:n], [pn], [f'KTs{which}'])
                    else:
                        cp(L.KT[which][:, j, c0:c0 + n], pb[:, 0:n], [pn], [f'KT{which}{gi}'])
        ck('p_qk')
        if not sample:
            wt, wn = load_w(l, 'f4')
            pb, pn = bank()
            for kc in range(8):
                mm(pb[0:4, 0:n], wt[:, kc * 4:(kc + 1) * 4], hT[:, kc, 0:n], kc == 0, kc == 7, [wn, 'hT'], [pn])
            ck('f4a')
            act(lfT[:, 0:n], pb[0:4, 0:n], AF.Exp, [pn, 'par2'], ['lfT'], bias=NBF[0:4, 0:1], scale=-1.0)
            act(lfT[:, 0:n], lfT[:, 0:n], AF.Ln, ['lfT', 'cb16'], ['lfT'], bias=one_t[0:4, 0:1])
            ts(lfT[:, 0:n], lfT[:, 0:n], -1.0, None, ALU.mult, None, ['lfT'], ['lfT'])
            ck('f4b')
            scan(cumT[:, 0:n], onesT[:, 0:n], lfT[:, 0:n], carryFT[:, 0:1], ['lfT', 'cb16', 'carryFT'], ['cumT'])
            ck('f4c')
            cp(carryFT[:, 0:1], cumT[:, n - 1:n], ['cumT'], ['carryFT'])
        ck('p_f4')
        need_tm = sample or (gi == NG - 1)
        if need_tm:
            ntk = 16 if sample else 128
            tcs = slice(0, 16) if sample else slice(384, 512)
            ptm = [bank(True), bank(True)]
        for c in range(8):
            wt, wn = load_w(l, f'fm{8 + c}')
            pb, pn = bank()
            for kc in range(8):
                mm(pb[:, 0:n], wt[:, kc * 128:(kc + 1) * 128], hT[:, kc, 0:n], kc == 0, kc == 7, [wn, 'hT'], [pn])
            if need_tm:
                tb, tbn = ptm[c // 4]
                for kc in range(8):
                    mm(tb[0:ntk, (c % 4) * 128:(c % 4 + 1) * 128], hT[:, kc, tcs], wt[:, kc * 128:(kc + 1) * 128],
                       kc == 0, kc == 7, [wn, 'hT'], [tbn])
            acc, accn = F(5)
            if sample:
                cp(L.XS[:, c, :, 3:7], pb[:, 0:16].rearrange("p (b t) -> p b t", b=NB), [pn], ['XS'])
                xv = lambda i: L.XS[:, c, :, i:i + 4]
                accv = acc[:, 0:16].rearrange("p (b t) -> p b t", b=NB)
                xres = 'XS'
            else:
                Xb = A.Xb[c % 2]
                xres = f'Xb{c % 2}'
                cp(Xb[:, 0:3], ccar[:, c, :], ['ccar'], [xres])
                cp(Xb[:, 3:3 + n], pb[:, 0:n], [pn], [xres], eng='act')
                cp(ccar[:, c, :], Xb[:, n:n + 3], [xres], ['ccar'])
                xv = lambda i: Xb[:, i:i + n]
                accv = acc[:, 0:n]
            ts(accv, xv(0), PR('cw', c0=c * 4, c1=c * 4 + 1), PR('cb', c0=c, c1=c + 1), ALU.mult, ALU.add,
               [xres, 'par'], [accn])
            for i in range(1, 4):
                stt(accv, xv(i), PR('cw', c0=c * 4 + i, c1=c * 4 + i + 1), accv, ALU.mult, ALU.add,
                    [xres, 'par', accn], [accn])
            if c < 4:
                act(A.xsT[:, c, 0:n], acc[:, 0:n], AF.Silu, [accn], ['xsT'])
            elif c < 6:
                act(A.BTf[:, c - 4, 0:n], acc[:, 0:n], AF.Silu, [accn], ['BT'])
                cp(A.BTb[:, c - 4, 0:n], A.BTf[:, c - 4, 0:n], ['BT'], ['BT'])
            else:
                act(A.CTb[:, c - 6, 0:n], acc[:, 0:n], AF.Silu, [accn], ['CT'])
        if need_tm:
            for half in range(2):
                st_, sn = next_stage()
                cp(st_[0:ntk, :], ptm[half][0][0:ntk, :], [ptm[half][1]], [sn], eng=('act' if half else 'dve'))
                if sample:
                    for b in range(NB):
                        dma(scv_o[l, b, :, half * 512:(half + 1) * 512], st_[4 * b + 1:4 * b + 4, :], [sn], [])
                else:
                    dma(pcv_o[l, :, half * 512:(half + 1) * 512], st_[125:128, :], [sn], [])
                release(ptm[half][1])
        ck('p_xbc')
        ntok = 16 if sample else 128
        for piece in ('tmD', 'tmC', 'tmA', 'tmB'):
            wt, wn = load_w(l, piece)
            W = {'tmA': 512, 'tmB': 512, 'tmC': 512, 'tmD': 12}[piece]
            for ti, tix in enumerate(tiles):
                cs = slice(ti * 128, ti * 128 + ntok)
                pb, pn = bank()
                for kc in range(8):
                    mm(pb[0:ntok, 0:W], hT[:, kc, cs], wt[:, kc * W:(kc + 1) * W], kc == 0, kc == 7, [wn, 'hT'], [pn])
                r0 = (tix * 128) if not sample else 0
                if piece in ('tmA', 'tmB'):
                    which = 'sb' if piece == 'tmA' else 'fx'
                    st_, sn = next_stage()
                    cp(st_[0:ntok, 0:512], pb[0:ntok, :], [pn], [sn])
                    if sample:
                        ko, vo = (ssbk_o, ssbv_o) if which == 'sb' else (sfxk_o, sfxv_o)
                        cp(L.Vnew[which][0:16, :], pb[0:16, 256:512], [pn], [f'Vnew{which}'], eng='act')
                    else:
                        ko, vo = (psbk_o, psbv_o) if which == 'sb' else (pfxk_o, pfxv_o)
                        cp(L.V[which][:, tix, :], pb[:, 256:512], [pn], [f'V{which}{tix}'], eng='act')
                    dma(ko[l, r0:r0 + ntok, :], st_[0:ntok, 0:256], [sn], [])
                    dma(vo[l, r0:r0 + ntok, :], st_[0:ntok, 256:512], [sn], [])
                elif piece == 'tmC':
                    act(A.zs[0:ntok, ti, :], pb[0:ntok, :], AF.Silu, [pn], ['zs'])
                else:
                    tt(tmp8[0:ntok, 0:4], pb[0:ntok, 0:4], PR('bf_rep', rows=ntok), ALU.add, [pn, 'par'], ['tmp8'])
                    act(tmp8[0:ntok, 0:4], tmp8[0:ntok, 0:4], AF.Exp, ['tmp8'], ['tmp8'], scale=-1.0)
                    act(tmp8[0:ntok, 0:4], tmp8[0:ntok, 0:4], AF.Ln, ['tmp8', 'cb16'], ['tmp8'], bias=one_t[0:ntok, 0:1])
                    ts(lf_tok[0:ntok, :], tmp8[0:ntok, 0:4], -1.0, None, ALU.mult, None, ['tmp8'], ['lf_tok'])
                    lo = slf_o if sample else plf_o
                    dma(lo[l, r0:r0 + ntok, :], lf_tok[0:ntok, :], ['lf_tok'], [])
                    if sample:
                        cp(L.lfS[0:16, :], lf_tok[0:16, :], ['lf_tok'], ['lfS'])
                    else:
                        pq, pqn = bank()
                        mm(pq[:, 0:4], C('linc'), lf_tok[:, :], True, True, ['lf_tok', 'cst'], [pqn])
                        mm(pq[:, 4:8], C('ones'), lf_tok[:, :], True, True, ['lf_tok', 'cst'], [pqn])
                        stt(L.negFk[:, tix, :], pq[:, 0:4], -1.0, carryF[:, :], ALU.mult, ALU.subtract, [pqn, 'carryF'],
                            [f'negFk{tix}'])
                        tt(carryF[:, :], carryF[:, :], pq[:, 4:8], ALU.add, [pqn, 'carryF'], ['carryF'])
                    tt(tmp8[0:ntok, 8:16], pb[0:ntok, 4:12], PR('dtb', rows=ntok), ALU.add, [pn, 'par'], ['tmp8'])
                    act(tmp8[0:ntok, 8:16], tmp8[0:ntok, 8:16], AF.Exp, ['tmp8'], ['tmp8'])
                    act(A.dts[0:ntok, ti, :], tmp8[0:ntok, 8:16], AF.Ln, ['tmp8', 'cb16'], ['dts'], bias=one_t[0:ntok, 0:1])
            ck('tm_' + piece)
        ck('p_tm')

        def ssd_all():
            for ti, tix in enumerate(tiles):
                yield from ssd_tile(l, A, L, ti, ntok, sample)
                yield
        if sample:
            for _ in ssd_all():
                pass
            return None
        return ssd_all()

    def attn_prompt(l, A, L, gi, bg=None):
        nkb = 4 * gi + 4
        n_iter = 4 * (nkb + 2) + 4 * (nkb + 1)
        per = max(1, -(-40 // n_iter))

        def tick():
            if bg is not None:
                for _ in range(per):
                    next(bg, None)
        spsum, spsn = F(2)
        spsumb, spsbn = H(0)
        for hc in range(2):
            pO, pOn = bank(True)
            for hp in range(2):
                h = hc * 2 + hp
                ps_ = slice(hp * 64, hp * 64 + 64)
                kbs = list(range(nkb - 1, -1, -1))
                st = {}

                def s0(idx):
                    kb = kbs[idx]
                    di = kb - 4 * gi
                    kres = f'KTsb{kb // 4}'
                    ksl = slice(kb * 128, (kb + 1) * 128)
                    pz, pzn = bank()
                    mm(pz[:, :], L.KT['sb'][ps_, hc, ksl], A.QT['sb'][ps_, hc, :], True, True, [kres, 'QTsb'], [pzn])
                    tb, tbn = F(idx % 2)
                    sp_, spn = H(1 + idx % 3)
                    act(tb[:], pz[:, :], AF.Exp, [pzn], [tbn])
                    act(sp_[:], tb[:], AF.Ln, [tbn, 'cb16'], [spn], bias=one_t[:, 0:1])
                    if di >= 0:
                        tt(sp_[:], sp_[:], msb(di), ALU.mult, [spn] + CB, [spn])
                    st[idx] = (kb, di, kres, ksl, sp_, spn)

                def s1(idx):
                    kb, di, kres, ksl, sp_, spn = st[idx]
                    w_, wn_ = H(4 + idx % 3)
                    pe_, pen = bank()
                    mm(pe_[:, :], L.KT['sb'][ps_, hc, ksl], A.QT['sb'][ps_, hc, :], True, False, [kres, 'QTsb'], [pen])
                    mm(pe_[:, :], nuincb[:], sp_[:], False, idx == 0, [spn] + CB, [pen])
                    if idx > 0:
                        mm(pe_[:, :], nonesb[:], spsumb[:], False, True, [spsbn] + CB, [pen])
                    act(w_[:], pe_[:, :], AF.Exp, [pen], [wn_])
                    if di >= 0:
                        tt(w_[:], w_[:], msb(di), ALU.mult, [wn_] + CB, [wn_])
                    if kb > 0:
                        if idx == 0:
                            cp(spsum[:], sp_[:], [spn], [spsn])
                        else:
                            tt(spsum[:], spsum[:], sp_[:], ALU.add, [spsn, spn], [spsn])
                        cp(spsumb[:], spsum[:], [spsn], [spsbn])
                    st[idx] = st[idx] + (w_, wn_)

                def s2(idx):
                    kb = st[idx][0]
                    w_, wn_ = st[idx][6], st[idx][7]
                    mm(pO[ps_, :], L.V['sb'][:, kb, h * 64:(h + 1) * 64], w_[:], idx == 0, kb == 0, [f'Vsb{kb}', wn_], [pOn])

                for it in range(nkb + 2):
                    tick()
                    if it < nkb:
                        s0(it)
                    if 0 <= it - 1 < nkb:
                        s1(it - 1)
                    if 0 <= it - 2 < nkb:
                        s2(it - 2)
            cp(A.yT['sb'][:, hc, :], pO[:, :], [pOn], ['ysbT'])
            release(pOn)
        fq, fqn = F(4)
        rec, recn = F(3)
        for hc in range(2):
            pN, pNn = bank(True)
            pD, pDn = bank(True)
            for hp in range(2):
                h = hc * 2 + hp
                ps_ = slice(hp * 64, hp * 64 + 64)
                pq, pqn = bank()
                mm(pq[:, :], sel4[:, h, :], cumT[:, :], True, True, ['cb16', 'cumT'], [pqn])
                cp(fq[:], pq[:, :], [pqn], [fqn], eng='act')
                st = {}

                def f0(kb):
                    di = kb - 4 * gi
                    kres = f'KTfx{kb // 4}'
                    ksl = slice(kb * 128, (kb + 1) * 128)
                    pz, pzn = bank()
                    mm(pz[:, :], L.KT['fx'][ps_, hc, ksl], A.QT['fx'][ps_, hc, :], True, True, [kres, 'QTfx'], [pzn])
                    tb, tbn = F(kb % 2)
                    w_, wn_ = H(4 + kb % 3)
                    stt(tb[:], pz[:, :], L.negFk[:, kb, h:h + 1], fq[:], ALU.add, ALU.add, [pzn, f'negFk{kb}', fqn], [tbn])
                    if di >= 0:
                        ts(tb[:], tb[:], 80.0, None, ALU.min, None, [tbn], [tbn])
                    act(w_[:], tb[:], AF.Exp, [tbn], [wn_])
                    if di >= 0:
                        tt(w_[:], w_[:], mfx(di), ALU.mult, [wn_] + CB, [wn_])
                    st[kb] = (w_, wn_)

                def f1(kb):
                    w_, wn_ = st[kb]
                    mm(pN[ps_, :], L.V['fx'][:, kb, h * 64:(h + 1) * 64], w_[:], kb == 0, kb == nkb - 1, [f'Vfx{kb}', wn_], [pNn])
                    mm(pD[ps_, :], onesb[:, 0:64], w_[:], kb == 0, kb == nkb - 1, [wn_] + CB, [pDn])

                for it in range(nkb + 1):
                    tick()
                    if it < nkb:
                        f0(it)
                    if 0 <= it - 1 < nkb:
                        f1(it - 1)
                recip(rec[ps_, :], pD[ps_, :], [pDn], [recn])
                tt(A.yT['fx'][ps_, hc, :], pN[ps_, :], rec[ps_, :], ALU.mult, [pNn, recn], ['yfxT'])
            release(pNn, pDn)

    def merge(l, A, n):
        for dc in range(8):
            pbr = []
            for bi, (nm, ysrc, nk, yres) in enumerate((('sbo', A.yT['sb'], 2, 'ysbT'), ('sso', A.yssT, 4, 'yssT'),
                                                        ('fxo', A.yT['fx'], 2, 'yfxT'))):
                wt, wn = load_w(l, f'fm{16 + bi * 8 + dc}')
                pg, pgn = bank()
                for kc in range(8):
                    mm(pg[:, 0:n], wt[:, kc * 128:(kc + 1) * 128], hT[:, kc, 0:n], kc == 0, kc == 7, [wn, 'hT'], [pgn])
                g_, gn_ = F(bi)
                act(g_[:, 0:n], pg[:, 0:n], AF.Sigmoid, [pgn], [gn_])
                wt, wn = load_w(l, f'{nm}{dc}')
                pb, pn = bank(True)
                for kc in range(nk):
                    mm(pb[:, 0:n], wt[:, kc * 128:(kc + 1) * 128], ysrc[:, kc, 0:n], kc == 0, kc == nk - 1, [wn, yres], [pn])
                pbr.append((pb, pn))
            ma, man = F(3)
            mb, mbn = F(4)
            tt(ma[:, 0:n], pbr[0][0][:, 0:n], Fp[0][:, 0:n], ALU.mult, [pbr[0][1], 'F0'], [man])
            tt(mb[:, 0:n], pbr[1][0][:, 0:n], Fp[1][:, 0:n], ALU.mult, [pbr[1][1], 'F1'], [mbn])
            tt(ma[:, 0:n], ma[:, 0:n], mb[:, 0:n], ALU.add, [man, mbn], [man])
            tt(mb[:, 0:n], pbr[2][0][:, 0:n], Fp[2][:, 0:n], ALU.mult, [pbr[2][1], 'F2'], [mbn])
            tt(A.mixT[:, dc, 0:n], ma[:, 0:n], mb[:, 0:n], ALU.add, [man, mbn], ['mixT'])
            release(*[x[1] for x in pbr])
        for dc in range(8):
            wt, wn = load_w(l, f'wo{dc}')
            pb, pn = bank()
            for kc in range(8):
                mm(pb[:, 0:n], wt[:, kc * 128:(kc + 1) * 128], A.mixT[:, kc, 0:n], kc == 0, kc == 7, [wn, 'mixT'], [pn])
            tt(xg[:, dc, 0:n], xg[:, dc, 0:n], pb[:, 0:n], ALU.add, ['xg', pn], ['xg'])

    def ffn(l, Dd, n):
        rms_to_hT(n, 'g2')
        for fc in range(32):
            wt, wn = load_w(l, f'up{fc}')
            pb, pn = bank()
            for kc in range(8):
                mm(pb[:, 0:n], wt[:, kc * 128:(kc + 1) * 128], hT[:, kc, 0:n], kc == 0, kc == 7, [wn, 'hT'], [pn])
            tb, tbn = F(fc % 2)
            act(tb[:, 0:n], pb[:, 0:n], AF.Relu, [pn], [tbn])
            tt(Dd.uT[:, fc, 0:n], tb[:, 0:n], tb[:, 0:n], ALU.mult, [tbn], ['uT'])
        for dc in range(8):
            wt, wn = load_w(l, f'dn{dc}')
            pb, pn = bank()
            for kc in range(32):
                mm(pb[:, 0:n], wt[:, kc * 128:(kc + 1) * 128], Dd.uT[:, kc, 0:n], kc == 0, kc == 31, [wn, 'uT'], [pn])
            tt(xg[:, dc, 0:n], xg[:, dc, 0:n], pb[:, 0:n], ALU.add, ['xg', pn], ['xg'])

    def final_out(Dd, n, tiles, sample):
        rms_to_hT(n, 'gf', out_f32=Dd.finT, ores='finT')
        ntok = 16 if sample else 128
        for ti, tix in enumerate(tiles):
            cs = slice(ti * 128, ti * 128 + ntok)
            for half in range(2):
                st_, sn = next_stage()
                pb, pn = bank()
                for k in range(4):
                    kc = half * 4 + k
                    tr(pb[0:ntok, k * 128:(k + 1) * 128], Dd.finT[:, kc, cs], ident, ['finT', 'cst'], [pn])
                cp(st_[0:ntok, :], pb[0:ntok, :], [pn], [sn], eng=('act' if half else 'dve'))
                if sample:
                    dma(ys_o[:, half * 512:(half + 1) * 512], st_[0:16, :], [sn], [])
                else:
                    dma(yp_o[tix * 128:(tix + 1) * 128, half * 512:(half + 1) * 512], st_[:, :], [sn], [])

    def gather(out_tile, res, src3d, l, b, j):
        col = b * NPG + j
        flat = src3d.rearrange("g r c -> (g r) c")
        P.dma('pool', lambda e: e.indirect_dma_start(out=out_tile[:], out_offset=None, in_=flat,
                                                     in_offset=bass.IndirectOffsetOnAxis(ap=idx_t[:, col:col + 1], axis=0),
                                                     element_offset=l * NPOOL * 128 * 512),
              ['idx_t'], [res])

    def attn_decode(l, A, L):
        NC_ = NPG * 8
        assert NC_ <= 512
        WH = ('sb', 'fx')
        for wi, which in enumerate(WH):
            memset(L.KTn[which][:].rearrange("p a b -> p (a b)"), 0.0, [f'KTn{which}'])
        memset(L.VnP[:], 0.0, ['VnP'])
        memset(L.cn4[:], 0.0, ['cn4'])
        S1, S2, S3 = L.S1, L.S2, L.S3
        Sn1, Sn2, Wn = L.Sn1, L.Sn2, L.Wn
        Sn1v = Sn1[:, :].rearrange("p (a b) -> p a b", a=2)
        Wnv = Wn[:, :].rearrange("p (a b) -> p a b", a=2)

        def jcol(which, j):
            return (NPG - 1 - j) if which == 'sb' else j
        for b in range(NB):
            qc = slice(b * 4, b * 4 + 4)
            for which in WH:
                Qb = L.Qbd[which]
                memset(Qb[:].rearrange("p a b -> p (a b)"), 0.0, [f'Qbd{which}'])
                for hc in range(2):
                    for hp in range(2):
                        ps_ = slice(hp * 64, hp * 64 + 64)
                        cp(Qb[ps_, hc, hp * 4:hp * 4 + 4], A.QT[which][ps_, hc, qc], [f'QT{which}', f'Qbd{which}'], [f'Qbd{which}'])
                    cp(L.KTn[which][:, hc, 0:4], L.KTs[which][:, hc, qc], [f'KTs{which}', f'KTn{which}'], [f'KTn{which}'])
            pS = {w_: [bank(True), bank(True)] for w_ in WH}
            for j in range(NPG):
                kt_, ktn = L.kpg[j % 4], f'kpg{j % 4}'
                gather(kt_, ktn, ckk_d, l, b, j)
                pt_, ptn = bank()
                for q4 in range(4):
                    tr(pt_[:, q4 * 128:(q4 + 1) * 128], kt_[:, q4 * 128:(q4 + 1) * 128], ident, [ktn, 'cst'], [ptn])
                kb_, kbn = L.ktb[j % 3], f'ktb{j % 3}'
                cp(kb_[:], pt_[:, :], [ptn], [kbn], eng=('act' if j % 2 else 'dve'))
                for wi, which in enumerate(WH):
                    jc = jcol(which, j)
                    for hc in range(2):
                        mm(pS[which][hc][0][:, jc * 8:(jc + 1) * 8], kb_[:, wi * 256 + hc * 128:wi * 256 + (hc + 1) * 128],
                           L.Qbd[which][:, hc, :], True, True, [kbn, f'Qbd{which}'], [pS[which][hc][1]])
            pSn, pSnn = bank(True)
            for wi, which in enumerate(WH):
                for hc in range(2):
                    mm(pSn[:, wi * 16 + hc * 8:wi * 16 + (hc + 1) * 8], L.KTn[which][:, hc, :], L.Qbd[which][:, hc, :], True, True,
                       [f'KTn{which}', f'Qbd{which}'], [pSnn])
            which = 'fx'
            mnew2 = C('mnfx').unsqueeze(1).to_broadcast([128, 2, 8])
            lfp, lfw, tot64, Fkd = L.lfp, L.lfw, L.tot64, L.Fkd
            P.dma('pool', lambda e, b=b: e.indirect_dma_start(
                out=lfp[0:NPG, :], out_offset=None, in_=clf_d.rearrange("l p c -> (l p) c"),
                in_offset=bass.IndirectOffsetOnAxis(ap=pcol[0:NPG, b:b + 1], axis=0),
                element_offset=l * NPOOL * 512), ['pcol'], ['lfp'])
            for h in range(4):
                colv = slice(h, 512, 4)
                scan(lfw[0:NPG, colv], ones_f[0:NPG, 0:128], lfp[0:NPG, colv], 0.0, ['lfp', 'cb16'], ['lfw'])
            cp(tot64[0:NPG, :], lfw[0:NPG, 508:512], ['lfw'], ['tot64'])
            pq, pqn = bank()
            mm(pq[0:NPG, 0:4], C('linc', rows=NPG, c1=NPG), tot64[0:NPG, :], True, True, ['tot64', 'cst'], [pqn])
            mm(pq[:, 8:12], C('ones', rows=NPG), tot64[0:NPG, :], True, True, ['tot64', 'cst'], [pqn])
            tt(tot64[0:NPG, :], pq[0:NPG, 0:4], tot64[0:NPG, :], ALU.subtract, [pqn, 'tot64'], ['tot64'])
            lfw3 = lfw[0:NPG, :].rearrange("p (r h) -> p r h", h=4)
            tt(lfw3, lfw3, tot64[0:NPG, :].unsqueeze(1).to_broadcast([NPG, 128, 4]), ALU.add, ['lfw', 'tot64'], ['lfw'])
            cp(L.ftot[:, :], pq[:, 8:12], [pqn], ['ftot'])
            for h in range(4):
                pt_, ptn = bank()
                tr(pt_[:, 0:NPG], lfw[0:NPG, h:512:4], C('ident', rows=NPG, c1=NPG), ['lfw', 'cst'], [ptn])
                ts(Fkd[:, 0:NPG, h], pt_[:, 0:NPG], -1.0, L.ftot[:, h:h + 1], ALU.mult, ALU.add, [ptn, 'ftot'], ['Fkd'])
            for hc in range(2):
                sl = slice(hc * NC_, (hc + 1) * NC_)
                o3 = S3[:, sl].rearrange("p (j a q) -> p j a q", a=2, q=4)
                i3 = pS[which][hc][0][:, 0:NC_].rearrange("p (j a q) -> p j a q", a=2, q=4)
                f3 = Fkd[:, 0:NPG, hc * 2:hc * 2 + 2].unsqueeze(3).to_broadcast([128, NPG, 2, 4])
                tt(o3, i3, f3, ALU.add, [pS[which][hc][1], 'Fkd'], ['S1'])
            release(pS['fx'][0][1], pS['fx'][1][1])
            act(S3[:, 0:2 * NC_], S3[:, 0:2 * NC_], AF.Exp, ['S1'], ['S1'])
            cp(L.S3b[which][:, 0:2 * NC_], S3[:, 0:2 * NC_], ['S1'], [f'S3b{which}'])
            pn_, pnn = bank()
            mm(pn_[0:16, 0:4], C('lincS', rows=16), L.lfS[0:16, :], True, True, ['lfS', 'cst'], [pnn])
            cp(L.cnS[0:16, :], pn_[0:16, 0:4], [pnn], ['cnS'])
            mm(pn_[0:4, 8:12], C('ident', rows=16, c0=b * 4, c1=b * 4 + 4), L.cnS[0:16, :], True, True, ['cnS', 'cst'], [pnn])
            cp(L.cn4[0:4, :], pn_[0:4, 8:12], [pnn, 'cn4'], ['cn4'])
            o3 = Wn[:, :].rearrange("p (h q) -> p h q", q=4)
            i3 = pSn[:, 16:32].rearrange("p (h q) -> p h q", q=4)
            tt(o3, i3, L.cn4[:, :].unsqueeze(2).to_broadcast([128, 4, 4]), ALU.subtract, [pSnn, 'cn4'], ['Wn'])
            act(Wn[:, :], Wn[:, :], AF.Exp, ['Wn'], ['Wn'])
            tt(Wnv, Wnv, mnew2, ALU.mult, ['Wn', 'cst'], ['Wn'])
            cp(L.Wnb[which][:], Wn[:, :], ['Wn'], [f'Wnb{which}'])
            for hc in range(2):
                sl = slice(hc * NC_, (hc + 1) * NC_)
                P.op('dve', lambda e, sl=sl, hc=hc: e.tensor_reduce(
                    L.ydec[:, hc, :], S3[:, sl].rearrange("p (j c) -> p c j", c=8), AX.X, ALU.add), ['S1'], ['ydec'])
                tt(L.ydec[:, hc, :], L.ydec[:, hc, :], Wn[:, hc * 8:(hc + 1) * 8], ALU.add, ['ydec', 'Wn'], ['ydec'])
                pd_, pdn = bank()
                mm(pd_[:, 0:8], C('ones'), L.ydec[:, hc, :], True, True, ['ydec', 'cst'], [pdn])
                recip(L.rden[:, hc, :], pd_[:, 0:8], [pdn], ['rden'])
            which = 'sb'
            mnew2 = C('mnsb').unsqueeze(1).to_broadcast([128, 2, 8])
            for hc in range(2):
                act(S1[:, hc * NC_:(hc + 1) * NC_], pS[which][hc][0][:, 0:NC_], AF.Exp, [pS[which][hc][1]], ['S1'])
            act(S1[:, 0:2 * NC_], S1[:, 0:2 * NC_], AF.Ln, ['S1', 'cb16'], ['S1'], bias=one_t[:, 0:1])
            act(Sn1[:, :], pSn[:, 0:16], AF.Exp, [pSnn], ['Sn1'])
            act(Sn1[:, :], Sn1[:, :], AF.Ln, ['Sn1', 'cb16'], ['Sn1'], bias=one_t[:, 0:1])
            tt(Sn1v, Sn1v, mnew2, ALU.mult, ['Sn1', 'cst'], ['Sn1'])
            pcs = [bank(True), bank(True)]
            for hc in range(2):
                mm(pcs[hc][0][:, 0:NC_], C('uincr'), S1[:, hc * NC_:(hc + 1) * NC_], True, True, ['S1', 'cst'], [pcs[hc][1]])
            pn_, pnn = bank(True)
            mm(pn_[:, 0:16], C('uincr'), Sn1[:, :], True, True, ['Sn1', 'cst'], [pnn])
            mm(pn_[:, 16:32], C('ones'), Sn1[:, :], True, True, ['Sn1', 'cst'], [pnn])
            cp(Sn2[:, :], pn_[:, 16:32], [pnn], ['Sn2'])
            for hc in range(2):
                ptt_, pttn = bank()
                mm(ptt_[:, 0:NC_], C('ones'), S1[:, hc * NC_:(hc + 1) * NC_], True, True, ['S1', 'cst'], [pttn])
                cp(S2[:, hc * NC_:(hc + 1) * NC_], ptt_[:, 0:NC_], [pttn], ['S2'], eng='act')
            for hc in range(2):
                for c8 in range(8):
                    colv = slice(hc * NC_ + c8, (hc + 1) * NC_, 8)
                    scan(S3[:, colv], ones_f[:, 0:NPG], S2[:, colv], Sn2[:, hc * 8 + c8:hc * 8 + c8 + 1],
                         ['S2', 'Sn2', 'cb16'], ['S1'])
            for hc in range(2):
                sl = slice(hc * NC_, (hc + 1) * NC_)
                tt(S3[:, sl], S3[:, sl], S2[:, sl], ALU.subtract, ['S1', 'S2'], ['S1'])
                tt(S3[:, sl], S3[:, sl], pcs[hc][0][:, 0:NC_], ALU.add, ['S1', pcs[hc][1]], ['S1'])
                tt(S3[:, sl], pS[which][hc][0][:, 0:NC_], S3[:, sl], ALU.subtract, [pS[which][hc][1], 'S1'], ['S1'])
            act(S3[:, 0:2 * NC_], S3[:, 0:2 * NC_], AF.Exp, ['S1'], ['S1'])
            cp(L.S3b[which][:, 0:2 * NC_], S3[:, 0:2 * NC_], ['S1'], [f'S3b{which}'])
            cp(Sn1[:, :], pn_[:, 0:16], [pnn], ['Sn1'])
            tt(Wn[:, :], pSn[:, 0:16], Sn1[:, :], ALU.subtract, [pSnn, 'Sn1'], ['Wn'])
            act(Wn[:, :], Wn[:, :], AF.Exp, ['Wn'], ['Wn'])
            tt(Wnv, Wnv, mnew2, ALU.mult, ['Wn', 'cst'], ['Wn'])
            cp(L.Wnb[which][:], Wn[:, :], ['Wn'], [f'Wnb{which}'])
            release(pcs[0][1], pcs[1][1], pnn, pS['sb'][0][1], pS['sb'][1][1], pSnn)
            pO = {w_: [bank(True), bank(True)] for w_ in WH}
            for j in range(NPG):
                vt_, vtn = L.vpg[j % 4], f'vpg{j % 4}'
                gather(vt_, vtn, cvv_d, l, b, j)
                vb_, vbn = L.ktb[j % 3], f'ktb{j % 3}'
                cp(vb_[:], vt_[:], [vtn], [vbn], eng=('act' if j % 2 else 'dve'))
                for wi, which in enumerate(WH):
                    jc = jcol(which, j)
                    for hc in range(2):
                        mm(pO[which][hc][0][:, 0:8], vb_[:, wi * 256 + hc * 128:wi * 256 + (hc + 1) * 128],
                           L.S3b[which][:, hc * NC_ + jc * 8:hc * NC_ + jc * 8 + 8],
                           j == 0, False, [vbn, f'S3b{which}'], [pO[which][hc][1]])
            for wi, which in enumerate(WH):
                dma(L.VnP[0:4, :], L.Vnew[which][b * 4:b * 4 + 4, :], [f'Vnew{which}', 'VnP'], ['VnP'])
                cp(L.VnPb[:], L.VnP[:], ['VnP'], ['VnPb'])
                for hc in range(2):
                    mm(pO[which][hc][0][:, 0:8], L.VnPb[:, hc * 128:(hc + 1) * 128], L.Wnb[which][:, hc * 8:(hc + 1) * 8], False, True,
                       ['VnPb', f'Wnb{which}'], [pO[which][hc][1]])
                for hc in range(2):
                    for hp in range(2):
                        ps_ = slice(hp * 64, hp * 64 + 64)
                        src = pO[which][hc][0][ps_, hp * 4:hp * 4 + 4]
                        if which == 'fx':
                            tt(A.yT['fx'][ps_, hc, qc], src, L.rden[ps_, hc, hp * 4:hp * 4 + 4], ALU.mult,
                               [pO[which][hc][1], 'rden'], ['yfxT'])
                        else:
                            cp(A.yT['sb'][ps_, hc, qc], src, [pO[which][hc][1]], ['ysbT'])
                release(pO[which][0][1], pO[which][1][1])

    def alloc_A(S, sample=False):
        A = Bag()
        if not sample:
            A.sF = [S.sb(f"sF{i}", [128, 512]) for i in range(5)]
            A.sH = [S.sb(f"sH{i}", [128, 512], BF16) for i in range(2)]
        A.QT = {'sb': S.sb("QTsb", [128, 2, 512], BF16), 'fx': S.sb("QTfx", [128, 2, 512], BF16)}
        A.xsT = S.sb("xsT", [128, 4, 512])
        A.BTf = S.sb("BTf", [128, 2, 512])
        A.BTb = S.sb("BTb", [128, 2, 512], BF16)
        A.CTb = S.sb("CTb", [128, 2, 512], BF16)
        A.zs = S.sb("zs", [128, 4, 512])
        A.dts = S.sb("dts", [128, 4, 8])
        A.yT = {'sb': S.sb("ysbT", [128, 2, 512], BF16), 'fx': S.sb("yfxT", [128, 2, 512], BF16)}
        A.yssT = S.sb("yssT", [128, 4, 512], BF16)
        A.mixT = S.sb("mixT", [128, 8, 512], BF16)
        A.Xb = [S.sb(f"Xb{i}", [128, 515]) for i in range(2)]
        A.B_tok = S.sb("B_tok", [128, 256], BF16)
        for nm in ('dta', 'acs', 'ea', 'te', 'wdt', 'cd'):
            setattr(A, nm, S.sb(nm, [128, 8]))
        A.scm = S.sb("scm", [128, 2, 128])
        A.Ld = [S.sb(f"Ld{i}", [128, 128]) for i in range(2)]
        A.MT = S.sb("MT", [128, 4, 128], BF16)
        A.ssq = S.sb("ssq", [128, 2])
        A.rs2 = S.sb("rs2", [128, 2])
        return A

    groups = [(gi, 512, list(range(gi * 4, gi * 4 + 4)), False) for gi in range(NG)]
    groups.append((NG, NS, [0], True))
    try:
        cast_layer(0)
        for l in range(DEPTH):
            dma(par[:], par_d[l], (), ['par'])
            act(arep[:], PR('alog'), AF.Exp, ['par'], ['par2'])
            ts(arep[:], arep[:], -1.0, None, ALU.mult, None, ['par2'], ['par2'])
            ts(NBF[:], PR('bf_col'), -1.0, None, ALU.mult, None, ['par'], ['par2'])
            memset(carryF[:], 0.0, ['carryF'])
            memset(carryFT[:], 0.0, ['carryFT'])
            memset(ccar[:].rearrange("p a b -> p (a b)"), 0.0, ['ccar'])
            memset(hst[:].rearrange("p a b -> p (a b)"), 0.0, ['hst'])
            memset(hstb[:].rearrange("p a b -> p (a b)"), 0.0, ['hstb'])
            LS = Scope()
            L = Bag()
            L.KT = {'sb': LS.sb("KTsb", [128, 2, T], BF16), 'fx': LS.sb("KTfx", [128, 2, T], BF16)}
            L.V = {'sb': LS.sb("Vsb", [128, NT, 256], BF16), 'fx': LS.sb("Vfx", [128, NT, 256], BF16)}
            L.negFk = LS.sb("negFk", [128, NT, 4])
            for (gi, n, tiles, sample) in groups:
                if sample:
                    LS.close()
                    LS = Scope()
                    L = Bag()
                    L.KTs = {'sb': LS.sb("KTssb", [128, 2, 16], BF16), 'fx': LS.sb("KTsfx", [128, 2, 16], BF16)}
                    L.Vnew = {'sb': LS.sb("Vnsb", [16, 256]), 'fx': LS.sb("Vnfx", [16, 256])}
                    L.lfS = LS.sb("lfS", [16, 4])
                    L.XS = LS.sb("XS", [128, 8, NB, 7])
                    L.h0 = [LS.sb(f"h0_{i}", [64, 8, 128]) for i in range(1)]
                    L.h0Tb = LS.sb("h0Tb", [128, NB, 8, 64], BF16)
                    L.CTm = LS.sb("CTm", [128, NB, 2, 16], BF16)
                    L.xwm = LS.sb("xwm", [16, 512], BF16)
                    L.cdS = LS.sb("cdS", [128, NB, 8])
                    L.kpg = [LS.sb(f"kpg{i}", [128, 512]) for i in range(4)]
                    L.vpg = [LS.sb(f"vpg{i}", [128, 512]) for i in range(4)]
                    L.ktb = [LS.sb(f"ktb{i}", [128, 512], BF16) for i in range(3)]
                    L.Qbd = {w_: LS.sb(f"Qbd{w_}", [128, 2, 8], BF16) for w_ in ('sb', 'fx')}
                    L.KTn = {w_: LS.sb(f"KTn{w_}", [128, 2, 128], BF16) for w_ in ('sb', 'fx')}
                    L.S1 = LS.sb("S1", [128, 1024])
                    L.S2 = LS.sb("S2", [128, 1024])
                    L.S3 = L.S1
                    L.S3b = {w_: LS.sb(f"S3b{w_}", [128, 1024], BF16) for w_ in ('sb', 'fx')}
                    L.VnPb = LS.sb("VnPb", [128, 256], BF16)
                    L.Wnb = {w_: LS.sb(f"Wnb{w_}", [128, 16], BF16) for w_ in ('sb', 'fx')}
                    L.Sn1 = LS.sb("Sn1", [128, 16])
                    L.Sn2 = LS.sb("Sn2", [128, 16])
                    L.Wn = LS.sb("Wn", [128, 16])
                    L.lfp = LS.sb("lfp", [64, 512])
                    L.lfw = LS.sb("lfw", [64, 512])
                    L.Fkd = LS.sb("Fkd", [128, 64, 4])
                    L.tot64 = LS.sb("tot64", [64, 4])
                    L.ftot = LS.sb("ftot", [128, 4])
                    L.ydec = LS.sb("ydec", [128, 2, 8])
                    L.rden = LS.sb("rden", [128, 2, 8])
                    L.VnP = LS.sb("VnP", [128, 256])
                    L.cnS = LS.sb("cnS", [16, 4])
                    L.cn4 = LS.sb("cn4", [128, 4])
                if l == 0:
                    if sample:
                        load_x(xs_d, NS, 0)
                    else:
                        for ti in range(4):
                            load_x(xp_d[(gi * 4 + ti) * 128:(gi * 4 + ti + 1) * 128, :], 128, ti * 128)
                else:
                    dma(xg[:].rearrange("p a b -> p (a b)"), xscr[gi], [f'xscr{gi}'], ['xg'])
                ck(f'load{l}.{gi}')
                AS = Scope()
                A = alloc_A(AS, sample)
                if sample:
                    dma(L.XS[:, :, :, 0:3], scv_d[l], (), ['XS'])
                bg = proj_group(l, A, L, gi, n, tiles, sample)
                ck(f'proj{l}.{gi}')
                if gi == 0 and l + 1 < DEPTH:
                    cast_layer(l + 1)
                if sample:
                    attn_decode(l, A, L)
                else:
                    attn_prompt(l, A, L, gi, bg)
                    for _ in bg:
                        pass
                    if gi == NG - 1:
                        for half in range(2):
                            st_, sn = next_stage()
                            pb, pn = bank()
                            for k in range(4):
                                h = half * 4 + k
                                tr(pb[0:64, k * 128:(k + 1) * 128], hst[:, h, :], ident, ['hst', 'cst'], [pn])
                            cp(st_[0:64, :], pb[0:64, :], [pn], [sn])
                            dma(pssm_o[l, half * 4:half * 4 + 4].rearrange("h p n -> p h n"),
                                st_[0:64, :].rearrange("p (h n) -> p h n", h=4), [sn], [])
                ck(f'attn{l}.{gi}')
                merge(l, A, n)
                ck(f'merge{l}.{gi}')
                AS.close()
                DS = Scope()
                Dd = Bag()
                Dd.uT = DS.sb("uT", [128, 32, 512], BF16)
                if l == DEPTH - 1:
                    Dd.finT = DS.sb("finT", [128, 8, 512])
                ffn(l, Dd, n)
                ck(f'ffn{l}.{gi}')
                if l == DEPTH - 1:
                    final_out(Dd, n, tiles, sample)
                else:
                    dma(xscr[gi], xg[:].rearrange("p a b -> p (a b)"), ['xg'], [f'xscr{gi}'])
                DS.close()
            LS.close()
    except _Stop:
        pass
    for sc in reversed(list(open_scopes)):
        if sc is not G:
            sc.close()
    P.emit()
    G.es.close()
    pes.close()
    return nc


def host_inputs(inp, cfg, core, kk, vv):
    T, NB, NPG, NPOOL, DEPTH = cfg['T'], cfg['NB'], cfg['NPG'], cfg['NPOOL'], cfg['DEPTH']
    b0 = core * NB
    m = {}
    m['xp'] = np.ascontiguousarray(inp['x_prompt'][core])
    m['xs'] = np.ascontiguousarray(inp['x_sample'][b0:b0 + NB].reshape(NB * 4, D))
    m['ckk'] = kk
    m['cvv'] = vv
    m['clf'] = inp['cache_fox_logf'].reshape(DEPTH, NPOOL, 512)
    m['sst'] = np.ascontiguousarray(inp['state_ssm'][:, b0:b0 + NB])
    sc = inp['state_conv'][:, b0:b0 + NB]
    m['scv'] = np.ascontiguousarray(sc.reshape(DEPTH, NB, 3, 8, 128).transpose(0, 4, 3, 1, 2))
    m['ptab'] = np.ascontiguousarray(inp['page_table'][b0:b0 + NB].reshape(1, NB * NPG).astype(np.int32))
    return m


_CACHE = {}


def run(inp, cfg, ncores, stop_at=None):
    inp = {k: np.asarray(v) for k, v in inp.items()}
    DEPTH = cfg['DEPTH']
    key = tuple(sorted(cfg.items()))
    if key not in _CACHE:
        _CACHE[key] = build(cfg, stop_at)
    nc = _CACHE[key]
    wall = prep_weights(inp, DEPTH)
    par = prep_params(inp, DEPTH)
    cst = prep_consts()
    cmask = prep_cmask()
    NPOOL_ = cfg['NPOOL']
    kk = np.concatenate([inp['cache_sb_k'].reshape(DEPTH * NPOOL_, 128, 256),
                         inp['cache_fox_k'].reshape(DEPTH * NPOOL_, 128, 256)], axis=2)
    vv = np.concatenate([inp['cache_sb_v'].reshape(DEPTH * NPOOL_, 128, 256),
                         inp['cache_fox_v'].reshape(DEPTH * NPOOL_, 128, 256)], axis=2)
    in_maps = []
    for c in range(ncores):
        m = host_inputs(inp, cfg, c, kk, vv)
        m['wall'] = wall
        m['par'] = par
        m['cst'] = cst
        m['cmask'] = cmask
        in_maps.append(m)
    res = run_bass_kernel_spmd(nc, in_maps, core_ids=list(range(ncores)))
    R = res.results
    T, NB = cfg['T'], cfg['NB']

    def cat(name, shape_per_core, axis):
        return np.concatenate([R[c][name].reshape(shape_per_core) for c in range(ncores)], axis=axis)
    y_prompt = cat('yp', (1, T, D), 0)
    y_sample = cat('ys', (NB, 4, D), 0)
    outs = [y_prompt, y_sample]
    for nm in ('psbk', 'psbv', 'pfxk', 'pfxv'):
        outs.append(cat(nm, (DEPTH, 1, T, 4, 64), 1))
    outs.append(cat('plf', (DEPTH, 1, T, 4), 1))
    outs.append(cat('pssm', (DEPTH, 1, 8, 64, 128), 1))
    outs.append(cat('pcv', (DEPTH, 1, 3, 1024), 1))
    for nm in ('ssbk', 'ssbv', 'sfxk', 'sfxv'):
        outs.append(cat(nm, (DEPTH, NB, 4, 4, 64), 1))
    outs.append(cat('slf', (DEPTH, NB, 4, 4), 1))
    outs.append(cat('sssm', (DEPTH, NB, 8, 64, 128), 1))
    outs.append(cat('scvo', (DEPTH, NB, 3, 1024), 1))
    return tuple(np.ascontiguousarray(o.astype(np.float32)) for o in outs)


def kernel(**inputs):
    return run(inputs, CFG_FULL, NCORES)
```
